# Optimizing a Trainium2 kernel written in Bass

```python
import math
import jax, jax.numpy as jnp
from jax import lax
import numpy as np

D_MODEL = 1024
BATCH = 2
SEQ = 16384
DEPTH = 2

D_MIX = D_MODEL
D_CONV = D_MIX // 4
D_POOL = D_MIX // 4
D_ATTN = D_MIX // 2
N_HEADS = 4
HEAD_DIM = D_ATTN // (2 * N_HEADS)
V_DIM = 2 * HEAD_DIM
CONV_WIDTH = 31
POOL_WINDOWS = (2, 4, 8, 16)
N_POOL_GROUPS = len(POOL_WINDOWS)
POOL_GROUP_DIM = D_POOL // N_POOL_GROUPS
D_FF = 256 * ((8 * D_MODEL // 3 + 255) // 256)
D_IN = 2 * D_CONV + D_POOL + 3 * D_ATTN
Q_BLOCK = 128
NORM_EPS = 1e-6

kernel_name = "hybrid_conv_pool_diffattn_encoder"


def rms_norm(x, g):
    xf = x.astype(jnp.float32)
    y = xf * lax.rsqrt(jnp.mean(xf * xf, axis=-1, keepdims=True) + NORM_EPS)
    return (y * g.astype(jnp.float32)).astype(x.dtype)


def layer_norm(x, g, b):
    xf = x.astype(jnp.float32)
    mu = jnp.mean(xf, axis=-1, keepdims=True)
    xc = xf - mu
    y = xc * lax.rsqrt(jnp.mean(xc * xc, axis=-1, keepdims=True) + NORM_EPS)
    return (y * g.astype(jnp.float32) + b.astype(jnp.float32)).astype(x.dtype)


def swiglu_ffn(h, w_gate, w_up, w_down):
    return (jax.nn.silu(h @ w_gate) * (h @ w_up)) @ w_down


def conformer_conv(u, w_dw, b_dw, ln_g, ln_b):
    a, gate = jnp.split(u, 2, axis=-1)
    z = a * jax.nn.sigmoid(gate)
    z = lax.conv_general_dilated(
        z, w_dw[:, None, :], window_strides=(1,),
        padding=((CONV_WIDTH // 2, CONV_WIDTH // 2),),
        dimension_numbers=("NWC", "WIO", "NWC"),
        feature_group_count=D_CONV) + b_dw
    return jax.nn.silu(layer_norm(z, ln_g, ln_b))


def multiscale_pool(u, w_grp, scale):
    B_, S_, _ = u.shape
    uf = u.astype(jnp.float32)
    cs = jnp.concatenate([jnp.zeros((B_, 1, D_POOL), jnp.float32), jnp.cumsum(uf, axis=1)], axis=1)
    t = jnp.arange(S_, dtype=jnp.int32)
    outs = []
    for g, w in enumerate(POOL_WINDOWS):
        sl = slice(g * POOL_GROUP_DIM, (g + 1) * POOL_GROUP_DIM)
        lo = jnp.clip(t - w // 2, 0, S_)
        hi = jnp.clip(t + w // 2, 0, S_)
        csg = cs[..., sl]
        win_sum = jnp.take(csg, hi, axis=1) - jnp.take(csg, lo, axis=1)
        cnt = (hi - lo).astype(jnp.float32)[None, :, None]
        outs.append(win_sum / cnt - uf[..., sl])
    d = jnp.stack(outs, axis=2)
    y = jnp.einsum("bsgc,gcd->bsgd", d, w_grp.astype(jnp.float32)).reshape(B_, S_, D_POOL)
    return (y * scale.astype(jnp.float32)).astype(u.dtype)


def diff_attention(q, k, v, lam, slopes):
    B_, S_ = q.shape[0], q.shape[1]
    n_blk = S_ // Q_BLOCK
    q_blocks = jnp.moveaxis(q.reshape(B_, n_blk, Q_BLOCK, N_HEADS, 2, HEAD_DIM), 1, 0)
    k_pos = jnp.arange(S_, dtype=jnp.int32)

    def block(args):
        q_blk, blk = args
        q_pos = blk * Q_BLOCK + jnp.arange(Q_BLOCK, dtype=jnp.int32)
        dist = jnp.abs(q_pos[:, None] - k_pos[None, :]).astype(jnp.float32)
        s = jnp.einsum("bqhcd,bkhcd->bhcqk", q_blk, k, preferred_element_type=jnp.float32)
        s = s - slopes[None, :, None, None, None] * dist
        p = jax.nn.softmax(s, axis=-1)
        a = p[:, :, 0] - lam * p[:, :, 1]
        return jnp.einsum("bhqk,bkhe->bqhe", a.astype(v.dtype), v)

    o = lax.map(block, (q_blocks, jnp.arange(n_blk, dtype=jnp.int32)))
    return jnp.moveaxis(o, 0, 1).reshape(B_, S_, N_HEADS, V_DIM)


def setup_inputs(seed: int = 0) -> dict:
    key = jax.random.key(seed)
    ks = iter(jax.random.split(key, 32))

    def nrm(shape, scale):
        return jax.random.normal(next(ks), shape, jnp.float32) * scale

    def gain(shape):
        return 1.0 + 0.02 * jax.random.normal(next(ks), shape, jnp.float32)

    L = DEPTH
    return {
        "x": nrm((BATCH, SEQ, D_MODEL), 1.0),
        "ffn1_norm": gain((L, D_MODEL)),
        "ffn1_w_gate": nrm((L, D_MODEL, D_FF), D_MODEL ** -0.5),
        "ffn1_w_up": nrm((L, D_MODEL, D_FF), D_MODEL ** -0.5),
        "ffn1_w_down": nrm((L, D_FF, D_MODEL), D_FF ** -0.5),
        "mix_norm": gain((L, D_MODEL)),
        "w_in": nrm((L, D_MODEL, D_IN), D_MODEL ** -0.5),
        "conv_dw": nrm((L, CONV_WIDTH, D_CONV), CONV_WIDTH ** -0.5),
        "conv_dw_bias": nrm((L, D_CONV), 0.02),
        "conv_ln_gain": gain((L, D_CONV)),
        "conv_ln_bias": nrm((L, D_CONV), 0.02),
        "pool_w": nrm((L, N_POOL_GROUPS, POOL_GROUP_DIM, POOL_GROUP_DIM), POOL_GROUP_DIM ** -0.5),
        "pool_scale": 1.0 + 0.1 * jax.random.normal(next(ks), (L, D_POOL), jnp.float32),
        "q_norm": gain((L, HEAD_DIM)),
        "k_norm": gain((L, HEAD_DIM)),
        "lambda_q1": nrm((L, HEAD_DIM), 0.1),
        "lambda_k1": nrm((L, HEAD_DIM), 0.1),
        "lambda_q2": nrm((L, HEAD_DIM), 0.1),
        "lambda_k2": nrm((L, HEAD_DIM), 0.1),
        "attn_subln": gain((L, V_DIM)),
        "w_out": nrm((L, D_MIX, D_MODEL), D_MIX ** -0.5),
        "ffn2_norm": gain((L, D_MODEL)),
        "ffn2_w_gate": nrm((L, D_MODEL, D_FF), D_MODEL ** -0.5),
        "ffn2_w_up": nrm((L, D_MODEL, D_FF), D_MODEL ** -0.5),
        "ffn2_w_down": nrm((L, D_FF, D_MODEL), D_FF ** -0.5),
        "post_norm": gain((L, D_MODEL)),
    }


def reference(x, ffn1_norm, ffn1_w_gate, ffn1_w_up, ffn1_w_down, mix_norm, w_in,
              conv_dw, conv_dw_bias, conv_ln_gain, conv_ln_bias, pool_w, pool_scale,
              q_norm, k_norm, lambda_q1, lambda_k1, lambda_q2, lambda_k2, attn_subln,
              w_out, ffn2_norm, ffn2_w_gate, ffn2_w_up, ffn2_w_down, post_norm):
    B_, S_ = x.shape[0], x.shape[1]
    slopes = jnp.exp2(-8.0 * jnp.arange(1, N_HEADS + 1, dtype=jnp.float32) / N_HEADS)
    for l in range(DEPTH):
        lambda_init = 0.8 - 0.6 * math.exp(-0.3 * l)
        x = x + 0.5 * swiglu_ffn(rms_norm(x, ffn1_norm[l]), ffn1_w_gate[l], ffn1_w_up[l], ffn1_w_down[l])
        h = rms_norm(x, mix_norm[l])
        u = h @ w_in[l]
        u_conv = u[..., :2 * D_CONV]
        u_pool = u[..., 2 * D_CONV:2 * D_CONV + D_POOL]
        q, k, v = jnp.split(u[..., 2 * D_CONV + D_POOL:], 3, axis=-1)
        y_conv = conformer_conv(u_conv, conv_dw[l], conv_dw_bias[l], conv_ln_gain[l], conv_ln_bias[l])
        y_pool = multiscale_pool(u_pool, pool_w[l], pool_scale[l])
        q = rms_norm(q.reshape(B_, S_, N_HEADS, 2, HEAD_DIM), q_norm[l]) * (HEAD_DIM ** -0.5)
        k = rms_norm(k.reshape(B_, S_, N_HEADS, 2, HEAD_DIM), k_norm[l])
        v = v.reshape(B_, S_, N_HEADS, V_DIM)
        lam = (jnp.exp(jnp.sum(lambda_q1[l].astype(jnp.float32) * lambda_k1[l].astype(jnp.float32)))
               - jnp.exp(jnp.sum(lambda_q2[l].astype(jnp.float32) * lambda_k2[l].astype(jnp.float32)))
               + lambda_init)
        o = diff_attention(q, k, v, lam, slopes)
        y_attn = (rms_norm(o, attn_subln[l]) * (1.0 - lambda_init)).reshape(B_, S_, D_ATTN)
        y = jnp.concatenate([y_conv, y_pool, y_attn.astype(y_conv.dtype)], axis=-1) @ w_out[l]
        x = x + y
        x = x + 0.5 * swiglu_ffn(rms_norm(x, ffn2_norm[l]), ffn2_w_gate[l], ffn2_w_up[l], ffn2_w_down[l])
        x = rms_norm(x, post_norm[l])
    return x
```

```python
import numpy as np
import ml_dtypes
import concourse.bass as bass
import concourse.mybir as mybir
from concourse.bass_utils import run_bass_kernel_spmd

F32 = mybir.dt.float32
BF16 = mybir.dt.bfloat16
AF = mybir.ActivationFunctionType
ALU = mybir.AluOpType

D = 1024
DFF = 2816
NFC = DFF // 128
DIN = 2304
S_LEN = 16384
TPC = 4096
NCORES = 8
EPS = 1e-6

ENGS = ("pe", "act", "dve", "pool", "sp")
SAME_ENGINE_SYNC = True
SEM_ROT = 24000
XQ = "pool"


class _Op:
    __slots__ = ("eng", "fn", "reads", "writes", "chan", "group", "deps", "sig",
                 "idx", "pos", "xdeps")


class Sched:
    def __init__(self, nc):
        self.nc = nc
        self.ops = []
        self.chan_state = {}
        self.last_on = {}

    def op(self, eng, fn, reads=(), writes=(), xdeps=()):
        o = _Op()
        o.eng, o.fn, o.reads, o.writes = eng, fn, tuple(reads), tuple(writes)
        o.chan = None
        o.group = None
        o.xdeps = tuple(xdeps)
        o.idx = len(self.ops)
        self.ops.append(o)
        self.last_on[eng] = o.idx
        return o

    def dma(self, eng, chan, fn, reads=(), writes=(), newgroup=True):
        o = self.op(eng, fn, reads, writes)
        o.chan = chan
        st = self.chan_state.setdefault(chan, {"groups": []})
        if newgroup or not st["groups"]:
            st["groups"].append([])
        st["groups"][-1].append(o.idx)
        o.group = len(st["groups"]) - 1
        return o

    def barrier(self):
        lasts = [i for i in self.last_on.values()]
        for st in self.chan_state.values():
            if st["groups"]:
                lasts.append(st["groups"][-1][-1])
        for e in ENGS:
            self.op(e, lambda h: None, xdeps=lasts)

    def finalize(self):
        nc = self.nc
        ops = self.ops
        last_w = {}
        readers = {}
        for o in ops:
            deps = set(o.xdeps)
            for r in o.reads:
                if r in last_w:
                    deps.add(last_w[r])
            for w in o.writes:
                if w in last_w:
                    deps.add(last_w[w])
                for rd in readers.get(w, ()):
                    deps.add(rd)
            deps.discard(o.idx)
            o.deps = deps
            for r in o.reads:
                readers.setdefault(r, []).append(o.idx)
            for w in o.writes:
                last_w[w] = o.idx
                readers[w] = []
        for chan, st in self.chan_state.items():
            cum = 0
            vals = []
            for g in st["groups"]:
                cum += 16 * len(g)
                vals.append(cum)
            st["vals"] = vals
            assert cum < 65000, (chan, cum)
        pos = {e: 0 for e in ENGS}
        for o in ops:
            o.pos = pos[o.eng]
            pos[o.eng] += 1
        waited = {e: {} for e in ENGS}
        need = []
        sig_needed = set()
        for o in ops:
            w = {}
            for d in o.deps:
                p = ops[d]
                if p.chan is not None:
                    s = ("c", p.chan)
                    key = p.group
                else:
                    if p.fn is None:
                        continue
                    s = ("e", p.eng)
                    key = p.pos
                    if p.eng == o.eng:
                        if p.eng == "pe" or not SAME_ENGINE_SYNC:
                            continue
                if s not in w or w[s][0] < key:
                    w[s] = (key, d)
            if o.chan is not None:
                st = self.chan_state[o.chan]
                if o.group > 0 and st["groups"][o.group][0] == o.idx:
                    s = ("c", o.chan)
                    key = o.group - 1
                    if s not in w or w[s][0] < key:
                        w[s] = (key, None)
            lst = []
            for s, (key, d) in w.items():
                if waited[o.eng].get(s, -1) >= key:
                    continue
                waited[o.eng][s] = key
                lst.append((s, key, d))
                if s[0] == "e":
                    sig_needed.add(d)
            need.append(lst)
        sigcount = {e: 0 for e in ENGS}
        for o in ops:
            if o.chan is None and o.idx in sig_needed:
                sigcount[o.eng] += 1
                o.sig = sigcount[o.eng]
            else:
                o.sig = None
        self._sem_ctx = []
        eng_sems = {}
        for e in ENGS:
            n = (sigcount[e] + SEM_ROT - 1) // SEM_ROT
            eng_sems[e] = [self._alloc_sem(f"s_{e}{i}") for i in range(max(n, 1))]
        chan_sems = {}
        for chan in self.chan_state:
            chan_sems[chan] = self._alloc_sem(f"c_{chan}")
        self.nsems = sum(len(v) for v in eng_sems.values()) + len(chan_sems)
        self.counts = dict(pos)

        def eng_wait_target(p):
            k = p.sig - 1
            return eng_sems[p.eng][k // SEM_ROT], (k % SEM_ROT) + 1

        streams = {e: [] for e in ENGS}
        for o in ops:
            streams[o.eng].append(o)

        semv = {}
        ptr = {e: 0 for e in ENGS}
        progress = True
        while progress:
            progress = False
            for e in ENGS:
                while ptr[e] < len(streams[e]):
                    o = streams[e][ptr[e]]
                    ok = True
                    for (s, key, d) in need[o.idx]:
                        if s[0] == "e":
                            sk, val = ("e", ops[d].eng), ops[d].sig
                        else:
                            sk, val = s, self.chan_state[s[1]]["vals"][key]
                        if semv.get(sk, 0) < val:
                            ok = False
                            break
                    if not ok:
                        break
                    if o.chan is not None:
                        semv[("c", o.chan)] = semv.get(("c", o.chan), 0) + 16
                    elif o.sig is not None:
                        semv[("e", o.eng)] = semv.get(("e", o.eng), 0) + 1
                        assert semv[("e", o.eng)] == o.sig
                    ptr[e] += 1
                    progress = True
        for e in ENGS:
            assert ptr[e] == len(streams[e]), ("DEADLOCK", e, ptr[e], len(streams[e]), need[streams[e][ptr[e]].idx])

        self.need = need
        self.streams = streams
        def run_stream(e, handle):
            for o in streams[e]:
                for (s, key, d) in need[o.idx]:
                    if s[0] == "e":
                        sem, val = eng_wait_target(ops[d])
                    else:
                        sem = chan_sems[s[1]]
                        val = self.chan_state[s[1]]["vals"][key]
                    handle.wait_ge(sem, val)
                ins = o.fn(handle)
                if ins is None:
                    assert o.chan is None and o.sig is None, "barrier op cannot signal"
                    continue
                if o.chan is not None:
                    ins.then_inc(chan_sems[o.chan], 16)
                elif o.sig is not None:
                    k = o.sig - 1
                    ins.then_inc(eng_sems[o.eng][k // SEM_ROT], 1)

        with nc.Block() as block:
            if streams["pe"]:
                @block.tensor
                def _(h):
                    run_stream("pe", h)
            if streams["act"]:
                @block.scalar
                def _(h):
                    run_stream("act", h)
            if streams["dve"]:
                @block.vector
                def _(h):
                    run_stream("dve", h)
            if streams["pool"]:
                @block.gpsimd
                def _(h):
                    run_stream("pool", h)
            if streams["sp"]:
                @block.sync
                def _(h):
                    run_stream("sp", h)
        for c in reversed(self._sem_ctx):
            c.__exit__(None, None, None)

    def _alloc_sem(self, name):
        c = self.nc.semaphore(name)
        s = c.__enter__()
        self._sem_ctx.append(c)
        return s


class Arena:
    def __init__(self, nc, nbytes, name="arena"):
        self.nc = nc
        self.n32 = nbytes // 4
        self.ctx = nc.sbuf_tensor(name, [128, self.n32], F32)
        self.t = self.ctx.__enter__()
        self.off = 0
        self.marks = []

    def alloc(self, shape, dt):
        esz = 2 if dt == BF16 else 4
        n = 1
        for s in shape:
            n *= s
        nb = (n * esz + 31) // 32 * 32
        a = self.off
        assert a + nb <= self.n32 * 4, ("SBUF arena overflow", a, nb, self.n32 * 4)
        self.off += nb
        ap = self.t[:, a // 4:(a + nb) // 4]
        if dt != F32:
            ap = ap.bitcast(dt)
        ap = ap[:, 0:n]
        if len(shape) == 2:
            ap = ap.rearrange("p (a b) -> p a b", a=shape[0])
        elif len(shape) == 3:
            ap = ap.rearrange("p (a b c) -> p a b c", a=shape[0], b=shape[1])
        return ap

    def mark(self):
        self.marks.append(self.off)

    def release(self):
        self.off = self.marks.pop()

    def close(self):
        self.ctx.__exit__(None, None, None)


class Ctx:
    pass


def make_ctx(nc):
    C = Ctx()
    C.nc = nc
    C.S = Sched(nc)
    C.A = Arena(nc, 207 * 1024)
    C.psctx = nc.psum_tensor("psum_all", [128, 4096], F32)
    C.ps = C.psctx.__enter__()
    C.uid = 0
    return C


def close_ctx(C):
    C.S.finalize()
    C.psctx.__exit__(None, None, None)
    C.A.close()


def bank(C, b, n=512, off=0):
    return C.ps[:, b * 512 + off:b * 512 + off + n]


def emit_ffn(C, T, xT_in, xT_out, wg, wu, wd, gvec, post_gvec=None, tag="f", NXB=2):
    S, A, nc = C.S, C.A, C.nc
    NT = 256
    ntiles = T // NT
    A.mark()
    Wg = A.alloc([8, DFF], BF16)
    Wu = A.alloc([8, DFF], BF16)
    Wd = A.alloc([NFC, D], BF16)
    onesf = A.alloc([128], F32)
    gv = A.alloc([8], F32)
    pgv = A.alloc([8], F32) if post_gvec is not None else None
    epsb = A.alloc([1], F32)
    xt = [A.alloc([8, NT], F32) for _ in range(NXB)]
    hT = [A.alloc([8, NT], BF16) for _ in range(2)]
    aT = A.alloc([NFC, NT], BF16)
    sq = [A.alloc([NT], F32) for _ in range(2)]
    sg = [A.alloc([NT], F32) for _ in range(2)]
    rstd = A.alloc([NT], F32)
    rstd2 = A.alloc([NT], F32)
    pg = [bank(C, 0, NT), bank(C, 1, NT)]
    pu = [bank(C, 2, NT), bank(C, 3, NT)]
    pd = [bank(C, 4, NT), bank(C, 5, NT)]
    pstat = bank(C, 6, NT)
    pstat2 = bank(C, 7, NT)
    R = lambda n: f"{tag}.{n}"

    S.op("pool", lambda h: h.memset(onesf, 1.0 / D), writes=[R("onesf")])
    S.op("pool", lambda h: h.memset(epsb, EPS), writes=[R("epsb")])
    S.dma("sp", R("cst"), lambda h: h.dma_start(out=gv, in_=gvec), writes=[R("gv")])
    if post_gvec is not None:
        S.dma("sp", R("cst"), lambda h: h.dma_start(out=pgv, in_=post_gvec), writes=[R("pgv")])
    xin_v = xT_in.rearrange("c p t -> p c t")
    xout_v = xT_out.rearrange("c p t -> p c t")

    def load(t):
        b = t % NXB
        S.dma(XQ, R(f"xin{b}"), lambda h: h.dma_start(out=xt[b], in_=xin_v[:, :, t * NT:(t + 1) * NT]),
              writes=[R(f"xt{b}")])

    for _t in range(min(NXB, ntiles)):
        load(_t)
    for dc in range(8):
        S.dma("pool", R("wld"), (lambda dc: lambda h: h.dma_start(out=Wg[:, dc, :], in_=wg[dc * 128:(dc + 1) * 128, :]))(dc),
              writes=[R(f"Wg{dc}")])
        S.dma("pool", R("wld"), (lambda dc: lambda h: h.dma_start(out=Wu[:, dc, :], in_=wu[dc * 128:(dc + 1) * 128, :]))(dc),
              writes=[R(f"Wu{dc}")])
    for fc in range(NFC):
        S.dma("pool", R("wld"), (lambda fc: lambda h: h.dma_start(out=Wd[:, fc, :], in_=wd[fc * 128:(fc + 1) * 128, :]))(fc),
              writes=[R(f"Wd{fc}")])

    def stats(xbuf, xres, pst, rs, rsres):
        for c in range(8):
            k = c % 2
            S.op("act", (lambda c, k: lambda h: h.activation(out=sq[k], in_=xbuf[:, c, :], func=AF.Square))(c, k),
                 reads=[xres], writes=[R(f"sq{k}")])
            S.op("pe", (lambda c, k: lambda h: h.matmul(pst, lhsT=onesf, rhs=sq[k], start=(c == 0), stop=(c == 7)))(c, k),
                 reads=[R(f"sq{k}"), R("onesf")], writes=[rsres + ".ps"])
        S.op("act", lambda h: h.activation(out=rs, in_=pst, func=AF.Sqrt, bias=epsb[:, 0:1], scale=1.0),
             reads=[rsres + ".ps", R("epsb")], writes=[rsres])
        S.op("dve", lambda h: h.reciprocal(out=rs, in_=rs), reads=[rsres], writes=[rsres])

    def make_h(t):
        b = t % 2
        xb = t % NXB
        stats(xt[xb], R(f"xt{xb}"), pstat, rstd, R("rstd"))
        for c in range(8):
            S.op("dve", (lambda c: lambda h: h.scalar_tensor_tensor(out=hT[b][:, c, :], in0=xt[xb][:, c, :], scalar=gv[:, c:c + 1],
                                                                   in1=rstd, op0=ALU.mult, op1=ALU.mult))(c),
                 reads=[R(f"xt{xb}"), R("rstd"), R("gv")], writes=[R(f"hT{b}")])

    def gateup(t):
        b = t % 2
        for fc in range(NFC):
            k = fc % 2
            for dc in range(8):
                S.op("pe", (lambda fc, dc, k: lambda h: h.matmul(pg[k], lhsT=Wg[:, dc, fc * 128:(fc + 1) * 128], rhs=hT[b][:, dc, :],
                                                                 start=(dc == 0), stop=(dc == 7)))(fc, dc, k),
                     reads=[R(f"Wg{dc}"), R(f"hT{b}")], writes=[R(f"pg{k}")])
            for dc in range(8):
                S.op("pe", (lambda fc, dc, k: lambda h: h.matmul(pu[k], lhsT=Wu[:, dc, fc * 128:(fc + 1) * 128], rhs=hT[b][:, dc, :],
                                                                 start=(dc == 0), stop=(dc == 7)))(fc, dc, k),
                     reads=[R(f"Wu{dc}"), R(f"hT{b}")], writes=[R(f"pu{k}")])
            S.op("act", (lambda k: lambda h: h.activation(out=sg[k], in_=pg[k], func=AF.Silu))(k),
                 reads=[R(f"pg{k}")], writes=[R(f"sg{k}")])
            S.op("dve", (lambda fc, k: lambda h: h.tensor_tensor(out=aT[:, fc, :], in0=sg[k], in1=pu[k], op=ALU.mult))(fc, k),
                 reads=[R(f"sg{k}"), R(f"pu{k}")], writes=[R(f"aT{fc}")])

    def down(t):
        b = t % NXB
        for oc in range(8):
            k = oc % 2
            for fc in range(NFC):
                S.op("pe", (lambda oc, fc, k: lambda h: h.matmul(pd[k], lhsT=Wd[:, fc, oc * 128:(oc + 1) * 128], rhs=aT[:, fc, :],
                                                                 start=(fc == 0), stop=(fc == NFC - 1)))(oc, fc, k),
                     reads=[R(f"Wd{fc}"), R(f"aT{fc}")], writes=[R(f"pd{k}")])
            S.op("dve", (lambda oc, k: lambda h: h.scalar_tensor_tensor(out=xt[b][:, oc, :], in0=pd[k], scalar=0.5, in1=xt[b][:, oc, :],
                                                                       op0=ALU.mult, op1=ALU.add))(oc, k),
                 reads=[R(f"pd{k}"), R(f"xt{b}")], writes=[R(f"xt{b}")])
        if post_gvec is not None:
            stats(xt[b], R(f"xt{b}"), pstat2, rstd2, R("rstd2"))
            for c in range(8):
                S.op("dve", (lambda c: lambda h: h.scalar_tensor_tensor(out=xt[b][:, c, :], in0=xt[b][:, c, :], scalar=pgv[:, c:c + 1],
                                                                       in1=rstd2, op0=ALU.mult, op1=ALU.mult))(c),
                     reads=[R(f"xt{b}"), R("rstd2"), R("pgv")], writes=[R(f"xt{b}")])
        S.dma(XQ, R(f"xout{b}"), lambda h: h.dma_start(out=xout_v[:, :, t * NT:(t + 1) * NT], in_=xt[b]),
              reads=[R(f"xt{b}")], writes=[R("xout")])

    make_h(0)
    for t in range(ntiles):
        gateup(t)
        if t + 1 < ntiles:
            make_h(t + 1)
        down(t)
        if t + NXB < ntiles:
            load(t + NXB)
    A.release()


_prog_cache = {}


def build_ffn_prog(T, post):
    key = ("ffn", T, post)
    if key in _prog_cache:
        return _prog_cache[key]
    nc = bass.Bass("TRN2", target_bir_lowering=False)
    xin = nc.dram_tensor("xin", [8, 128, T], F32, kind="ExternalInput").ap()
    wg = nc.dram_tensor("wg", [D, DFF], F32, kind="ExternalInput").ap()
    wu = nc.dram_tensor("wu", [D, DFF], F32, kind="ExternalInput").ap()
    wd = nc.dram_tensor("wd", [DFF, D], F32, kind="ExternalInput").ap()
    gvec = nc.dram_tensor("gvec", [128, 8], F32, kind="ExternalInput").ap()
    pg = nc.dram_tensor("pgvec", [128, 8], F32, kind="ExternalInput").ap() if post else None
    xout = nc.dram_tensor("xout", [8, 128, T], F32, kind="ExternalOutput").ap()
    C = make_ctx(nc)
    emit_ffn(C, T, xin, xout, wg, wu, wd, gvec, pg)
    C.S.op("sp", lambda h: None, reads=["f.xout"])
    close_ctx(C)
    _prog_cache[key] = nc
    return nc


def to_T(x2d):
    T = x2d.shape[0]
    return np.ascontiguousarray(x2d.T.reshape(8, 128, T))


def from_T(xT):
    T = xT.shape[2]
    return np.ascontiguousarray(xT.reshape(1024, T).T)


def gain_pc(g):
    return np.ascontiguousarray(g.reshape(8, 128).T).astype(np.float32)


def run_ffn(xT_list, wg, wu, wd, g, post_g=None):
    T = xT_list[0].shape[2]
    nc = build_ffn_prog(T, post_g is not None)
    maps = []
    for xT in xT_list:
        m = {"xin": xT, "wg": wg, "wu": wu, "wd": wd, "gvec": gain_pc(g)}
        if post_g is not None:
            m["pgvec"] = gain_pc(post_g)
        maps.append(m)
    res = run_bass_kernel_spmd(nc, maps, core_ids=list(range(len(maps))))
    return [r["xout"] for r in res.results]


def emit_inproj(C, T, xT_in, w_in, gvec, qg, kg, zT, upT, qT, kT, Vd, tag="i"):
    S, A, nc = C.S, C.A, C.nc
    NT = 256
    ntiles = T // NT
    A.mark()
    W = A.alloc([8, DIN], BF16)
    onesf = A.alloc([128], F32)
    blk = A.alloc([128], F32)
    gv = A.alloc([8], F32)
    qgv = A.alloc([1], F32)
    kgv = A.alloc([1], F32)
    epsb = A.alloc([1], F32)
    eps64 = A.alloc([1], F32)
    xt = [A.alloc([8, NT], F32) for _ in range(2)]
    hT = A.alloc([8, NT], BF16)
    sq = [A.alloc([NT], F32) for _ in range(2)]
    rstd = A.alloc([NT], F32)
    sgm = [A.alloc([NT], F32) for _ in range(2)]
    rq = [A.alloc([NT], F32) for _ in range(2)]
    ob = [A.alloc([NT], BF16) for _ in range(4)]
    vb = [A.alloc([512], BF16) for _ in range(2)]
    pu = [bank(C, 0, NT), bank(C, 1, NT), bank(C, 2, NT), bank(C, 3, NT)]
    pv = [bank(C, 4, 512), bank(C, 5, 512)]
    pstat = bank(C, 6, NT)
    pqs = bank(C, 7, NT)
    R = lambda n: f"{tag}.{n}"

    S.op("pool", lambda h: h.memset(onesf, 1.0 / D), writes=[R("onesf")])
    S.op("pool", lambda h: h.memset(blk, 0.0), writes=[R("blk")])
    S.op("pool", lambda h: h.memset(blk[0:64, 0:64], 1.0 / 64), writes=[R("blk")])
    S.op("pool", lambda h: h.memset(blk[64:128, 64:128], 1.0 / 64), writes=[R("blk")])
    S.op("pool", lambda h: h.memset(epsb, EPS), writes=[R("epsb")])
    S.op("pool", lambda h: h.memset(eps64, 64.0 * EPS), writes=[R("eps64")])
    S.dma("pool", R("cst"), lambda h: h.dma_start(out=gv, in_=gvec), writes=[R("gv")])
    S.dma("pool", R("cst"), lambda h: h.dma_start(out=qgv, in_=qg), writes=[R("qgv")])
    S.dma("pool", R("cst"), lambda h: h.dma_start(out=kgv, in_=kg), writes=[R("kgv")])
    xin_v = xT_in.rearrange("c p t -> p c t")

    def load(t):
        b = t % 2
        S.dma(XQ, R(f"xin{b}"), lambda h: h.dma_start(out=xt[b], in_=xin_v[:, :, t * NT:(t + 1) * NT]),
              writes=[R(f"xt{b}")])

    load(0)
    if ntiles > 1:
        load(1)
    for dc in range(8):
        S.dma("pool", R("wld"), (lambda dc: lambda h: h.dma_start(out=W[:, dc, :], in_=w_in[dc * 128:(dc + 1) * 128, :]))(dc),
              writes=[R(f"W{dc}")])

    ocount = [0]

    def out_store(dst_ap, src, srcres):
        k = ocount[0] % 4
        ocount[0] += 1
        return k

    def do_tile(t):
        b = t % 2
        xb, xres = xt[b], R(f"xt{b}")
        tsl = slice(t * NT, (t + 1) * NT)
        for c in range(8):
            k = c % 2
            S.op("act", (lambda c, k: lambda h: h.activation(out=sq[k], in_=xb[:, c, :], func=AF.Square))(c, k),
                 reads=[xres], writes=[R(f"sq{k}")])
            S.op("pe", (lambda c, k: lambda h: h.matmul(pstat, lhsT=onesf, rhs=sq[k], start=(c == 0), stop=(c == 7)))(c, k),
                 reads=[R(f"sq{k}"), R("onesf")], writes=[R("pstat")])
        S.op("act", lambda h: h.activation(out=rstd, in_=pstat, func=AF.Sqrt, bias=epsb[:, 0:1], scale=1.0),
             reads=[R("pstat"), R("epsb")], writes=[R("rstd")])
        S.op("dve", lambda h: h.reciprocal(out=rstd, in_=rstd), reads=[R("rstd")], writes=[R("rstd")])
        for c in range(8):
            S.op("dve", (lambda c: lambda h: h.scalar_tensor_tensor(out=hT[:, c, :], in0=xb[:, c, :], scalar=gv[:, c:c + 1],
                                                                   in1=rstd, op0=ALU.mult, op1=ALU.mult))(c),
                 reads=[xres, R("rstd"), R("gv")], writes=[R("hT")])
        if t + 2 < ntiles:
            pass

        def proj(j, pk):
            for dc in range(8):
                S.op("pe", (lambda dc: lambda h: h.matmul(pu[pk], lhsT=W[:, dc, j * 128:(j + 1) * 128], rhs=hT[:, dc, :],
                                                          start=(dc == 0), stop=(dc == 7)))(dc),
                     reads=[R(f"W{dc}"), R("hT")], writes=[R(f"pu{pk}")])

        def store(dst, k):
            S.dma(XQ, R(f"ost{k}"), lambda h: h.dma_start(out=dst, in_=ob[k]), reads=[R(f"ob{k}")], writes=[R("outs")])

        for j in range(2):
            proj(2 + j, 0)
            proj(j, 1)
            k = ocount[0] % 4
            ocount[0] += 1
            S.op("act", (lambda j: lambda h: h.activation(out=sgm[j], in_=pu[0], func=AF.Sigmoid))(j),
                 reads=[R("pu0")], writes=[R(f"sgm{j}")])
            S.op("dve", (lambda j, k: lambda h: h.tensor_tensor(out=ob[k], in0=sgm[j], in1=pu[1], op=ALU.mult))(j, k),
                 reads=[R(f"sgm{j}"), R("pu1")], writes=[R(f"ob{k}")])
            store(zT[j, :, tsl], k)
        for j in range(2):
            pk = 2 + j
            proj(4 + j, pk)
            k = ocount[0] % 4
            ocount[0] += 1
            S.op("dve", (lambda pk, k: lambda h: h.tensor_copy(out=ob[k], in_=pu[pk]))(pk, k),
                 reads=[R(f"pu{pk}")], writes=[R(f"ob{k}")])
            store(upT[j, :, tsl], k)
        for j in range(8):
            pk = j % 4
            r = j % 2
            isq = j < 4
            proj(6 + j, pk)
            k = ocount[0] % 4
            ocount[0] += 1
            S.op("act", (lambda pk, r: lambda h: h.activation(out=sq[r], in_=pu[pk], func=AF.Square))(pk, r),
                 reads=[R(f"pu{pk}")], writes=[R(f"sq{r}")])
            S.op("pe", (lambda r: lambda h: h.matmul(pqs, lhsT=blk, rhs=sq[r], start=True, stop=True))(r),
                 reads=[R(f"sq{r}"), R("blk")], writes=[R("pqs")])
            if isq:
                S.op("act", (lambda r: lambda h: h.activation(out=rq[r], in_=pqs, func=AF.Sqrt, bias=eps64[:, 0:1], scale=64.0))(r),
                     reads=[R("pqs"), R("eps64")], writes=[R(f"rq{r}")])
            else:
                S.op("act", (lambda r: lambda h: h.activation(out=rq[r], in_=pqs, func=AF.Sqrt, bias=epsb[:, 0:1], scale=1.0))(r),
                     reads=[R("pqs"), R("epsb")], writes=[R(f"rq{r}")])
            S.op("dve", (lambda r: lambda h: h.reciprocal(out=rq[r], in_=rq[r]))(r), reads=[R(f"rq{r}")], writes=[R(f"rq{r}")])
            gvv = qgv if isq else kgv
            S.op("dve", (lambda pk, r, k, gvv: lambda h: h.scalar_tensor_tensor(out=ob[k], in0=pu[pk], scalar=gvv[:, 0:1], in1=rq[r],
                                                                               op0=ALU.mult, op1=ALU.mult))(pk, r, k, gvv),
                 reads=[R(f"pu{pk}"), R(f"rq{r}"), R("qgv"), R("kgv")], writes=[R(f"ob{k}")])
            dst = (qT if isq else kT)[j % 4, :, tsl]
            store(dst, k)
        def vpart(s):
            k = s % 2
            for dc in range(8):
                S.op("pe", (lambda dc: lambda h: h.matmul(pv[k], lhsT=hT[:, dc, s * 128:(s + 1) * 128], rhs=W[:, dc, 1792:2304],
                                                          start=(dc == 0), stop=(dc == 7)))(dc),
                     reads=[R(f"W{dc}"), R("hT")], writes=[R(f"pv{k}")])
            S.op("act", lambda h: h.activation(out=vb[k], in_=pv[k], func=AF.Copy), reads=[R(f"pv{k}")], writes=[R(f"vb{k}")])
            cidx = t * (NT // 128) + s
            S.dma(XQ, R(f"vst{k}"), lambda h: h.dma_start(out=Vd[:, :, cidx, :].rearrange("h p e -> p h e"),
                                                          in_=vb[k].rearrange("p (h e) -> p h e", h=4)),
                  reads=[R(f"vb{k}")], writes=[R("outs")])

        for s in range(NT // 128):
            vpart(s)
        if t + 2 < ntiles:
            load(t + 2)

    for t in range(ntiles):
        do_tile(t)
    A.release()


def build_inproj_prog(T):
    key = ("inproj", T)
    if key in _prog_cache:
        return _prog_cache[key]
    nc = bass.Bass("TRN2", target_bir_lowering=False)
    xin = nc.dram_tensor("xin", [8, 128, T], F32, kind="ExternalInput").ap()
    w_in = nc.dram_tensor("w_in", [D, DIN], F32, kind="ExternalInput").ap()
    gvec = nc.dram_tensor("gvec", [128, 8], F32, kind="ExternalInput").ap()
    qg = nc.dram_tensor("qg", [128, 1], F32, kind="ExternalInput").ap()
    kg = nc.dram_tensor("kg", [128, 1], F32, kind="ExternalInput").ap()
    zT = nc.dram_tensor("zT", [2, 128, T], BF16, kind="ExternalOutput").ap()
    upT = nc.dram_tensor("upT", [2, 128, T], BF16, kind="ExternalOutput").ap()
    qT = nc.dram_tensor("qT", [4, 128, T], BF16, kind="ExternalOutput").ap()
    kT = nc.dram_tensor("kT", [4, 128, T], BF16, kind="ExternalOutput").ap()
    Vd = nc.dram_tensor("Vd", [4, 128, T // 128, 128], BF16, kind="ExternalOutput").ap()
    C = make_ctx(nc)
    emit_inproj(C, T, xin, w_in, gvec, qg, kg, zT, upT, qT, kT, Vd)
    C.S.op("pool", lambda h: None, reads=["i.outs"])
    close_ctx(C)
    _prog_cache[key] = nc
    return nc


def emit_attn(C, T, Sq, qT, kTf, Vf, qpos, kposc, lamv, subg, lamc, yT, tag="a"):
    S, A, nc = C.S, C.A, C.nc
    NQ = 512
    nqt = T // NQ
    nkc = Sq // 128
    A.mark()
    Kt = A.alloc([Sq], BF16)
    Vt = A.alloc([nkc, 128], BF16)
    kpc = A.alloc([nkc], F32)
    negk = A.alloc([nkc], F32)
    onesb = A.alloc([128], BF16)
    ones128 = A.alloc([128], F32)
    epsb = A.alloc([1], F32)
    lamt = A.alloc([4, 64], F32)
    lprod = A.alloc([2, 64], F32)
    lsum = A.alloc([2], F32)
    neglam = A.alloc([1], F32)
    sgv = A.alloc([1], F32)
    lct = A.alloc([2], F32)
    Qt = [A.alloc([NQ], BF16) for _ in range(2)]
    qpb = [A.alloc([NQ], F32) for _ in range(2)]
    dist = [A.alloc([NQ], F32) for _ in range(2)]
    tmp = [A.alloc([2 * NQ], F32) for _ in range(2)]
    Pt = [A.alloc([2 * NQ], BF16) for _ in range(2)]
    r1 = A.alloc([NQ], F32)
    o1 = A.alloc([NQ], F32)
    o2 = A.alloc([NQ], F32)
    sqo = A.alloc([NQ], F32)
    rs = A.alloc([NQ], F32)
    yb = [A.alloc([NQ], BF16) for _ in range(2)]
    R = lambda n: f"{tag}.{n}"
    psS = [C.ps[:, 0:1024], C.ps[:, 1024:2048]]
    acc = [bank(C, 4), bank(C, 5)]
    den = [bank(C, 6), bank(C, 7)]

    S.op("pool", lambda h: h.memset(onesb, 1.0), writes=[R("onesb")])
    S.op("pool", lambda h: h.memset(ones128, 1.0 / 128), writes=[R("ones128")])
    S.op("pool", lambda h: h.memset(epsb, EPS), writes=[R("epsb")])
    S.dma("pool", R("cst"), lambda h: h.dma_start(out=kpc, in_=kposc), writes=[R("kpc")])
    S.dma("pool", R("cst"), lambda h: h.dma_start(out=lamt, in_=lamv), writes=[R("lamt")])
    S.dma("pool", R("cst"), lambda h: h.dma_start(out=sgv, in_=subg), writes=[R("sgv")])
    S.dma("pool", R("cst"), lambda h: h.dma_start(out=lct, in_=lamc), writes=[R("lct")])
    S.op("dve", lambda h: h.tensor_tensor(out=lprod[:, 0, :], in0=lamt[:, 0, :], in1=lamt[:, 1, :], op=ALU.mult),
         reads=[R("lamt")], writes=[R("lprod")])
    S.op("dve", lambda h: h.tensor_tensor(out=lprod[:, 1, :], in0=lamt[:, 2, :], in1=lamt[:, 3, :], op=ALU.mult),
         reads=[R("lamt")], writes=[R("lprod")])
    S.op("dve", lambda h: h.reduce_sum(out=lsum, in_=lprod, axis=mybir.AxisListType.X), reads=[R("lprod")], writes=[R("lsum")])
    S.op("act", lambda h: h.activation(out=lsum, in_=lsum, func=AF.Exp), reads=[R("lsum")], writes=[R("lsum")])
    S.op("dve", lambda h: h.tensor_tensor(out=neglam, in0=lsum[:, 1:2], in1=lsum[:, 0:1], op=ALU.subtract),
         reads=[R("lsum")], writes=[R("neglam")])
    S.op("dve", lambda h: h.tensor_scalar(out=neglam, in0=neglam, scalar1=lct[:, 0:1], scalar2=None, op0=ALU.add),
         reads=[R("neglam"), R("lct")], writes=[R("neglam")])
    S.op("dve", lambda h: h.tensor_scalar(out=sgv, in0=sgv, scalar1=lct[:, 1:2], scalar2=None, op0=ALU.mult),
         reads=[R("sgv"), R("lct")], writes=[R("sgv")])

    cnt = [0]

    def head(hd):
        slope = 2.0 ** (-8.0 * (hd + 1) / 4)
        S.dma("pool", R("kld"), lambda h: h.dma_start(out=Kt, in_=kTf[hd]), writes=[R("Kt")])
        S.dma("pool", R("vld"), lambda h: h.dma_start(out=Vt, in_=Vf[hd]), writes=[R("Vt")])
        S.op("dve", lambda h: h.tensor_scalar(out=negk, in0=kpc, scalar1=-slope, scalar2=None, op0=ALU.mult),
             reads=[R("kpc")], writes=[R("negk")])

        def qtile(qt):
            qb = (hd * nqt + qt) % 2
            qs = slice(qt * NQ, (qt + 1) * NQ)
            S.dma("pool", R(f"qld{qb}"), lambda h: h.dma_start(out=Qt[qb], in_=qT[hd, :, qs]), writes=[R(f"Qt{qb}")])
            S.dma("pool", R(f"pld{qb}"), lambda h: h.dma_start(out=qpb[qb], in_=qpos[:, qs]), writes=[R(f"qpb{qb}")])

            def chunk(kc):
                i = cnt[0] % 2
                cnt[0] += 1
                ks = slice(kc * 128, (kc + 1) * 128)
                S.op("pe", lambda h: h.matmul(psS[i][:, 0:NQ], lhsT=Kt[0:64, ks], rhs=Qt[qb][0:64, :], start=True, stop=True),
                     reads=[R("Kt"), R(f"Qt{qb}")], writes=[R(f"psS{i}a")])
                S.op("pe", lambda h: h.matmul(psS[i][:, NQ:2 * NQ], lhsT=Kt[64:128, ks], rhs=Qt[qb][64:128, :], start=True, stop=True),
                     reads=[R("Kt"), R(f"Qt{qb}")], writes=[R(f"psS{i}b")])
                S.op("act", lambda h: h.activation(out=dist[i], in_=qpb[qb], func=AF.Abs, bias=negk[:, kc:kc + 1], scale=slope),
                     reads=[R(f"qpb{qb}"), R("negk")], writes=[R(f"dist{i}")])
                S.op("dve", lambda h: h.tensor_tensor(out=tmp[i][:, 0:NQ], in0=psS[i][:, 0:NQ], in1=dist[i], op=ALU.subtract),
                     reads=[R(f"dist{i}"), R(f"psS{i}a")], writes=[R(f"tmp{i}")])
                S.op("dve", lambda h: h.tensor_tensor(out=tmp[i][:, NQ:2 * NQ], in0=psS[i][:, NQ:2 * NQ], in1=dist[i], op=ALU.subtract),
                     reads=[R(f"dist{i}"), R(f"psS{i}b")], writes=[R(f"tmp{i}")])
                S.op("act", lambda h: h.activation(out=Pt[i], in_=tmp[i], func=AF.Exp), reads=[R(f"tmp{i}")], writes=[R(f"Pt{i}")])
                first, last = (kc == 0), (kc == nkc - 1)
                for m in range(2):
                    S.op("pe", (lambda m: lambda h: h.matmul(acc[m], lhsT=Vt[:, kc, :], rhs=Pt[i][:, m * NQ:(m + 1) * NQ],
                                                             start=first, stop=last))(m),
                         reads=[R("Vt"), R(f"Pt{i}")], writes=[R(f"acc{m}")])
                    S.op("pe", (lambda m: lambda h: h.matmul(den[m], lhsT=onesb, rhs=Pt[i][:, m * NQ:(m + 1) * NQ],
                                                             start=first, stop=last))(m),
                         reads=[R("onesb"), R(f"Pt{i}")], writes=[R(f"den{m}")])

            for kc in range(nkc):
                chunk(kc)
            S.op("dve", lambda h: h.reciprocal(out=r1, in_=den[0]), reads=[R("den0")], writes=[R("r1")])
            S.op("dve", lambda h: h.tensor_tensor(out=o1, in0=acc[0], in1=r1, op=ALU.mult), reads=[R("acc0"), R("r1")], writes=[R("o1")])
            S.op("dve", lambda h: h.reciprocal(out=r1, in_=den[1]), reads=[R("den1"), R("o1")], writes=[R("r1")])
            S.op("dve", lambda h: h.tensor_tensor(out=o2, in0=acc[1], in1=r1, op=ALU.mult), reads=[R("acc1"), R("r1")], writes=[R("o2")])
            S.op("dve", lambda h: h.scalar_tensor_tensor(out=o1, in0=o2, scalar=neglam[:, 0:1], in1=o1, op0=ALU.mult, op1=ALU.add),
                 reads=[R("o2"), R("o1"), R("neglam")], writes=[R("o1")])
            S.op("act", lambda h: h.activation(out=sqo, in_=o1, func=AF.Square), reads=[R("o1")], writes=[R("sqo")])
            S.op("pe", lambda h: h.matmul(den[0], lhsT=ones128, rhs=sqo, start=True, stop=True),
                 reads=[R("ones128"), R("sqo")], writes=[R("den0")])
            S.op("act", lambda h: h.activation(out=rs, in_=den[0], func=AF.Sqrt, bias=epsb[:, 0:1], scale=1.0),
                 reads=[R("den0"), R("epsb")], writes=[R("rs")])
            S.op("dve", lambda h: h.reciprocal(out=rs, in_=rs), reads=[R("rs")], writes=[R("rs")])
            S.op("dve", lambda h: h.scalar_tensor_tensor(out=yb[qb], in0=o1, scalar=sgv[:, 0:1], in1=rs, op0=ALU.mult, op1=ALU.mult),
                 reads=[R("o1"), R("rs"), R("sgv")], writes=[R(f"yb{qb}")])
            S.dma("pool", R(f"yst{qb}"), lambda h: h.dma_start(out=yT[hd, :, qs], in_=yb[qb]), reads=[R(f"yb{qb}")], writes=[R("outs")])

        for qt in range(nqt):
            qtile(qt)

    for hd in range(4):
        head(hd)
    A.release()


def build_attn_prog(T, Sq):
    key = ("attn", T, Sq)
    if key in _prog_cache:
        return _prog_cache[key]
    nc = bass.Bass("TRN2", target_bir_lowering=False)
    qT = nc.dram_tensor("qT", [4, 128, T], BF16, kind="ExternalInput").ap()
    kTf = nc.dram_tensor("kTf", [4, 128, Sq], BF16, kind="ExternalInput").ap()
    Vf = nc.dram_tensor("Vf", [4, 128, Sq // 128, 128], BF16, kind="ExternalInput").ap()
    qpos = nc.dram_tensor("qpos", [128, T], F32, kind="ExternalInput").ap()
    kposc = nc.dram_tensor("kposc", [128, Sq // 128], F32, kind="ExternalInput").ap()
    lamv = nc.dram_tensor("lamv", [128, 4, 64], F32, kind="ExternalInput").ap()
    subg = nc.dram_tensor("subg", [128, 1], F32, kind="ExternalInput").ap()
    lamc = nc.dram_tensor("lamc", [128, 2], F32, kind="ExternalInput").ap()
    yT = nc.dram_tensor("yT", [4, 128, T], BF16, kind="ExternalOutput").ap()
    C = make_ctx(nc)
    emit_attn(C, T, Sq, qT, kTf, Vf, qpos, kposc, lamv, subg, lamc, yT)
    C.S.op("pool", lambda h: None, reads=["a.outs"])
    close_ctx(C)
    _prog_cache[key] = nc
    return nc


HALO = 16


def emit_mix(C, T, x1T, zTh, upTh, yaT, w_out, convw, cvec, pool_w, invcnt, ident, x2T, tag="m"):
    S, A, nc = C.S, C.A, C.nc
    NT = 256
    ntiles = T // NT
    A.mark()
    Wo = A.alloc([8, D], BF16)
    idt = A.alloc([128], F32)
    cw = A.alloc([2, 31], F32)
    cv = A.alloc([8], F32)
    Dg = A.alloc([2, 31, 128], BF16)
    pst = A.alloc([2, 128], F32)
    PWf = A.alloc([2, 128], BF16)
    PWh = A.alloc([2, 128], BF16)
    ones256 = A.alloc([128], F32)
    epsb = A.alloc([1], F32)
    xt = [A.alloc([8, NT], F32) for _ in range(2)]
    zt = [A.alloc([2, NT + 2 * HALO], BF16) for _ in range(2)]
    ut = [A.alloc([2, NT + 2 * HALO], BF16) for _ in range(2)]
    ic = [A.alloc([2, NT], F32) for _ in range(2)]
    ycat = [A.alloc([8, NT], BF16) for _ in range(2)]
    cz = A.alloc([2, NT], F32)
    sqc = A.alloc([2, NT], F32)
    mean = A.alloc([NT], F32)
    m2 = A.alloc([NT], F32)
    rstd = A.alloc([NT], F32)
    t1 = A.alloc([2, NT], F32)
    pa = A.alloc([NT], F32)
    R = lambda n: f"{tag}.{n}"
    pc = bank(C, 0, NT)
    pm = bank(C, 1, NT)
    pvv = bank(C, 2, NT)
    pA = bank(C, 3, NT)
    pB = bank(C, 4, NT)
    po = [bank(C, 5, NT), bank(C, 6, NT)]

    S.op("pool", lambda h: h.memset(ones256, 1.0 / 256), writes=[R("ones256")])
    S.op("pool", lambda h: h.memset(epsb, EPS), writes=[R("epsb")])
    S.op("pool", lambda h: h.memset(pst, 0.0), writes=[R("pst")])
    S.dma("pool", R("cst"), lambda h: h.dma_start(out=idt, in_=ident), writes=[R("idt")])
    S.dma("pool", R("cst"), lambda h: h.dma_start(out=cw, in_=convw.rearrange("c p k -> p c k")), writes=[R("cw")])
    S.dma("pool", R("cst"), lambda h: h.dma_start(out=cv, in_=cvec), writes=[R("cv")])
    for g in range(4):
        c, r = g // 2, g % 2
        S.dma("pool", R("cst"), (lambda g, c, r: lambda h: h.dma_start(out=pst[64 * r:64 * r + 64, c, 64 * r:64 * r + 64], in_=pool_w[g]))(g, c, r),
              reads=[R("pst")], writes=[R(f"pst{g}")])
    for c in range(2):
        S.op("dve", (lambda c: lambda h: h.tensor_copy(out=PWf[:, c, :], in_=pst[:, c, :]))(c),
             reads=[R("pst0"), R("pst1"), R("pst2"), R("pst3")], writes=[R("PWf")])
        S.op("dve", (lambda c: lambda h: h.tensor_copy(out=PWh[:, c, :], in_=pst[:, c, :]))(c),
             reads=[R("pst0"), R("pst1"), R("pst2"), R("pst3")], writes=[R("PWh")])
        S.op("dve", (lambda c: lambda h: h.memset(PWh[0:64, c, :], 0.0))(c), reads=[R("PWh")], writes=[R("PWh")])
        for k in range(31):
            S.op("dve", (lambda c, k: lambda h: h.tensor_scalar(out=Dg[:, c, k, :], in0=idt, scalar1=cw[:, c, k:k + 1], scalar2=None,
                                                               op0=ALU.mult))(c, k),
                 reads=[R("idt"), R("cw")], writes=[R("Dg")])
    for dc in range(8):
        S.dma("pool", R("wld"), (lambda dc: lambda h: h.dma_start(out=Wo[:, dc, :], in_=w_out[dc * 128:(dc + 1) * 128, :]))(dc),
              writes=[R(f"Wo{dc}")])
    x1v = x1T.rearrange("c p t -> p c t")
    x2v = x2T.rearrange("c p t -> p c t")
    zv = zTh.rearrange("c p t -> p c t")
    uv = upTh.rearrange("c p t -> p c t")
    yav = yaT.rearrange("c p t -> p c t")
    icv = invcnt.rearrange("c p t -> p c t")

    def load(t):
        b = t % 2
        ts_ = slice(t * NT, (t + 1) * NT)
        th = slice(t * NT, (t + 1) * NT + 2 * HALO)
        S.dma(XQ, R(f"ld{b}"), lambda h: h.dma_start(out=xt[b], in_=x1v[:, :, ts_]), writes=[R(f"xt{b}")])
        S.dma(XQ, R(f"ld{b}"), lambda h: h.dma_start(out=zt[b], in_=zv[:, :, th]), writes=[R(f"zt{b}")], newgroup=False)
        S.dma(XQ, R(f"ld{b}"), lambda h: h.dma_start(out=ut[b], in_=uv[:, :, th]), writes=[R(f"ut{b}")], newgroup=False)
        S.dma(XQ, R(f"ld{b}"), lambda h: h.dma_start(out=ic[b], in_=icv[:, :, ts_]), writes=[R(f"ic{b}")], newgroup=False)
        S.dma(XQ, R(f"ld{b}"), lambda h: h.dma_start(out=ycat[b][:, 4:8, :], in_=yav[:, :, ts_]), writes=[R(f"ya{b}")], newgroup=False)

    def do_tile(t):
        b = t % 2
        ts_ = slice(t * NT, (t + 1) * NT)
        for c in range(2):
            for k in range(31):
                S.op("pe", (lambda c, k: lambda h: h.matmul(pc, lhsT=Dg[:, c, k, :], rhs=zt[b][:, c, k + 1:k + 1 + NT],
                                                            start=(k == 0), stop=(k == 30)))(c, k),
                     reads=[R("Dg"), R(f"zt{b}")], writes=[R("pc")])
            S.op("act", (lambda c: lambda h: h.activation(out=cz[:, c, :], in_=pc, func=AF.Identity, bias=cv[:, c:c + 1], scale=1.0))(c),
                 reads=[R("pc"), R("cv")], writes=[R(f"cz{c}")])
            S.op("act", (lambda c: lambda h: h.activation(out=sqc[:, c, :], in_=cz[:, c, :], func=AF.Square))(c),
                 reads=[R(f"cz{c}")], writes=[R(f"sqc{c}")])
        for c in range(2):
            S.op("pe", (lambda c: lambda h: h.matmul(pm, lhsT=ones256, rhs=cz[:, c, :], start=(c == 0), stop=(c == 1)))(c),
                 reads=[R("ones256"), R(f"cz{c}")], writes=[R("pm")])
        for c in range(2):
            S.op("pe", (lambda c: lambda h: h.matmul(pvv, lhsT=ones256, rhs=sqc[:, c, :], start=(c == 0), stop=(c == 1)))(c),
                 reads=[R("ones256"), R(f"sqc{c}")], writes=[R("pvv")])
        S.op("dve", lambda h: h.tensor_copy(out=mean, in_=pm), reads=[R("pm")], writes=[R("mean")])
        S.op("dve", lambda h: h.tensor_tensor(out=m2, in0=mean, in1=mean, op=ALU.mult), reads=[R("mean")], writes=[R("m2")])
        S.op("dve", lambda h: h.tensor_tensor(out=m2, in0=pvv, in1=m2, op=ALU.subtract), reads=[R("pvv"), R("m2")], writes=[R("m2")])
        S.op("act", lambda h: h.activation(out=rstd, in_=m2, func=AF.Sqrt, bias=epsb[:, 0:1], scale=1.0),
             reads=[R("m2"), R("epsb")], writes=[R("rstd")])
        S.op("dve", lambda h: h.reciprocal(out=rstd, in_=rstd), reads=[R("rstd")], writes=[R("rstd")])
        for c in range(2):
            S.op("dve", (lambda c: lambda h: h.tensor_tensor(out=t1[:, c, :], in0=cz[:, c, :], in1=mean, op=ALU.subtract))(c),
                 reads=[R(f"cz{c}"), R("mean")], writes=[R(f"t1{c}")])
            S.op("dve", (lambda c: lambda h: h.scalar_tensor_tensor(out=t1[:, c, :], in0=t1[:, c, :], scalar=cv[:, 2 + c:3 + c], in1=rstd,
                                                                   op0=ALU.mult, op1=ALU.mult))(c),
                 reads=[R(f"t1{c}"), R("rstd"), R("cv")], writes=[R(f"t1{c}")])
            S.op("act", (lambda c: lambda h: h.activation(out=ycat[b][:, c, :], in_=t1[:, c, :], func=AF.Silu, bias=cv[:, 4 + c:5 + c], scale=1.0))(c),
                 reads=[R(f"t1{c}"), R("cv")], writes=[R(f"yc{b}")])
        for c in range(2):
            taps = list(range(-2, 2)) if c == 0 else list(range(-8, 8))
            narrow = (1, ) if c == 0 else (4, )
            for i, tau in enumerate(taps):
                full = (-narrow[0] <= tau <= narrow[0] - 1)
                Wm = PWf if full else PWh
                S.op("pe", (lambda c, tau, Wm, i, n: lambda h: h.matmul(pA, lhsT=Wm[:, c, :], rhs=ut[b][:, c, HALO + tau:HALO + tau + NT],
                                                                        start=(i == 0), stop=(i == n - 1)))(c, tau, Wm, i, len(taps)),
                     reads=[R("PWf"), R("PWh"), R(f"ut{b}")], writes=[R("pA")])
            S.op("pe", (lambda c: lambda h: h.matmul(pB, lhsT=PWf[:, c, :], rhs=ut[b][:, c, HALO:HALO + NT], start=True, stop=True))(c),
                 reads=[R("PWf"), R(f"ut{b}")], writes=[R("pB")])
            S.op("dve", (lambda c: lambda h: h.tensor_tensor(out=pa, in0=pA, in1=ic[b][:, c, :], op=ALU.mult))(c),
                 reads=[R("pA"), R(f"ic{b}")], writes=[R("pa")])
            S.op("dve", (lambda c: lambda h: h.tensor_tensor(out=pa, in0=pa, in1=pB, op=ALU.subtract))(c),
                 reads=[R("pa"), R("pB")], writes=[R("pa")])
            S.op("dve", (lambda c: lambda h: h.tensor_scalar(out=ycat[b][:, 2 + c, :], in0=pa, scalar1=cv[:, 6 + c:7 + c], scalar2=None,
                                                            op0=ALU.mult))(c),
                 reads=[R("pa"), R("cv")], writes=[R(f"yc{b}")])
        for oc in range(8):
            k = oc % 2
            for c in range(8):
                S.op("pe", (lambda oc, c, k: lambda h: h.matmul(po[k], lhsT=Wo[:, c, oc * 128:(oc + 1) * 128], rhs=ycat[b][:, c, :],
                                                                start=(c == 0), stop=(c == 7)))(oc, c, k),
                     reads=[R(f"Wo{c}"), R(f"yc{b}"), R(f"ya{b}")], writes=[R(f"po{k}")])
            S.op("dve", (lambda oc, k: lambda h: h.tensor_tensor(out=xt[b][:, oc, :], in0=po[k], in1=xt[b][:, oc, :], op=ALU.add))(oc, k),
                 reads=[R(f"po{k}"), R(f"xt{b}")], writes=[R(f"xt{b}")])
        S.dma(XQ, R(f"st{b}"), lambda h: h.dma_start(out=x2v[:, :, ts_], in_=xt[b]), reads=[R(f"xt{b}")], writes=[R("outs")])
        if t + 2 < ntiles:
            load(t + 2)

    load(0)
    if ntiles > 1:
        load(1)
    for t in range(ntiles):
        do_tile(t)
    A.release()


def build_mix_prog(T):
    key = ("mix", T)
    if key in _prog_cache:
        return _prog_cache[key]
    nc = bass.Bass("TRN2", target_bir_lowering=False)
    x1T = nc.dram_tensor("x1T", [8, 128, T], F32, kind="ExternalInput").ap()
    zTh = nc.dram_tensor("zTh", [2, 128, T + 2 * HALO], BF16, kind="ExternalInput").ap()
    upTh = nc.dram_tensor("upTh", [2, 128, T + 2 * HALO], BF16, kind="ExternalInput").ap()
    yaT = nc.dram_tensor("yaT", [4, 128, T], BF16, kind="ExternalInput").ap()
    w_out = nc.dram_tensor("w_out", [D, D], F32, kind="ExternalInput").ap()
    convw = nc.dram_tensor("convw", [2, 128, 31], F32, kind="ExternalInput").ap()
    cvec = nc.dram_tensor("cvec", [128, 8], F32, kind="ExternalInput").ap()
    pool_w = nc.dram_tensor("pool_w", [4, 64, 64], F32, kind="ExternalInput").ap()
    invcnt = nc.dram_tensor("invcnt", [2, 128, T], F32, kind="ExternalInput").ap()
    ident = nc.dram_tensor("ident", [128, 128], F32, kind="ExternalInput").ap()
    x2T = nc.dram_tensor("x2T", [8, 128, T], F32, kind="ExternalOutput").ap()
    C = make_ctx(nc)
    emit_mix(C, T, x1T, zTh, upTh, yaT, w_out, convw, cvec, pool_w, invcnt, ident, x2T)
    C.S.op("pool", lambda h: None, reads=["m.outs"])
    close_ctx(C)
    _prog_cache[key] = nc
    return nc


def pool_invcnt(pos0, T, Sq):
    t = np.arange(pos0, pos0 + T)
    out = np.zeros((2, 128, T), np.float32)
    for g, w in enumerate((2, 4, 8, 16)):
        lo = np.clip(t - w // 2, 0, Sq)
        hi = np.clip(t + w // 2, 0, Sq)
        out[g // 2, (g % 2) * 64:(g % 2) * 64 + 64, :] = (1.0 / (hi - lo).astype(np.float32))[None, :]
    return out


def _launch(nc, maps):
    res = run_bass_kernel_spmd(nc, maps, core_ids=list(range(len(maps))))
    return res.results


def kernel(x, ffn1_norm, ffn1_w_gate, ffn1_w_up, ffn1_w_down, mix_norm, w_in,
           conv_dw, conv_dw_bias, conv_ln_gain, conv_ln_bias, pool_w, pool_scale,
           q_norm, k_norm, lambda_q1, lambda_k1, lambda_q2, lambda_k2, attn_subln,
           w_out, ffn2_norm, ffn2_w_gate, ffn2_w_up, ffn2_w_down, post_norm):
    import math
    f32 = np.float32
    x = np.asarray(x, f32)
    B, Sq, _ = x.shape
    T = Sq // 4
    ncore = 8
    xT = [to_T(x[c // 4, (c % 4) * T:(c % 4 + 1) * T]) for c in range(ncore)]
    ident = np.eye(128, dtype=f32)
    kposc = (np.arange(Sq // 128, dtype=f32)[None] * 128 + np.arange(128, dtype=f32)[:, None]).copy()
    qpos = [np.ascontiguousarray(np.broadcast_to(np.arange((c % 4) * T, (c % 4 + 1) * T, dtype=f32)[None], (128, T))) for c in range(ncore)]
    invc = [pool_invcnt((c % 4) * T, T, Sq) for c in range(ncore)]
    A = lambda a: np.ascontiguousarray(np.asarray(a, f32))
    for l in range(2):
        lambda_init = 0.8 - 0.6 * math.exp(-0.3 * l)
        nc = build_ffn_prog(T, False)
        g1 = gain_pc(A(ffn1_norm[l]))
        r = _launch(nc, [{"xin": xT[c], "wg": A(ffn1_w_gate[l]), "wu": A(ffn1_w_up[l]), "wd": A(ffn1_w_down[l]), "gvec": g1}
                         for c in range(ncore)])
        x1T = [r[c]["xout"] for c in range(ncore)]
        nc = build_inproj_prog(T)
        qg = np.tile(A(q_norm[l]), 2)[:, None].copy()
        kg = np.tile(A(k_norm[l]), 2)[:, None].copy()
        r = _launch(nc, [{"xin": x1T[c], "w_in": A(w_in[l]), "gvec": gain_pc(A(mix_norm[l])), "qg": qg, "kg": kg}
                         for c in range(ncore)])
        kTf, Vf, zh, uh = [], [], [], []
        for b in range(B):
            cs = range(4 * b, 4 * b + 4)
            kTf.append(np.concatenate([r[c]["kT"] for c in cs], axis=2))
            Vf.append(np.concatenate([r[c]["Vd"] for c in cs], axis=2))
            zf = np.concatenate([r[c]["zT"] for c in cs], axis=2)
            uf = np.concatenate([r[c]["upT"] for c in cs], axis=2)
            zh.append(np.pad(zf, ((0, 0), (0, 0), (HALO, HALO))))
            uh.append(np.pad(uf, ((0, 0), (0, 0), (HALO, HALO))))
        qTl = [r[c]["qT"] for c in range(ncore)]
        nc = build_attn_prog(T, Sq)
        lamv = np.ascontiguousarray(np.broadcast_to(np.stack([A(lambda_q1[l]), A(lambda_k1[l]), A(lambda_q2[l]), A(lambda_k2[l])])[None],
                                                    (128, 4, 64)))
        lamc = np.ascontiguousarray(np.broadcast_to(np.array([-lambda_init, 1.0 - lambda_init], f32)[None], (128, 2)))
        subg = A(attn_subln[l])[:, None].copy()
        r = _launch(nc, [{"qT": qTl[c], "kTf": kTf[c // 4], "Vf": Vf[c // 4], "qpos": qpos[c], "kposc": kposc, "lamv": lamv,
                          "subg": subg, "lamc": lamc} for c in range(ncore)])
        yaT = [r[c]["yT"] for c in range(ncore)]
        nc = build_mix_prog(T)
        cb, lg, lb, psc = A(conv_dw_bias[l]), A(conv_ln_gain[l]), A(conv_ln_bias[l]), A(pool_scale[l])
        cvec = np.stack([cb[:128], cb[128:], lg[:128], lg[128:], lb[:128], lb[128:], psc[:128], psc[128:]], 1).astype(f32)
        convw = np.ascontiguousarray(A(conv_dw[l]).T.reshape(2, 128, 31))
        maps = []
        for c in range(ncore):
            o = (c % 4) * T
            maps.append({"x1T": x1T[c], "zTh": np.ascontiguousarray(zh[c // 4][:, :, o:o + T + 2 * HALO]),
                         "upTh": np.ascontiguousarray(uh[c // 4][:, :, o:o + T + 2 * HALO]), "yaT": yaT[c], "w_out": A(w_out[l]),
                         "convw": convw, "cvec": cvec, "pool_w": A(pool_w[l]), "invcnt": invc[c], "ident": ident})
        r = _launch(nc, maps)
        x2T = [r[c]["x2T"] for c in range(ncore)]
        nc = build_ffn_prog(T, True)
        r = _launch(nc, [{"xin": x2T[c], "wg": A(ffn2_w_gate[l]), "wu": A(ffn2_w_up[l]), "wd": A(ffn2_w_down[l]),
                          "gvec": gain_pc(A(ffn2_norm[l])), "pgvec": gain_pc(A(post_norm[l]))} for c in range(ncore)])
        xT = [r[c]["xout"] for c in range(ncore)]
    out = np.empty((B, Sq, D), f32)
    for c in range(ncore):
        out[c // 4, (c % 4) * T:(c % 4 + 1) * T] = from_T(xT[c])
    return out
```

```python
import numpy as np
import ml_dtypes
import concourse.bass as bass
import concourse.mybir as mybir
from concourse.bass_utils import run_bass_kernel_spmd

F32 = mybir.dt.float32
BF16 = mybir.dt.bfloat16
AF = mybir.ActivationFunctionType
ALU = mybir.AluOpType

D = 1024
DFF = 2816
NFC = DFF // 128
DIN = 2304
S_LEN = 16384
TPC = 4096
NCORES = 8
EPS = 1e-6

ENGS = ("pe", "act", "dve", "pool", "sp")
SAME_ENGINE_SYNC = True
SEM_ROT = 24000
XQ = "pool"


class _Op:
    __slots__ = ("eng", "fn", "reads", "writes", "chan", "group", "deps", "sig",
                 "idx", "pos", "xdeps", "inc")


class Sched:
    def __init__(self, nc):
        self.nc = nc
        self.ops = []
        self.chan_state = {}
        self.last_on = {}

    def op(self, eng, fn, reads=(), writes=(), xdeps=()):
        o = _Op()
        o.eng, o.fn, o.reads, o.writes = eng, fn, tuple(reads), tuple(writes)
        o.chan = None
        o.group = None
        o.xdeps = tuple(xdeps)
        o.idx = len(self.ops)
        self.ops.append(o)
        self.last_on[eng] = o.idx
        return o

    def dma(self, eng, chan, fn, reads=(), writes=(), newgroup=True, inc=16):
        o = self.op(eng, fn, reads, writes)
        o.chan = chan
        o.inc = inc
        st = self.chan_state.setdefault(chan, {"groups": []})
        if newgroup or not st["groups"]:
            st["groups"].append([])
        st["groups"][-1].append(o.idx)
        o.group = len(st["groups"]) - 1
        return o

    def barrier(self):
        lasts = [i for i in self.last_on.values()]
        for st in self.chan_state.values():
            if st["groups"]:
                lasts.append(st["groups"][-1][-1])
        for e in ENGS:
            self.op(e, None, xdeps=lasts)

    def finalize(self):
        nc = self.nc
        ops = self.ops
        last_w = {}
        readers = {}
        for o in ops:
            deps = set(o.xdeps)
            for r in o.reads:
                if r in last_w:
                    deps.add(last_w[r])
            for w in o.writes:
                if w in last_w:
                    deps.add(last_w[w])
                for rd in readers.get(w, ()):
                    deps.add(rd)
            deps.discard(o.idx)
            o.deps = deps
            for r in o.reads:
                readers.setdefault(r, []).append(o.idx)
            for w in o.writes:
                last_w[w] = o.idx
                readers[w] = []
        for chan, st in self.chan_state.items():
            cum = 0
            vals = []
            for g in st["groups"]:
                cum += sum(ops[i].inc for i in g)
                vals.append(cum)
            st["vals"] = vals
            assert cum < 65000, (chan, cum)
        pos = {e: 0 for e in ENGS}
        for o in ops:
            o.pos = pos[o.eng]
            pos[o.eng] += 1
        waited = {e: {} for e in ENGS}
        need = []
        sig_needed = set()
        for o in ops:
            w = {}
            for d in o.deps:
                p = ops[d]
                if p.chan is not None:
                    if o.chan == p.chan and o.group == p.group:
                        continue
                    s = ("c", p.chan)
                    key = p.group
                else:
                    if p.fn is None:
                        continue
                    s = ("e", p.eng)
                    key = p.pos
                    if p.eng == o.eng:
                        if p.eng == "pe" or not SAME_ENGINE_SYNC:
                            continue
                if s not in w or w[s][0] < key:
                    w[s] = (key, d)
            if o.chan is not None:
                st = self.chan_state[o.chan]
                if o.group > 0 and st["groups"][o.group][0] == o.idx:
                    s = ("c", o.chan)
                    key = o.group - 1
                    if s not in w or w[s][0] < key:
                        w[s] = (key, None)
            lst = []
            for s, (key, d) in w.items():
                if waited[o.eng].get(s, -1) >= key:
                    continue
                waited[o.eng][s] = key
                lst.append((s, key, d))
                if s[0] == "e":
                    sig_needed.add(d)
            need.append(lst)
        sigcount = {e: 0 for e in ENGS}
        for o in ops:
            if o.chan is None and o.idx in sig_needed:
                sigcount[o.eng] += 1
                o.sig = sigcount[o.eng]
            else:
                o.sig = None
        self._sem_ctx = []
        eng_sems = {}
        for e in ENGS:
            n = (sigcount[e] + SEM_ROT - 1) // SEM_ROT
            eng_sems[e] = [self._alloc_sem(f"s_{e}{i}") for i in range(max(n, 1))]
        chan_sems = {}
        for chan in self.chan_state:
            chan_sems[chan] = self._alloc_sem(f"c_{chan}")
        self.nsems = sum(len(v) for v in eng_sems.values()) + len(chan_sems)
        self.counts = dict(pos)

        def eng_wait_target(p):
            k = p.sig - 1
            return eng_sems[p.eng][k // SEM_ROT], (k % SEM_ROT) + 1

        streams = {e: [] for e in ENGS}
        for o in ops:
            streams[o.eng].append(o)

        semv = {}
        ptr = {e: 0 for e in ENGS}
        progress = True
        while progress:
            progress = False
            for e in ENGS:
                while ptr[e] < len(streams[e]):
                    o = streams[e][ptr[e]]
                    ok = True
                    for (s, key, d) in need[o.idx]:
                        if s[0] == "e":
                            sk, val = ("e", ops[d].eng), ops[d].sig
                        else:
                            sk, val = s, self.chan_state[s[1]]["vals"][key]
                        if semv.get(sk, 0) < val:
                            ok = False
                            break
                    if not ok:
                        break
                    if o.chan is not None:
                        semv[("c", o.chan)] = semv.get(("c", o.chan), 0) + o.inc
                    elif o.sig is not None:
                        semv[("e", o.eng)] = semv.get(("e", o.eng), 0) + 1
                        assert semv[("e", o.eng)] == o.sig
                    ptr[e] += 1
                    progress = True
        stuck = []
        for e in ENGS:
            if ptr[e] != len(streams[e]):
                o = streams[e][ptr[e]]
                unsat = []
                for (s_, key, d) in need[o.idx]:
                    if s_[0] == "e":
                        sk, val = ("e", ops[d].eng), ops[d].sig
                    else:
                        sk, val = s_, self.chan_state[s_[1]]["vals"][key]
                    if semv.get(sk, 0) < val:
                        unsat.append((sk, val, semv.get(sk, 0)))
                stuck.append((e, ptr[e], len(streams[e]), o.reads, o.writes, o.chan, unsat))
        assert not stuck, ("DEADLOCK", stuck)

        self.need = need
        self.streams = streams
        def run_stream(e, handle):
            for o in streams[e]:
                for (s, key, d) in need[o.idx]:
                    if s[0] == "e":
                        sem, val = eng_wait_target(ops[d])
                    else:
                        sem = chan_sems[s[1]]
                        val = self.chan_state[s[1]]["vals"][key]
                    handle.wait_ge(sem, val)
                ins = o.fn(handle) if o.fn is not None else None
                if ins is None:
                    assert o.chan is None and o.sig is None, "barrier op cannot signal"
                    continue
                if o.chan is not None:
                    ins.then_inc(chan_sems[o.chan], o.inc)
                elif o.sig is not None:
                    k = o.sig - 1
                    ins.then_inc(eng_sems[o.eng][k // SEM_ROT], 1)

        with nc.Block() as block:
            if streams["pe"]:
                @block.tensor
                def _(h):
                    run_stream("pe", h)
            if streams["act"]:
                @block.scalar
                def _(h):
                    run_stream("act", h)
            if streams["dve"]:
                @block.vector
                def _(h):
                    run_stream("dve", h)
            if streams["pool"]:
                @block.gpsimd
                def _(h):
                    run_stream("pool", h)
            if streams["sp"]:
                @block.sync
                def _(h):
                    run_stream("sp", h)
        for c in reversed(self._sem_ctx):
            c.__exit__(None, None, None)

    def _alloc_sem(self, name):
        c = self.nc.semaphore(name)
        s = c.__enter__()
        self._sem_ctx.append(c)
        return s


class Arena:
    def __init__(self, nc, nbytes, name="arena"):
        self.nc = nc
        self.n32 = nbytes // 4
        self.ctx = nc.sbuf_tensor(name, [128, self.n32], F32)
        self.t = self.ctx.__enter__()
        self.off = 0
        self.marks = []

    def alloc(self, shape, dt):
        esz = 2 if dt == BF16 else 4
        n = 1
        for s in shape:
            n *= s
        nb = (n * esz + 31) // 32 * 32
        a = self.off
        assert a + nb <= self.n32 * 4, ("SBUF arena overflow", a, nb, self.n32 * 4)
        self.off += nb
        ap = self.t[:, a // 4:(a + nb) // 4]
        if dt != F32:
            ap = ap.bitcast(dt)
        ap = ap[:, 0:n]
        if len(shape) == 2:
            ap = ap.rearrange("p (a b) -> p a b", a=shape[0])
        elif len(shape) == 3:
            ap = ap.rearrange("p (a b c) -> p a b c", a=shape[0], b=shape[1])
        return ap

    def mark(self):
        self.marks.append(self.off)

    def release(self):
        self.off = self.marks.pop()

    def close(self):
        self.ctx.__exit__(None, None, None)


class Ctx:
    pass


def make_ctx(nc):
    C = Ctx()
    C.nc = nc
    C.S = Sched(nc)
    C.A = Arena(nc, 207 * 1024)
    C.psctx = nc.psum_tensor("psum_all", [128, 4096], F32)
    C.ps = C.psctx.__enter__()
    C.uid = 0
    return C


def close_ctx(C):
    C.S.finalize()
    C.psctx.__exit__(None, None, None)
    C.A.close()


def bank(C, b, n=512, off=0):
    return C.ps[:, b * 512 + off:b * 512 + off + n]


def emit_ffn(C, T, xT_in, xT_out, wg, wu, wd, gvec, post_gvec=None, tag="f", NXB=2):
    S, A, nc = C.S, C.A, C.nc
    NT = 256
    ntiles = T // NT
    A.mark()
    Wg = A.alloc([8, DFF], BF16)
    Wu = A.alloc([8, DFF], BF16)
    Wd = A.alloc([NFC, D], BF16)
    onesf = A.alloc([128], F32)
    gv = A.alloc([8], F32)
    pgv = A.alloc([8], F32) if post_gvec is not None else None
    epsb = A.alloc([1], F32)
    xt = [A.alloc([8, NT], F32) for _ in range(NXB)]
    hT = [A.alloc([8, NT], BF16) for _ in range(2)]
    aT = A.alloc([NFC, NT], BF16)
    sq = [A.alloc([NT], F32) for _ in range(2)]
    sg = [A.alloc([NT], F32) for _ in range(2)]
    rstd = A.alloc([NT], F32)
    rstd2 = A.alloc([NT], F32)
    pg = [bank(C, 0, NT), bank(C, 1, NT)]
    pu = [bank(C, 2, NT), bank(C, 3, NT)]
    pd = [bank(C, 4, NT), bank(C, 5, NT)]
    pstat = bank(C, 6, NT)
    pstat2 = bank(C, 7, NT)
    R = lambda n: f"{tag}.{n}"

    S.op("pool", lambda h: h.memset(onesf, 1.0 / D), writes=[R("onesf")])
    S.op("pool", lambda h: h.memset(epsb, EPS), writes=[R("epsb")])
    S.dma("sp", R("cst"), lambda h: h.dma_start(out=gv, in_=gvec), writes=[R("gv")])
    if post_gvec is not None:
        S.dma("sp", R("cst"), lambda h: h.dma_start(out=pgv, in_=post_gvec), writes=[R("pgv")])
    xin_v = xT_in.rearrange("c p t -> p c t")
    xout_v = xT_out.rearrange("c p t -> p c t")

    def load(t):
        b = t % NXB
        S.dma(XQ, R(f"xin{b}"), lambda h: h.dma_start(out=xt[b], in_=xin_v[:, :, t * NT:(t + 1) * NT]),
              writes=[R(f"xt{b}")])

    for _t in range(min(NXB, ntiles)):
        load(_t)
    for dc in range(8):
        S.dma("pool", R("wld"), (lambda dc: lambda h: h.dma_start(out=Wg[:, dc, :], in_=wg[dc * 128:(dc + 1) * 128, :]))(dc),
              writes=[R(f"Wg{dc}")])
        S.dma("pool", R("wld"), (lambda dc: lambda h: h.dma_start(out=Wu[:, dc, :], in_=wu[dc * 128:(dc + 1) * 128, :]))(dc),
              writes=[R(f"Wu{dc}")])
    for fc in range(NFC):
        S.dma("pool", R("wld"), (lambda fc: lambda h: h.dma_start(out=Wd[:, fc, :], in_=wd[fc * 128:(fc + 1) * 128, :]))(fc),
              writes=[R(f"Wd{fc}")])

    def stats(xbuf, xres, pst, rs, rsres):
        for c in range(8):
            k = c % 2
            S.op("act", (lambda c, k: lambda h: h.activation(out=sq[k], in_=xbuf[:, c, :], func=AF.Square))(c, k),
                 reads=[xres], writes=[R(f"sq{k}")])
            S.op("pe", (lambda c, k: lambda h: h.matmul(pst, lhsT=onesf, rhs=sq[k], start=(c == 0), stop=(c == 7)))(c, k),
                 reads=[R(f"sq{k}"), R("onesf")], writes=[rsres + ".ps"])
        S.op("act", lambda h: h.activation(out=rs, in_=pst, func=AF.Sqrt, bias=epsb[:, 0:1], scale=1.0),
             reads=[rsres + ".ps", R("epsb")], writes=[rsres])
        S.op("dve", lambda h: h.reciprocal(out=rs, in_=rs), reads=[rsres], writes=[rsres])

    def make_h(t):
        b = t % 2
        xb = t % NXB
        stats(xt[xb], R(f"xt{xb}"), pstat, rstd, R("rstd"))
        for c in range(8):
            S.op("dve", (lambda c: lambda h: h.scalar_tensor_tensor(out=hT[b][:, c, :], in0=xt[xb][:, c, :], scalar=gv[:, c:c + 1],
                                                                   in1=rstd, op0=ALU.mult, op1=ALU.mult))(c),
                 reads=[R(f"xt{xb}"), R("rstd"), R("gv")], writes=[R(f"hT{b}")])

    def gateup(t):
        b = t % 2
        for fc in range(NFC):
            k = fc % 2
            for dc in range(8):
                S.op("pe", (lambda fc, dc, k: lambda h: h.matmul(pg[k], lhsT=Wg[:, dc, fc * 128:(fc + 1) * 128], rhs=hT[b][:, dc, :],
                                                                 start=(dc == 0), stop=(dc == 7)))(fc, dc, k),
                     reads=[R(f"Wg{dc}"), R(f"hT{b}")], writes=[R(f"pg{k}")])
            for dc in range(8):
                S.op("pe", (lambda fc, dc, k: lambda h: h.matmul(pu[k], lhsT=Wu[:, dc, fc * 128:(fc + 1) * 128], rhs=hT[b][:, dc, :],
                                                                 start=(dc == 0), stop=(dc == 7)))(fc, dc, k),
                     reads=[R(f"Wu{dc}"), R(f"hT{b}")], writes=[R(f"pu{k}")])
            S.op("act", (lambda k: lambda h: h.activation(out=sg[k], in_=pg[k], func=AF.Silu))(k),
                 reads=[R(f"pg{k}")], writes=[R(f"sg{k}")])
            S.op("dve", (lambda fc, k: lambda h: h.tensor_tensor(out=aT[:, fc, :], in0=sg[k], in1=pu[k], op=ALU.mult))(fc, k),
                 reads=[R(f"sg{k}"), R(f"pu{k}")], writes=[R(f"aT{fc}")])

    def down(t):
        b = t % NXB
        for oc in range(8):
            k = oc % 2
            for fc in range(NFC):
                S.op("pe", (lambda oc, fc, k: lambda h: h.matmul(pd[k], lhsT=Wd[:, fc, oc * 128:(oc + 1) * 128], rhs=aT[:, fc, :],
                                                                 start=(fc == 0), stop=(fc == NFC - 1)))(oc, fc, k),
                     reads=[R(f"Wd{fc}"), R(f"aT{fc}")], writes=[R(f"pd{k}")])
            S.op("dve", (lambda oc, k: lambda h: h.scalar_tensor_tensor(out=xt[b][:, oc, :], in0=pd[k], scalar=0.5, in1=xt[b][:, oc, :],
                                                                       op0=ALU.mult, op1=ALU.add))(oc, k),
                 reads=[R(f"pd{k}"), R(f"xt{b}")], writes=[R(f"xt{b}")])
        if post_gvec is not None:
            stats(xt[b], R(f"xt{b}"), pstat2, rstd2, R("rstd2"))
            for c in range(8):
                S.op("dve", (lambda c: lambda h: h.scalar_tensor_tensor(out=xt[b][:, c, :], in0=xt[b][:, c, :], scalar=pgv[:, c:c + 1],
                                                                       in1=rstd2, op0=ALU.mult, op1=ALU.mult))(c),
                     reads=[R(f"xt{b}"), R("rstd2"), R("pgv")], writes=[R(f"xt{b}")])
        S.dma(XQ, R(f"xout{b}"), lambda h: h.dma_start(out=xout_v[:, :, t * NT:(t + 1) * NT], in_=xt[b]),
              reads=[R(f"xt{b}")], writes=[R("xout")])

    make_h(0)
    for t in range(ntiles):
        gateup(t)
        if t + 1 < ntiles:
            make_h(t + 1)
        down(t)
        if t + NXB < ntiles:
            load(t + NXB)
    A.release()


_prog_cache = {}


def build_ffn_prog(T, post):
    key = ("ffn", T, post)
    if key in _prog_cache:
        return _prog_cache[key]
    nc = bass.Bass("TRN2", target_bir_lowering=False)
    xin = nc.dram_tensor("xin", [8, 128, T], F32, kind="ExternalInput").ap()
    wg = nc.dram_tensor("wg", [D, DFF], F32, kind="ExternalInput").ap()
    wu = nc.dram_tensor("wu", [D, DFF], F32, kind="ExternalInput").ap()
    wd = nc.dram_tensor("wd", [DFF, D], F32, kind="ExternalInput").ap()
    gvec = nc.dram_tensor("gvec", [128, 8], F32, kind="ExternalInput").ap()
    pg = nc.dram_tensor("pgvec", [128, 8], F32, kind="ExternalInput").ap() if post else None
    xout = nc.dram_tensor("xout", [8, 128, T], F32, kind="ExternalOutput").ap()
    C = make_ctx(nc)
    emit_ffn(C, T, xin, xout, wg, wu, wd, gvec, pg)
    C.S.op("sp", lambda h: None, reads=["f.xout"])
    close_ctx(C)
    _prog_cache[key] = nc
    return nc


def to_T(x2d):
    T = x2d.shape[0]
    return np.ascontiguousarray(x2d.T.reshape(8, 128, T))


def from_T(xT):
    T = xT.shape[2]
    return np.ascontiguousarray(xT.reshape(1024, T).T)


def gain_pc(g):
    return np.ascontiguousarray(g.reshape(8, 128).T).astype(np.float32)


def run_ffn(xT_list, wg, wu, wd, g, post_g=None):
    T = xT_list[0].shape[2]
    nc = build_ffn_prog(T, post_g is not None)
    maps = []
    for xT in xT_list:
        m = {"xin": xT, "wg": wg, "wu": wu, "wd": wd, "gvec": gain_pc(g)}
        if post_g is not None:
            m["pgvec"] = gain_pc(post_g)
        maps.append(m)
    res = run_bass_kernel_spmd(nc, maps, core_ids=list(range(len(maps))))
    return [r["xout"] for r in res.results]


def emit_inproj(C, T, xT_in, w_in, gvec, qg, kg, zT, upT, qT, kT, Vd, tag="i", zoff=0):
    S, A, nc = C.S, C.A, C.nc
    NT = 256
    ntiles = T // NT
    A.mark()
    W = A.alloc([8, DIN], BF16)
    onesf = A.alloc([128], F32)
    blk = A.alloc([128], F32)
    gv = A.alloc([8], F32)
    qgv = A.alloc([1], F32)
    kgv = A.alloc([1], F32)
    epsb = A.alloc([1], F32)
    eps64 = A.alloc([1], F32)
    xt = [A.alloc([8, NT], F32) for _ in range(2)]
    hT = A.alloc([8, NT], BF16)
    sq = [A.alloc([NT], F32) for _ in range(2)]
    rstd = A.alloc([NT], F32)
    sgm = [A.alloc([NT], F32) for _ in range(2)]
    rq = [A.alloc([NT], F32) for _ in range(2)]
    ob = [A.alloc([NT], BF16) for _ in range(4)]
    vb = [A.alloc([512], BF16) for _ in range(2)]
    pu = [bank(C, 0, NT), bank(C, 1, NT), bank(C, 2, NT), bank(C, 3, NT)]
    pv = [bank(C, 4, 512), bank(C, 5, 512)]
    pstat = bank(C, 6, NT)
    pqs = bank(C, 7, NT)
    R = lambda n: f"{tag}.{n}"

    S.op("pool", lambda h: h.memset(onesf, 1.0 / D), writes=[R("onesf")])
    S.op("pool", lambda h: h.memset(blk, 0.0), writes=[R("blk")])
    S.op("pool", lambda h: h.memset(blk[0:64, 0:64], 1.0 / 64), writes=[R("blk")])
    S.op("pool", lambda h: h.memset(blk[64:128, 64:128], 1.0 / 64), writes=[R("blk")])
    S.op("pool", lambda h: h.memset(epsb, EPS), writes=[R("epsb")])
    S.op("pool", lambda h: h.memset(eps64, 64.0 * EPS), writes=[R("eps64")])
    S.dma("pool", R("cst"), lambda h: h.dma_start(out=gv, in_=gvec), writes=[R("gv")])
    S.dma("pool", R("cst"), lambda h: h.dma_start(out=qgv, in_=qg), writes=[R("qgv")])
    S.dma("pool", R("cst"), lambda h: h.dma_start(out=kgv, in_=kg), writes=[R("kgv")])
    xin_v = xT_in.rearrange("c p t -> p c t")

    def load(t):
        b = t % 2
        S.dma(XQ, R(f"xin{b}"), lambda h: h.dma_start(out=xt[b], in_=xin_v[:, :, t * NT:(t + 1) * NT]),
              writes=[R(f"xt{b}")])

    load(0)
    if ntiles > 1:
        load(1)
    for dc in range(8):
        S.dma("pool", R("wld"), (lambda dc: lambda h: h.dma_start(out=W[:, dc, :], in_=w_in[dc * 128:(dc + 1) * 128, :]))(dc),
              writes=[R(f"W{dc}")])

    ocount = [0]

    def out_store(dst_ap, src, srcres):
        k = ocount[0] % 4
        ocount[0] += 1
        return k

    def do_tile(t):
        b = t % 2
        xb, xres = xt[b], R(f"xt{b}")
        tsl = slice(t * NT, (t + 1) * NT)
        for c in range(8):
            k = c % 2
            S.op("act", (lambda c, k: lambda h: h.activation(out=sq[k], in_=xb[:, c, :], func=AF.Square))(c, k),
                 reads=[xres], writes=[R(f"sq{k}")])
            S.op("pe", (lambda c, k: lambda h: h.matmul(pstat, lhsT=onesf, rhs=sq[k], start=(c == 0), stop=(c == 7)))(c, k),
                 reads=[R(f"sq{k}"), R("onesf")], writes=[R("pstat")])
        S.op("act", lambda h: h.activation(out=rstd, in_=pstat, func=AF.Sqrt, bias=epsb[:, 0:1], scale=1.0),
             reads=[R("pstat"), R("epsb")], writes=[R("rstd")])
        S.op("dve", lambda h: h.reciprocal(out=rstd, in_=rstd), reads=[R("rstd")], writes=[R("rstd")])
        for c in range(8):
            S.op("dve", (lambda c: lambda h: h.scalar_tensor_tensor(out=hT[:, c, :], in0=xb[:, c, :], scalar=gv[:, c:c + 1],
                                                                   in1=rstd, op0=ALU.mult, op1=ALU.mult))(c),
                 reads=[xres, R("rstd"), R("gv")], writes=[R("hT")])
        if t + 2 < ntiles:
            pass

        def proj(j, pk):
            for dc in range(8):
                S.op("pe", (lambda dc: lambda h: h.matmul(pu[pk], lhsT=W[:, dc, j * 128:(j + 1) * 128], rhs=hT[:, dc, :],
                                                          start=(dc == 0), stop=(dc == 7)))(dc),
                     reads=[R(f"W{dc}"), R("hT")], writes=[R(f"pu{pk}")])

        def store(dst, k):
            S.dma(XQ, R(f"ost{k}"), lambda h: h.dma_start(out=dst, in_=ob[k]), reads=[R(f"ob{k}")], writes=[R("outs")])

        for j in range(2):
            proj(2 + j, 0)
            proj(j, 1)
            k = ocount[0] % 4
            ocount[0] += 1
            S.op("act", (lambda j: lambda h: h.activation(out=sgm[j], in_=pu[0], func=AF.Sigmoid))(j),
                 reads=[R("pu0")], writes=[R(f"sgm{j}")])
            S.op("dve", (lambda j, k: lambda h: h.tensor_tensor(out=ob[k], in0=sgm[j], in1=pu[1], op=ALU.mult))(j, k),
                 reads=[R(f"sgm{j}"), R("pu1")], writes=[R(f"ob{k}")])
            store(zT[j, :, t * NT + zoff:(t + 1) * NT + zoff], k)
        for j in range(2):
            pk = 2 + j
            proj(4 + j, pk)
            k = ocount[0] % 4
            ocount[0] += 1
            S.op("dve", (lambda pk, k: lambda h: h.tensor_copy(out=ob[k], in_=pu[pk]))(pk, k),
                 reads=[R(f"pu{pk}")], writes=[R(f"ob{k}")])
            store(upT[j, :, t * NT + zoff:(t + 1) * NT + zoff], k)
        for j in range(8):
            pk = j % 4
            r = j % 2
            isq = j < 4
            proj(6 + j, pk)
            k = ocount[0] % 4
            ocount[0] += 1
            S.op("act", (lambda pk, r: lambda h: h.activation(out=sq[r], in_=pu[pk], func=AF.Square))(pk, r),
                 reads=[R(f"pu{pk}")], writes=[R(f"sq{r}")])
            S.op("pe", (lambda r: lambda h: h.matmul(pqs, lhsT=blk, rhs=sq[r], start=True, stop=True))(r),
                 reads=[R(f"sq{r}"), R("blk")], writes=[R("pqs")])
            if isq:
                S.op("act", (lambda r: lambda h: h.activation(out=rq[r], in_=pqs, func=AF.Sqrt, bias=eps64[:, 0:1], scale=64.0))(r),
                     reads=[R("pqs"), R("eps64")], writes=[R(f"rq{r}")])
            else:
                S.op("act", (lambda r: lambda h: h.activation(out=rq[r], in_=pqs, func=AF.Sqrt, bias=epsb[:, 0:1], scale=1.0))(r),
                     reads=[R("pqs"), R("epsb")], writes=[R(f"rq{r}")])
            S.op("dve", (lambda r: lambda h: h.reciprocal(out=rq[r], in_=rq[r]))(r), reads=[R(f"rq{r}")], writes=[R(f"rq{r}")])
            gvv = qgv if isq else kgv
            S.op("dve", (lambda pk, r, k, gvv: lambda h: h.scalar_tensor_tensor(out=ob[k], in0=pu[pk], scalar=gvv[:, 0:1], in1=rq[r],
                                                                               op0=ALU.mult, op1=ALU.mult))(pk, r, k, gvv),
                 reads=[R(f"pu{pk}"), R(f"rq{r}"), R("qgv"), R("kgv")], writes=[R(f"ob{k}")])
            dst = (qT if isq else kT)[j % 4, :, tsl]
            store(dst, k)
        def vpart(s):
            k = s % 2
            for dc in range(8):
                S.op("pe", (lambda dc: lambda h: h.matmul(pv[k], lhsT=hT[:, dc, s * 128:(s + 1) * 128], rhs=W[:, dc, 1792:2304],
                                                          start=(dc == 0), stop=(dc == 7)))(dc),
                     reads=[R(f"W{dc}"), R("hT")], writes=[R(f"pv{k}")])
            S.op("act", lambda h: h.activation(out=vb[k], in_=pv[k], func=AF.Copy), reads=[R(f"pv{k}")], writes=[R(f"vb{k}")])
            cidx = t * (NT // 128) + s
            S.dma(XQ, R(f"vst{k}"), lambda h: h.dma_start(out=Vd[:, :, cidx, :].rearrange("h p e -> p h e"),
                                                          in_=vb[k].rearrange("p (h e) -> p h e", h=4)),
                  reads=[R(f"vb{k}")], writes=[R("outs")])

        for s in range(NT // 128):
            vpart(s)
        if t + 2 < ntiles:
            load(t + 2)

    for t in range(ntiles):
        do_tile(t)
    A.release()


def build_inproj_prog(T):
    key = ("inproj", T)
    if key in _prog_cache:
        return _prog_cache[key]
    nc = bass.Bass("TRN2", target_bir_lowering=False)
    xin = nc.dram_tensor("xin", [8, 128, T], F32, kind="ExternalInput").ap()
    w_in = nc.dram_tensor("w_in", [D, DIN], F32, kind="ExternalInput").ap()
    gvec = nc.dram_tensor("gvec", [128, 8], F32, kind="ExternalInput").ap()
    qg = nc.dram_tensor("qg", [128, 1], F32, kind="ExternalInput").ap()
    kg = nc.dram_tensor("kg", [128, 1], F32, kind="ExternalInput").ap()
    zT = nc.dram_tensor("zT", [2, 128, T], BF16, kind="ExternalOutput").ap()
    upT = nc.dram_tensor("upT", [2, 128, T], BF16, kind="ExternalOutput").ap()
    qT = nc.dram_tensor("qT", [4, 128, T], BF16, kind="ExternalOutput").ap()
    kT = nc.dram_tensor("kT", [4, 128, T], BF16, kind="ExternalOutput").ap()
    Vd = nc.dram_tensor("Vd", [4, 128, T // 128, 128], BF16, kind="ExternalOutput").ap()
    C = make_ctx(nc)
    emit_inproj(C, T, xin, w_in, gvec, qg, kg, zT, upT, qT, kT, Vd)
    C.S.op("pool", lambda h: None, reads=["i.outs"])
    close_ctx(C)
    _prog_cache[key] = nc
    return nc


def emit_attn(C, T, Sq, qT, kTf, Vf, qpos, kposc, lamv, subg, lamc, yT, tag="a", gathered=False):
    S, A, nc = C.S, C.A, C.nc
    NQ = 512
    nqt = T // NQ
    nkc = Sq // 128
    A.mark()
    Kt = A.alloc([Sq], BF16)
    Vt = A.alloc([nkc, 128], BF16)
    kpc = A.alloc([nkc], F32)
    negk = A.alloc([nkc], F32)
    onesb = A.alloc([128], BF16)
    ones128 = A.alloc([128], F32)
    epsb = A.alloc([1], F32)
    lamt = A.alloc([4, 64], F32)
    lprod = A.alloc([2, 64], F32)
    lsum = A.alloc([2], F32)
    neglam = A.alloc([1], F32)
    sgv = A.alloc([1], F32)
    lct = A.alloc([2], F32)
    Qt = [A.alloc([NQ], BF16) for _ in range(2)]
    qpb = [A.alloc([NQ], F32) for _ in range(2)]
    dist = [A.alloc([NQ], F32) for _ in range(2)]
    tmp = [A.alloc([2 * NQ], F32) for _ in range(2)]
    Pt = [A.alloc([2 * NQ], BF16) for _ in range(2)]
    r1 = A.alloc([NQ], F32)
    o1 = A.alloc([NQ], F32)
    o2 = A.alloc([NQ], F32)
    sqo = A.alloc([NQ], F32)
    rs = A.alloc([NQ], F32)
    yb = [A.alloc([NQ], BF16) for _ in range(2)]
    R = lambda n: f"{tag}.{n}"
    psS = [C.ps[:, 0:1024], C.ps[:, 1024:2048]]
    acc = [bank(C, 4), bank(C, 5)]
    den = [bank(C, 6), bank(C, 7)]

    S.op("pool", lambda h: h.memset(onesb, 1.0), writes=[R("onesb")])
    S.op("pool", lambda h: h.memset(ones128, 1.0 / 128), writes=[R("ones128")])
    S.op("pool", lambda h: h.memset(epsb, EPS), writes=[R("epsb")])
    S.dma("pool", R("cst"), lambda h: h.dma_start(out=kpc, in_=kposc), writes=[R("kpc")])
    S.dma("pool", R("cst"), lambda h: h.dma_start(out=lamt, in_=lamv), writes=[R("lamt")])
    S.dma("pool", R("cst"), lambda h: h.dma_start(out=sgv, in_=subg), writes=[R("sgv")])
    S.dma("pool", R("cst"), lambda h: h.dma_start(out=lct, in_=lamc), writes=[R("lct")])
    S.op("dve", lambda h: h.tensor_tensor(out=lprod[:, 0, :], in0=lamt[:, 0, :], in1=lamt[:, 1, :], op=ALU.mult),
         reads=[R("lamt")], writes=[R("lprod")])
    S.op("dve", lambda h: h.tensor_tensor(out=lprod[:, 1, :], in0=lamt[:, 2, :], in1=lamt[:, 3, :], op=ALU.mult),
         reads=[R("lamt")], writes=[R("lprod")])
    S.op("dve", lambda h: h.reduce_sum(out=lsum, in_=lprod, axis=mybir.AxisListType.X), reads=[R("lprod")], writes=[R("lsum")])
    S.op("act", lambda h: h.activation(out=lsum, in_=lsum, func=AF.Exp), reads=[R("lsum")], writes=[R("lsum")])
    S.op("dve", lambda h: h.tensor_tensor(out=neglam, in0=lsum[:, 1:2], in1=lsum[:, 0:1], op=ALU.subtract),
         reads=[R("lsum")], writes=[R("neglam")])
    S.op("dve", lambda h: h.tensor_scalar(out=neglam, in0=neglam, scalar1=lct[:, 0:1], scalar2=None, op0=ALU.add),
         reads=[R("neglam"), R("lct")], writes=[R("neglam")])
    S.op("dve", lambda h: h.tensor_scalar(out=sgv, in0=sgv, scalar1=lct[:, 1:2], scalar2=None, op0=ALU.mult),
         reads=[R("sgv"), R("lct")], writes=[R("sgv")])

    cnt = [0]

    def head(hd):
        slope = 2.0 ** (-8.0 * (hd + 1) / 4)
        if gathered:
            S.dma("pool", R("kld"), lambda h: h.dma_start(out=Kt.rearrange("p (r t) -> p r t", r=4),
                                                          in_=kTf[hd].rearrange("r p t -> p r t")), writes=[R("Kt")])
            S.dma("pool", R("vld"), lambda h: h.dma_start(out=Vt.rearrange("p (r c) e -> p r (c e)", r=4),
                                                          in_=Vf[hd].rearrange("r p c e -> p r (c e)")), writes=[R("Vt")])
        else:
            S.dma("pool", R("kld"), lambda h: h.dma_start(out=Kt, in_=kTf[hd]), writes=[R("Kt")])
            S.dma("pool", R("vld"), lambda h: h.dma_start(out=Vt, in_=Vf[hd]), writes=[R("Vt")])
        S.op("dve", lambda h: h.tensor_scalar(out=negk, in0=kpc, scalar1=-slope, scalar2=None, op0=ALU.mult),
             reads=[R("kpc")], writes=[R("negk")])

        def qtile(qt):
            qb = (hd * nqt + qt) % 2
            qs = slice(qt * NQ, (qt + 1) * NQ)
            S.dma("pool", R(f"qld{qb}"), lambda h: h.dma_start(out=Qt[qb], in_=qT[hd, :, qs]), writes=[R(f"Qt{qb}")])
            S.dma("pool", R(f"pld{qb}"), lambda h: h.dma_start(out=qpb[qb], in_=qpos[:, qs]), writes=[R(f"qpb{qb}")])

            def score(kc):
                i = kc % 2
                ks = slice(kc * 128, (kc + 1) * 128)
                S.op("pe", lambda h: h.matmul(psS[i][:, 0:NQ], lhsT=Kt[0:64, ks], rhs=Qt[qb][0:64, :], start=True, stop=True),
                     reads=[R("Kt"), R(f"Qt{qb}")], writes=[R(f"psS{i}a")])
                S.op("pe", lambda h: h.matmul(psS[i][:, NQ:2 * NQ], lhsT=Kt[64:128, ks], rhs=Qt[qb][64:128, :], start=True, stop=True),
                     reads=[R("Kt"), R(f"Qt{qb}")], writes=[R(f"psS{i}b")])
                S.op("act", lambda h: h.activation(out=dist[i], in_=qpb[qb], func=AF.Abs, bias=negk[:, kc:kc + 1], scale=slope),
                     reads=[R(f"qpb{qb}"), R("negk")], writes=[R(f"dist{i}")])
                S.op("dve", lambda h: h.tensor_tensor(out=tmp[i][:, 0:NQ], in0=psS[i][:, 0:NQ], in1=dist[i], op=ALU.subtract),
                     reads=[R(f"dist{i}"), R(f"psS{i}a")], writes=[R(f"tmp{i}")])
                S.op("dve", lambda h: h.tensor_tensor(out=tmp[i][:, NQ:2 * NQ], in0=psS[i][:, NQ:2 * NQ], in1=dist[i], op=ALU.subtract),
                     reads=[R(f"dist{i}"), R(f"psS{i}b")], writes=[R(f"tmp{i}")])
                S.op("act", lambda h: h.activation(out=Pt[i], in_=tmp[i], func=AF.Exp), reads=[R(f"tmp{i}")], writes=[R(f"Pt{i}")])

            def accum(kc):
                i = kc % 2
                first, last = (kc == 0), (kc == nkc - 1)
                for m in range(2):
                    S.op("pe", (lambda m: lambda h: h.matmul(acc[m], lhsT=Vt[:, kc, :], rhs=Pt[i][:, m * NQ:(m + 1) * NQ],
                                                             start=first, stop=last))(m),
                         reads=[R("Vt"), R(f"Pt{i}")], writes=[R(f"acc{m}")])
                    S.op("pe", (lambda m: lambda h: h.matmul(den[m], lhsT=onesb, rhs=Pt[i][:, m * NQ:(m + 1) * NQ],
                                                             start=first, stop=last))(m),
                         reads=[R("onesb"), R(f"Pt{i}")], writes=[R(f"den{m}")])

            score(0)
            for kc in range(nkc):
                if kc + 1 < nkc:
                    score(kc + 1)
                accum(kc)
            S.op("dve", lambda h: h.reciprocal(out=r1, in_=den[0]), reads=[R("den0")], writes=[R("r1")])
            S.op("dve", lambda h: h.tensor_tensor(out=o1, in0=acc[0], in1=r1, op=ALU.mult), reads=[R("acc0"), R("r1")], writes=[R("o1")])
            S.op("dve", lambda h: h.reciprocal(out=r1, in_=den[1]), reads=[R("den1"), R("o1")], writes=[R("r1")])
            S.op("dve", lambda h: h.tensor_tensor(out=o2, in0=acc[1], in1=r1, op=ALU.mult), reads=[R("acc1"), R("r1")], writes=[R("o2")])
            S.op("dve", lambda h: h.scalar_tensor_tensor(out=o1, in0=o2, scalar=neglam[:, 0:1], in1=o1, op0=ALU.mult, op1=ALU.add),
                 reads=[R("o2"), R("o1"), R("neglam")], writes=[R("o1")])
            S.op("act", lambda h: h.activation(out=sqo, in_=o1, func=AF.Square), reads=[R("o1")], writes=[R("sqo")])
            S.op("pe", lambda h: h.matmul(den[0], lhsT=ones128, rhs=sqo, start=True, stop=True),
                 reads=[R("ones128"), R("sqo")], writes=[R("den0")])
            S.op("act", lambda h: h.activation(out=rs, in_=den[0], func=AF.Sqrt, bias=epsb[:, 0:1], scale=1.0),
                 reads=[R("den0"), R("epsb")], writes=[R("rs")])
            S.op("dve", lambda h: h.reciprocal(out=rs, in_=rs), reads=[R("rs")], writes=[R("rs")])
            S.op("dve", lambda h: h.scalar_tensor_tensor(out=yb[qb], in0=o1, scalar=sgv[:, 0:1], in1=rs, op0=ALU.mult, op1=ALU.mult),
                 reads=[R("o1"), R("rs"), R("sgv")], writes=[R(f"yb{qb}")])
            S.dma("pool", R(f"yst{qb}"), lambda h: h.dma_start(out=yT[hd, :, qs], in_=yb[qb]), reads=[R(f"yb{qb}")], writes=[R("outs")])

        for qt in range(nqt):
            qtile(qt)

    for hd in range(4):
        head(hd)
    A.release()


def build_attn_prog(T, Sq):
    key = ("attn", T, Sq)
    if key in _prog_cache:
        return _prog_cache[key]
    nc = bass.Bass("TRN2", target_bir_lowering=False)
    qT = nc.dram_tensor("qT", [4, 128, T], BF16, kind="ExternalInput").ap()
    kTf = nc.dram_tensor("kTf", [4, 128, Sq], BF16, kind="ExternalInput").ap()
    Vf = nc.dram_tensor("Vf", [4, 128, Sq // 128, 128], BF16, kind="ExternalInput").ap()
    qpos = nc.dram_tensor("qpos", [128, T], F32, kind="ExternalInput").ap()
    kposc = nc.dram_tensor("kposc", [128, Sq // 128], F32, kind="ExternalInput").ap()
    lamv = nc.dram_tensor("lamv", [128, 4, 64], F32, kind="ExternalInput").ap()
    subg = nc.dram_tensor("subg", [128, 1], F32, kind="ExternalInput").ap()
    lamc = nc.dram_tensor("lamc", [128, 2], F32, kind="ExternalInput").ap()
    yT = nc.dram_tensor("yT", [4, 128, T], BF16, kind="ExternalOutput").ap()
    C = make_ctx(nc)
    emit_attn(C, T, Sq, qT, kTf, Vf, qpos, kposc, lamv, subg, lamc, yT)
    C.S.op("pool", lambda h: None, reads=["a.outs"])
    close_ctx(C)
    _prog_cache[key] = nc
    return nc


HALO = 16


def emit_mix(C, T, x1T, zTh, upTh, yaT, w_out, convw, cvec, pool_w, invcnt, ident, x2T, tag="m"):
    S, A, nc = C.S, C.A, C.nc
    NT = 256
    ntiles = T // NT
    A.mark()
    Wo = A.alloc([8, D], BF16)
    idt = A.alloc([128], F32)
    cw = A.alloc([2, 31], F32)
    cv = A.alloc([8], F32)
    Dg = A.alloc([2, 31, 128], BF16)
    pst = A.alloc([2, 128], F32)
    PWf = A.alloc([2, 128], BF16)
    PWh = A.alloc([2, 128], BF16)
    ones256 = A.alloc([128], F32)
    epsb = A.alloc([1], F32)
    xt = [A.alloc([8, NT], F32) for _ in range(2)]
    zt = [A.alloc([2, NT + 2 * HALO], BF16) for _ in range(2)]
    ut = [A.alloc([2, NT + 2 * HALO], BF16) for _ in range(2)]
    ic = [A.alloc([2, NT], F32) for _ in range(2)]
    ycat = [A.alloc([8, NT], BF16) for _ in range(2)]
    cz = A.alloc([2, NT], F32)
    sqc = A.alloc([2, NT], F32)
    mean = A.alloc([NT], F32)
    m2 = A.alloc([NT], F32)
    rstd = A.alloc([NT], F32)
    t1 = A.alloc([2, NT], F32)
    pa = A.alloc([NT], F32)
    R = lambda n: f"{tag}.{n}"
    pc = bank(C, 0, NT)
    pm = bank(C, 1, NT)
    pvv = bank(C, 2, NT)
    pA = bank(C, 3, NT)
    pB = bank(C, 4, NT)
    po = [bank(C, 5, NT), bank(C, 6, NT)]

    S.op("pool", lambda h: h.memset(ones256, 1.0 / 256), writes=[R("ones256")])
    S.op("pool", lambda h: h.memset(epsb, EPS), writes=[R("epsb")])
    S.op("pool", lambda h: h.memset(pst, 0.0), writes=[R("pst")])
    S.dma("pool", R("cst"), lambda h: h.dma_start(out=idt, in_=ident), writes=[R("idt")])
    S.dma("pool", R("cst"), lambda h: h.dma_start(out=cw, in_=convw.rearrange("c p k -> p c k")), writes=[R("cw")])
    S.dma("pool", R("cst"), lambda h: h.dma_start(out=cv, in_=cvec), writes=[R("cv")])
    for g in range(4):
        c, r = g // 2, g % 2
        S.dma("pool", R("cst"), (lambda g, c, r: lambda h: h.dma_start(out=pst[64 * r:64 * r + 64, c, 64 * r:64 * r + 64], in_=pool_w[g]))(g, c, r),
              reads=[R("pst")], writes=[R(f"pst{g}")])
    for c in range(2):
        S.op("dve", (lambda c: lambda h: h.tensor_copy(out=PWf[:, c, :], in_=pst[:, c, :]))(c),
             reads=[R("pst0"), R("pst1"), R("pst2"), R("pst3")], writes=[R("PWf")])
        S.op("dve", (lambda c: lambda h: h.tensor_copy(out=PWh[:, c, :], in_=pst[:, c, :]))(c),
             reads=[R("pst0"), R("pst1"), R("pst2"), R("pst3")], writes=[R("PWh")])
        S.op("dve", (lambda c: lambda h: h.memset(PWh[0:64, c, :], 0.0))(c), reads=[R("PWh")], writes=[R("PWh")])
        for k in range(31):
            S.op("dve", (lambda c, k: lambda h: h.tensor_scalar(out=Dg[:, c, k, :], in0=idt, scalar1=cw[:, c, k:k + 1], scalar2=None,
                                                               op0=ALU.mult))(c, k),
                 reads=[R("idt"), R("cw")], writes=[R("Dg")])
    for dc in range(8):
        S.dma("pool", R("wld"), (lambda dc: lambda h: h.dma_start(out=Wo[:, dc, :], in_=w_out[dc * 128:(dc + 1) * 128, :]))(dc),
              writes=[R(f"Wo{dc}")])
    x1v = x1T.rearrange("c p t -> p c t")
    x2v = x2T.rearrange("c p t -> p c t")
    zv = zTh.rearrange("c p t -> p c t")
    uv = upTh.rearrange("c p t -> p c t")
    yav = yaT.rearrange("c p t -> p c t")
    icv = invcnt.rearrange("c p t -> p c t")

    def load(t):
        b = t % 2
        ts_ = slice(t * NT, (t + 1) * NT)
        th = slice(t * NT, (t + 1) * NT + 2 * HALO)
        S.dma(XQ, R(f"ld{b}"), lambda h: h.dma_start(out=xt[b], in_=x1v[:, :, ts_]), writes=[R(f"xt{b}")])
        S.dma(XQ, R(f"ld{b}"), lambda h: h.dma_start(out=zt[b], in_=zv[:, :, th]), writes=[R(f"zt{b}")], newgroup=False)
        S.dma(XQ, R(f"ld{b}"), lambda h: h.dma_start(out=ut[b], in_=uv[:, :, th]), writes=[R(f"ut{b}")], newgroup=False)
        S.dma(XQ, R(f"ld{b}"), lambda h: h.dma_start(out=ic[b], in_=icv[:, :, ts_]), writes=[R(f"ic{b}")], newgroup=False)
        S.dma(XQ, R(f"ld{b}"), lambda h: h.dma_start(out=ycat[b][:, 4:8, :], in_=yav[:, :, ts_]), writes=[R(f"ya{b}")], newgroup=False)

    def do_tile(t):
        b = t % 2
        ts_ = slice(t * NT, (t + 1) * NT)
        for c in range(2):
            for k in range(31):
                S.op("pe", (lambda c, k: lambda h: h.matmul(pc, lhsT=Dg[:, c, k, :], rhs=zt[b][:, c, k + 1:k + 1 + NT],
                                                            start=(k == 0), stop=(k == 30)))(c, k),
                     reads=[R("Dg"), R(f"zt{b}")], writes=[R("pc")])
            S.op("act", (lambda c: lambda h: h.activation(out=cz[:, c, :], in_=pc, func=AF.Identity, bias=cv[:, c:c + 1], scale=1.0))(c),
                 reads=[R("pc"), R("cv")], writes=[R(f"cz{c}")])
            S.op("act", (lambda c: lambda h: h.activation(out=sqc[:, c, :], in_=cz[:, c, :], func=AF.Square))(c),
                 reads=[R(f"cz{c}")], writes=[R(f"sqc{c}")])
        for c in range(2):
            S.op("pe", (lambda c: lambda h: h.matmul(pm, lhsT=ones256, rhs=cz[:, c, :], start=(c == 0), stop=(c == 1)))(c),
                 reads=[R("ones256"), R(f"cz{c}")], writes=[R("pm")])
        for c in range(2):
            S.op("pe", (lambda c: lambda h: h.matmul(pvv, lhsT=ones256, rhs=sqc[:, c, :], start=(c == 0), stop=(c == 1)))(c),
                 reads=[R("ones256"), R(f"sqc{c}")], writes=[R("pvv")])
        S.op("dve", lambda h: h.tensor_copy(out=mean, in_=pm), reads=[R("pm")], writes=[R("mean")])
        S.op("dve", lambda h: h.tensor_tensor(out=m2, in0=mean, in1=mean, op=ALU.mult), reads=[R("mean")], writes=[R("m2")])
        S.op("dve", lambda h: h.tensor_tensor(out=m2, in0=pvv, in1=m2, op=ALU.subtract), reads=[R("pvv"), R("m2")], writes=[R("m2")])
        S.op("act", lambda h: h.activation(out=rstd, in_=m2, func=AF.Sqrt, bias=epsb[:, 0:1], scale=1.0),
             reads=[R("m2"), R("epsb")], writes=[R("rstd")])
        S.op("dve", lambda h: h.reciprocal(out=rstd, in_=rstd), reads=[R("rstd")], writes=[R("rstd")])
        for c in range(2):
            S.op("dve", (lambda c: lambda h: h.tensor_tensor(out=t1[:, c, :], in0=cz[:, c, :], in1=mean, op=ALU.subtract))(c),
                 reads=[R(f"cz{c}"), R("mean")], writes=[R(f"t1{c}")])
            S.op("dve", (lambda c: lambda h: h.scalar_tensor_tensor(out=t1[:, c, :], in0=t1[:, c, :], scalar=cv[:, 2 + c:3 + c], in1=rstd,
                                                                   op0=ALU.mult, op1=ALU.mult))(c),
                 reads=[R(f"t1{c}"), R("rstd"), R("cv")], writes=[R(f"t1{c}")])
            S.op("act", (lambda c: lambda h: h.activation(out=ycat[b][:, c, :], in_=t1[:, c, :], func=AF.Silu, bias=cv[:, 4 + c:5 + c], scale=1.0))(c),
                 reads=[R(f"t1{c}"), R("cv")], writes=[R(f"yc{b}")])
        for c in range(2):
            taps = list(range(-2, 2)) if c == 0 else list(range(-8, 8))
            narrow = (1, ) if c == 0 else (4, )
            for i, tau in enumerate(taps):
                full = (-narrow[0] <= tau <= narrow[0] - 1)
                Wm = PWf if full else PWh
                S.op("pe", (lambda c, tau, Wm, i, n: lambda h: h.matmul(pA, lhsT=Wm[:, c, :], rhs=ut[b][:, c, HALO + tau:HALO + tau + NT],
                                                                        start=(i == 0), stop=(i == n - 1)))(c, tau, Wm, i, len(taps)),
                     reads=[R("PWf"), R("PWh"), R(f"ut{b}")], writes=[R("pA")])
            S.op("pe", (lambda c: lambda h: h.matmul(pB, lhsT=PWf[:, c, :], rhs=ut[b][:, c, HALO:HALO + NT], start=True, stop=True))(c),
                 reads=[R("PWf"), R(f"ut{b}")], writes=[R("pB")])
            S.op("dve", (lambda c: lambda h: h.tensor_tensor(out=pa, in0=pA, in1=ic[b][:, c, :], op=ALU.mult))(c),
                 reads=[R("pA"), R(f"ic{b}")], writes=[R("pa")])
            S.op("dve", (lambda c: lambda h: h.tensor_tensor(out=pa, in0=pa, in1=pB, op=ALU.subtract))(c),
                 reads=[R("pa"), R("pB")], writes=[R("pa")])
            S.op("dve", (lambda c: lambda h: h.tensor_scalar(out=ycat[b][:, 2 + c, :], in0=pa, scalar1=cv[:, 6 + c:7 + c], scalar2=None,
                                                            op0=ALU.mult))(c),
                 reads=[R("pa"), R("cv")], writes=[R(f"yc{b}")])
        for oc in range(8):
            k = oc % 2
            for c in range(8):
                S.op("pe", (lambda oc, c, k: lambda h: h.matmul(po[k], lhsT=Wo[:, c, oc * 128:(oc + 1) * 128], rhs=ycat[b][:, c, :],
                                                                start=(c == 0), stop=(c == 7)))(oc, c, k),
                     reads=[R(f"Wo{c}"), R(f"yc{b}"), R(f"ya{b}")], writes=[R(f"po{k}")])
            S.op("dve", (lambda oc, k: lambda h: h.tensor_tensor(out=xt[b][:, oc, :], in0=po[k], in1=xt[b][:, oc, :], op=ALU.add))(oc, k),
                 reads=[R(f"po{k}"), R(f"xt{b}")], writes=[R(f"xt{b}")])
        S.dma(XQ, R(f"st{b}"), lambda h: h.dma_start(out=x2v[:, :, ts_], in_=xt[b]), reads=[R(f"xt{b}")], writes=[R("outs")])
        if t + 2 < ntiles:
            load(t + 2)

    load(0)
    if ntiles > 1:
        load(1)
    for t in range(ntiles):
        do_tile(t)
    A.release()


def build_mix_prog(T):
    key = ("mix", T)
    if key in _prog_cache:
        return _prog_cache[key]
    nc = bass.Bass("TRN2", target_bir_lowering=False)
    x1T = nc.dram_tensor("x1T", [8, 128, T], F32, kind="ExternalInput").ap()
    zTh = nc.dram_tensor("zTh", [2, 128, T + 2 * HALO], BF16, kind="ExternalInput").ap()
    upTh = nc.dram_tensor("upTh", [2, 128, T + 2 * HALO], BF16, kind="ExternalInput").ap()
    yaT = nc.dram_tensor("yaT", [4, 128, T], BF16, kind="ExternalInput").ap()
    w_out = nc.dram_tensor("w_out", [D, D], F32, kind="ExternalInput").ap()
    convw = nc.dram_tensor("convw", [2, 128, 31], F32, kind="ExternalInput").ap()
    cvec = nc.dram_tensor("cvec", [128, 8], F32, kind="ExternalInput").ap()
    pool_w = nc.dram_tensor("pool_w", [4, 64, 64], F32, kind="ExternalInput").ap()
    invcnt = nc.dram_tensor("invcnt", [2, 128, T], F32, kind="ExternalInput").ap()
    ident = nc.dram_tensor("ident", [128, 128], F32, kind="ExternalInput").ap()
    x2T = nc.dram_tensor("x2T", [8, 128, T], F32, kind="ExternalOutput").ap()
    C = make_ctx(nc)
    emit_mix(C, T, x1T, zTh, upTh, yaT, w_out, convw, cvec, pool_w, invcnt, ident, x2T)
    C.S.op("pool", lambda h: None, reads=["m.outs"])
    close_ctx(C)
    _prog_cache[key] = nc
    return nc


def pool_invcnt(pos0, T, Sq):
    t = np.arange(pos0, pos0 + T)
    out = np.zeros((2, 128, T), np.float32)
    for g, w in enumerate((2, 4, 8, 16)):
        lo = np.clip(t - w // 2, 0, Sq)
        hi = np.clip(t + w // 2, 0, Sq)
        out[g // 2, (g % 2) * 64:(g % 2) * 64 + 64, :] = (1.0 / (hi - lo).astype(np.float32))[None, :]
    return out


def _launch(nc, maps):
    res = run_bass_kernel_spmd(nc, maps, core_ids=list(range(len(maps))))
    return res.results


def kernel_unfused(x, ffn1_norm, ffn1_w_gate, ffn1_w_up, ffn1_w_down, mix_norm, w_in,
                   conv_dw, conv_dw_bias, conv_ln_gain, conv_ln_bias, pool_w, pool_scale,
                   q_norm, k_norm, lambda_q1, lambda_k1, lambda_q2, lambda_k2, attn_subln,
                   w_out, ffn2_norm, ffn2_w_gate, ffn2_w_up, ffn2_w_down, post_norm, _launch=_launch):
    import math
    f32 = np.float32
    x = np.asarray(x, f32)
    B, Sq, _ = x.shape
    T = Sq // 4
    ncore = 4 * B
    xT = [to_T(x[c // 4, (c % 4) * T:(c % 4 + 1) * T]) for c in range(ncore)]
    ident = np.eye(128, dtype=f32)
    kposc = (np.arange(Sq // 128, dtype=f32)[None] * 128 + np.arange(128, dtype=f32)[:, None]).copy()
    qpos = [np.ascontiguousarray(np.broadcast_to(np.arange((c % 4) * T, (c % 4 + 1) * T, dtype=f32)[None], (128, T))) for c in range(ncore)]
    invc = [pool_invcnt((c % 4) * T, T, Sq) for c in range(ncore)]
    A = lambda a: np.ascontiguousarray(np.asarray(a, f32))
    for l in range(2):
        lambda_init = 0.8 - 0.6 * math.exp(-0.3 * l)
        nc = build_ffn_prog(T, False)
        g1 = gain_pc(A(ffn1_norm[l]))
        r = _launch(nc, [{"xin": xT[c], "wg": A(ffn1_w_gate[l]), "wu": A(ffn1_w_up[l]), "wd": A(ffn1_w_down[l]), "gvec": g1}
                         for c in range(ncore)])
        x1T = [r[c]["xout"] for c in range(ncore)]
        nc = build_inproj_prog(T)
        qg = np.tile(A(q_norm[l]), 2)[:, None].copy()
        kg = np.tile(A(k_norm[l]), 2)[:, None].copy()
        r = _launch(nc, [{"xin": x1T[c], "w_in": A(w_in[l]), "gvec": gain_pc(A(mix_norm[l])), "qg": qg, "kg": kg}
                         for c in range(ncore)])
        kTf, Vf, zh, uh = [], [], [], []
        for b in range(B):
            cs = range(4 * b, 4 * b + 4)
            kTf.append(np.concatenate([r[c]["kT"] for c in cs], axis=2))
            Vf.append(np.concatenate([r[c]["Vd"] for c in cs], axis=2))
            zf = np.concatenate([r[c]["zT"] for c in cs], axis=2)
            uf = np.concatenate([r[c]["upT"] for c in cs], axis=2)
            zh.append(np.pad(zf, ((0, 0), (0, 0), (HALO, HALO))))
            uh.append(np.pad(uf, ((0, 0), (0, 0), (HALO, HALO))))
        qTl = [r[c]["qT"] for c in range(ncore)]
        nc = build_attn_prog(T, Sq)
        lamv = np.ascontiguousarray(np.broadcast_to(np.stack([A(lambda_q1[l]), A(lambda_k1[l]), A(lambda_q2[l]), A(lambda_k2[l])])[None],
                                                    (128, 4, 64)))
        lamc = np.ascontiguousarray(np.broadcast_to(np.array([-lambda_init, 1.0 - lambda_init], f32)[None], (128, 2)))
        subg = A(attn_subln[l])[:, None].copy()
        r = _launch(nc, [{"qT": qTl[c], "kTf": kTf[c // 4], "Vf": Vf[c // 4], "qpos": qpos[c], "kposc": kposc, "lamv": lamv,
                          "subg": subg, "lamc": lamc} for c in range(ncore)])
        yaT = [r[c]["yT"] for c in range(ncore)]
        nc = build_mix_prog(T)
        cb, lg, lb, psc = A(conv_dw_bias[l]), A(conv_ln_gain[l]), A(conv_ln_bias[l]), A(pool_scale[l])
        cvec = np.stack([cb[:128], cb[128:], lg[:128], lg[128:], lb[:128], lb[128:], psc[:128], psc[128:]], 1).astype(f32)
        convw = np.ascontiguousarray(A(conv_dw[l]).T.reshape(2, 128, 31))
        maps = []
        for c in range(ncore):
            o = (c % 4) * T
            maps.append({"x1T": x1T[c], "zTh": np.ascontiguousarray(zh[c // 4][:, :, o:o + T + 2 * HALO]),
                         "upTh": np.ascontiguousarray(uh[c // 4][:, :, o:o + T + 2 * HALO]), "yaT": yaT[c], "w_out": A(w_out[l]),
                         "convw": convw, "cvec": cvec, "pool_w": A(pool_w[l]), "invcnt": invc[c], "ident": ident})
        r = _launch(nc, maps)
        x2T = [r[c]["x2T"] for c in range(ncore)]
        nc = build_ffn_prog(T, True)
        r = _launch(nc, [{"xin": x2T[c], "wg": A(ffn2_w_gate[l]), "wu": A(ffn2_w_up[l]), "wd": A(ffn2_w_down[l]),
                          "gvec": gain_pc(A(ffn2_norm[l])), "pgvec": gain_pc(A(post_norm[l]))} for c in range(ncore)])
        xT = [r[c]["xout"] for c in range(ncore)]
    out = np.empty((B, Sq, D), f32)
    for c in range(ncore):
        out[c // 4, (c % 4) * T:(c % 4 + 1) * T] = from_T(xT[c])
    return out


LAYERS_PER_LAUNCH = 2
LAYER_W = ("wg1", "wu1", "wd1", "w_in", "w_out", "wg2", "wu2", "wd2")


def emit_exchange(C, T, kT, Vd, kTg, Vg, zTh, upTh, Ed, Eg, selL, selR, tag="x"):
    S, A, nc = C.S, C.A, C.nc
    R = lambda n: f"{tag}.{n}"
    groups = [[0, 1, 2, 3], [4, 5, 6, 7]]
    A.mark()
    H = HALO
    eg = A.alloc([4, 4, 2 * H], BF16)
    hl = A.alloc([4, H], BF16)
    hr = A.alloc([4, H], BF16)
    sl = A.alloc([4], F32)
    sr = A.alloc([4], F32)
    S.dma("pool", R("cst"), lambda h: h.dma_start(out=sl, in_=selL), writes=[R("sl")])
    S.dma("pool", R("cst"), lambda h: h.dma_start(out=sr, in_=selR), writes=[R("sr")])
    for j, src in enumerate((zTh, upTh)):
        for c in range(2):
            S.dma("pool", R("edge"), (lambda j, c, src: lambda h: h.dma_start(out=Ed[2 * j + c, :, 0:H], in_=src[c, :, H:2 * H]))(j, c, src),
                  writes=[R(f"Ed{j}{c}a")], newgroup=(j == 0 and c == 0))
            S.dma("pool", R("edge"), (lambda j, c, src: lambda h: h.dma_start(out=Ed[2 * j + c, :, H:2 * H], in_=src[c, :, T:T + H]))(j, c, src),
                  writes=[R(f"Ed{j}{c}b")], newgroup=False)
    for hd in range(4):
        S.dma("pool", R("cc"), (lambda hd: lambda h: h.collective_compute("AllGather", ALU.bypass, replica_groups=groups,
                                                                          ins=[kT[hd]], outs=[kTg[hd].rearrange("r p t -> (r p) t")]))(hd),
              writes=[R(f"kTg{hd}")], inc=1)
        S.dma("pool", R("cc"), (lambda hd: lambda h: h.collective_compute("AllGather", ALU.bypass, replica_groups=groups,
                                                                          ins=[Vd[hd].rearrange("p c e -> p (c e)")],
                                                                          outs=[Vg[hd].rearrange("r p c e -> (r p) (c e)")]))(hd),
              writes=[R(f"Vg{hd}")], inc=1)
    S.dma("pool", R("cc"), lambda h: h.collective_compute("AllGather", ALU.bypass, replica_groups=groups,
                                                          ins=[Ed.rearrange("j p e -> (j p) e")], outs=[Eg.rearrange("r j p e -> (r j p) e")]),
          reads=[R(f"Ed{j}{c}{x}") for j in range(2) for c in range(2) for x in "ab"], writes=[R("Eg")], inc=1)
    S.dma("pool", R("egl"), lambda h: h.dma_start(out=eg, in_=Eg.rearrange("r j p e -> p r j e")), reads=[R("Eg")], writes=[R("eg")])
    for r in range(4):
        if r == 0:
            S.op("dve", lambda h: h.tensor_scalar(out=hl, in0=eg[:, 0, :, H:2 * H], scalar1=sl[:, 0:1], scalar2=None, op0=ALU.mult),
                 reads=[R("eg"), R("sl")], writes=[R("hl")])
            S.op("dve", lambda h: h.tensor_scalar(out=hr, in0=eg[:, 0, :, 0:H], scalar1=sr[:, 0:1], scalar2=None, op0=ALU.mult),
                 reads=[R("eg"), R("sr")], writes=[R("hr")])
        else:
            S.op("dve", (lambda r: lambda h: h.scalar_tensor_tensor(out=hl, in0=eg[:, r, :, H:2 * H], scalar=sl[:, r:r + 1], in1=hl,
                                                                   op0=ALU.mult, op1=ALU.add))(r),
                 reads=[R("eg"), R("sl"), R("hl")], writes=[R("hl")])
            S.op("dve", (lambda r: lambda h: h.scalar_tensor_tensor(out=hr, in0=eg[:, r, :, 0:H], scalar=sr[:, r:r + 1], in1=hr,
                                                                   op0=ALU.mult, op1=ALU.add))(r),
                 reads=[R("eg"), R("sr"), R("hr")], writes=[R("hr")])
    for j, dst in enumerate((zTh, upTh)):
        S.dma("pool", R("hst"), (lambda j, dst: lambda h: h.dma_start(out=dst[:, :, 0:H].rearrange("c p e -> p c e"),
                                                                    in_=hl[:, 2 * j:2 * j + 2, :]))(j, dst),
              reads=[R("hl")], writes=[R(f"haloL{j}")], newgroup=(j == 0))
        S.dma("pool", R("hst"), (lambda j, dst: lambda h: h.dma_start(out=dst[:, :, T + H:T + 2 * H].rearrange("c p e -> p c e"),
                                                                    in_=hr[:, 2 * j:2 * j + 2, :]))(j, dst),
              reads=[R("hr")], writes=[R(f"haloR{j}")], newgroup=False)
    A.release()


def build_fused_prog(T, Sq, nlayers=2):
    key = ("fused", T, Sq, nlayers)
    if key in _prog_cache:
        return _prog_cache[key]
    nc = bass.Bass("TRN2", target_bir_lowering=False)
    EI = lambda name, shape, dt=F32: nc.dram_tensor(name, shape, dt, kind="ExternalInput").ap()
    IN = lambda name, shape, dt: nc.dram_tensor(name, shape, dt).ap()
    xin = EI("xin", [8, 128, T])
    wshape = {"wg1": [D, DFF], "wu1": [D, DFF], "wd1": [DFF, D], "w_in": [D, DIN], "w_out": [D, D],
              "wg2": [D, DFF], "wu2": [D, DFF], "wd2": [DFF, D]}
    Wt = [{k: EI(f"{k}_{l}", wshape[k]) for k in LAYER_W} for l in range(nlayers)]
    P = []
    for l in range(nlayers):
        P.append({"g1": EI(f"g1_{l}", [128, 8]), "gm": EI(f"gm_{l}", [128, 8]), "g2": EI(f"g2_{l}", [128, 8]),
                  "gp": EI(f"gp_{l}", [128, 8]), "qg": EI(f"qg_{l}", [128, 1]), "kg": EI(f"kg_{l}", [128, 1]),
                  "lamv": EI(f"lamv_{l}", [128, 4, 64]), "subg": EI(f"subg_{l}", [128, 1]), "lamc": EI(f"lamc_{l}", [128, 2]),
                  "convw": EI(f"convw_{l}", [2, 128, 31]), "cvec": EI(f"cvec_{l}", [128, 8]), "pool_w": EI(f"poolw_{l}", [4, 64, 64])})
    qpos = EI("qpos", [128, T])
    kposc = EI("kposc", [128, Sq // 128])
    invcnt = EI("invcnt", [2, 128, T])
    ident = EI("ident", [128, 128])
    selL = EI("selL", [128, 4])
    selR = EI("selR", [128, 4])
    xout = nc.dram_tensor("xout", [8, 128, T], F32, kind="ExternalOutput").ap()
    xa = IN("xa", [8, 128, T], F32)
    xb = IN("xb", [8, 128, T], F32)
    xc = IN("xc", [8, 128, T], F32)
    zTh = IN("zTh", [2, 128, T + 2 * HALO], BF16)
    upTh = IN("upTh", [2, 128, T + 2 * HALO], BF16)
    qT = IN("qT", [4, 128, T], BF16)
    kT = IN("kT", [4, 128, T], BF16)
    Vd = IN("Vd", [4, 128, T // 128, 128], BF16)
    kTg = IN("kTg", [4, 4, 128, T], BF16)
    Vg = IN("Vg", [4, 4, 128, T // 128, 128], BF16)
    Ed = IN("Ed", [4, 128, 2 * HALO], BF16)
    Eg = IN("Eg", [4, 4, 128, 2 * HALO], BF16)
    yaT = IN("yaT", [4, 128, T], BF16)
    C = make_ctx(nc)
    S = C.S
    cur = xin
    for l in range(nlayers):
        w, p = Wt[l], P[l]
        emit_ffn(C, T, cur, xa, w["wg1"], w["wu1"], w["wd1"], p["g1"], None, tag="f")
        S.barrier()
        emit_inproj(C, T, xa, w["w_in"], p["gm"], p["qg"], p["kg"], zTh, upTh, qT, kT, Vd, tag="i", zoff=HALO)
        S.barrier()
        emit_exchange(C, T, kT, Vd, kTg, Vg, zTh, upTh, Ed, Eg, selL, selR, tag="x")
        S.barrier()
        emit_attn(C, T, Sq, qT, kTg, Vg, qpos, kposc, p["lamv"], p["subg"], p["lamc"], yaT, tag="a", gathered=True)
        S.barrier()
        emit_mix(C, T, xa, zTh, upTh, yaT, w["w_out"], p["convw"], p["cvec"], p["pool_w"], invcnt, ident, xb, tag="m")
        S.barrier()
        dst = xc if l < nlayers - 1 else xout
        emit_ffn(C, T, xb, dst, w["wg2"], w["wu2"], w["wd2"], p["g2"], p["gp"], tag="f")
        S.barrier()
        cur = xc
    close_ctx(C)
    _prog_cache[key] = nc
    return nc


def fused_inputs(c, T, Sq, xT, weights, params):
    f32 = np.float32
    r = c % 4
    m = {"xin": xT, "qpos": np.ascontiguousarray(np.broadcast_to(np.arange(r * T, (r + 1) * T, dtype=f32)[None], (128, T))),
         "kposc": (np.arange(Sq // 128, dtype=f32)[None] * 128 + np.arange(128, dtype=f32)[:, None]).copy(),
         "invcnt": pool_invcnt(r * T, T, Sq), "ident": np.eye(128, dtype=f32)}
    sl = np.zeros((128, 4), f32)
    sr = np.zeros((128, 4), f32)
    if r > 0:
        sl[:, r - 1] = 1.0
    if r < 3:
        sr[:, r + 1] = 1.0
    m["selL"], m["selR"] = sl, sr
    m.update(weights)
    m.update(params)
    return m


def kernel_fused(inputs, launch):
    import math
    f32 = np.float32
    A = lambda a: np.ascontiguousarray(np.asarray(a, f32))
    x = A(inputs["x"])
    B, Sq, _ = x.shape
    T = Sq // 4
    ncore = 4 * B
    weights, params = {}, {}
    names = {"wg1": "ffn1_w_gate", "wu1": "ffn1_w_up", "wd1": "ffn1_w_down", "w_in": "w_in", "w_out": "w_out",
             "wg2": "ffn2_w_gate", "wu2": "ffn2_w_up", "wd2": "ffn2_w_down"}
    for l in range(2):
        lambda_init = 0.8 - 0.6 * math.exp(-0.3 * l)
        for k, src in names.items():
            weights[f"{k}_{l}"] = A(inputs[src][l])
        params[f"g1_{l}"] = gain_pc(A(inputs["ffn1_norm"][l]))
        params[f"gm_{l}"] = gain_pc(A(inputs["mix_norm"][l]))
        params[f"g2_{l}"] = gain_pc(A(inputs["ffn2_norm"][l]))
        params[f"gp_{l}"] = gain_pc(A(inputs["post_norm"][l]))
        params[f"qg_{l}"] = np.tile(A(inputs["q_norm"][l]), 2)[:, None].copy()
        params[f"kg_{l}"] = np.tile(A(inputs["k_norm"][l]), 2)[:, None].copy()
        params[f"lamv_{l}"] = np.ascontiguousarray(np.broadcast_to(
            np.stack([A(inputs["lambda_q1"][l]), A(inputs["lambda_k1"][l]), A(inputs["lambda_q2"][l]), A(inputs["lambda_k2"][l])])[None],
            (128, 4, 64)))
        params[f"subg_{l}"] = A(inputs["attn_subln"][l])[:, None].copy()
        params[f"lamc_{l}"] = np.ascontiguousarray(np.broadcast_to(np.array([-lambda_init, 1.0 - lambda_init], f32)[None], (128, 2)))
        params[f"convw_{l}"] = np.ascontiguousarray(A(inputs["conv_dw"][l]).T.reshape(2, 128, 31))
        cb, lg, lb, psc = (A(inputs[n][l]) for n in ("conv_dw_bias", "conv_ln_gain", "conv_ln_bias", "pool_scale"))
        params[f"cvec_{l}"] = np.stack([cb[:128], cb[128:], lg[:128], lg[128:], lb[:128], lb[128:], psc[:128], psc[128:]], 1).astype(f32)
        params[f"poolw_{l}"] = A(inputs["pool_w"][l])
    xT = [to_T(x[c // 4, (c % 4) * T:(c % 4 + 1) * T]) for c in range(ncore)]
    if LAYERS_PER_LAUNCH == 2:
        nc = build_fused_prog(T, Sq, 2)
        r = launch(nc, [fused_inputs(c, T, Sq, xT[c], weights, params) for c in range(ncore)])
        xT = [r[c]["xout"] for c in range(ncore)]
    else:
        nc = build_fused_prog(T, Sq, 1)
        for l in range(2):
            wl = {k[:-2] + "_0": v for k, v in weights.items() if k.endswith(f"_{l}")}
            pl = {k[:-2] + "_0": v for k, v in params.items() if k.endswith(f"_{l}")}
            r = launch(nc, [fused_inputs(c, T, Sq, xT[c], wl, pl) for c in range(ncore)])
            xT = [r[c]["xout"] for c in range(ncore)]
    out = np.empty((B, Sq, D), f32)
    for c in range(ncore):
        out[c // 4, (c % 4) * T:(c % 4 + 1) * T] = from_T(xT[c])
    return out


def kernel(**inputs):
    return kernel_fused(inputs, _launch)
```

```python
import numpy as np
import ml_dtypes
import concourse.bass as bass
import concourse.mybir as mybir
from concourse.bass_utils import run_bass_kernel_spmd

F32 = mybir.dt.float32
BF16 = mybir.dt.bfloat16
AF = mybir.ActivationFunctionType
ALU = mybir.AluOpType

D = 1024
DFF = 2816
NFC = DFF // 128
DIN = 2304
S_LEN = 16384
TPC = 4096
NCORES = 8
EPS = 1e-6

ENGS = ("pe", "act", "dve", "pool", "sp")
SAME_ENGINE_SYNC = True
SEM_ROT = 24000
XQ = "pool"


class _Op:
    __slots__ = ("eng", "fn", "reads", "writes", "chan", "group", "deps", "sig",
                 "idx", "pos", "xdeps", "inc")


class Sched:
    def __init__(self, nc):
        self.nc = nc
        self.ops = []
        self.chan_state = {}
        self.last_on = {}

    def op(self, eng, fn, reads=(), writes=(), xdeps=()):
        o = _Op()
        o.eng, o.fn, o.reads, o.writes = eng, fn, tuple(reads), tuple(writes)
        o.chan = None
        o.group = None
        o.xdeps = tuple(xdeps)
        o.idx = len(self.ops)
        self.ops.append(o)
        self.last_on[eng] = o.idx
        return o

    def dma(self, eng, chan, fn, reads=(), writes=(), newgroup=True, inc=16):
        o = self.op(eng, fn, reads, writes)
        o.chan = chan
        o.inc = inc
        st = self.chan_state.setdefault(chan, {"groups": []})
        if newgroup or not st["groups"]:
            st["groups"].append([])
        st["groups"][-1].append(o.idx)
        o.group = len(st["groups"]) - 1
        return o

    def barrier(self):
        lasts = [i for i in self.last_on.values()]
        for st in self.chan_state.values():
            if st["groups"]:
                lasts.append(st["groups"][-1][-1])
        for e in ENGS:
            self.op(e, None, xdeps=lasts)

    def finalize(self):
        nc = self.nc
        ops = self.ops
        last_w = {}
        readers = {}
        for o in ops:
            deps = set(o.xdeps)
            for r in o.reads:
                if r in last_w:
                    deps.add(last_w[r])
            for w in o.writes:
                if w in last_w:
                    deps.add(last_w[w])
                for rd in readers.get(w, ()):
                    deps.add(rd)
            deps.discard(o.idx)
            o.deps = deps
            for r in o.reads:
                readers.setdefault(r, []).append(o.idx)
            for w in o.writes:
                last_w[w] = o.idx
                readers[w] = []
        for chan, st in self.chan_state.items():
            cum = 0
            vals = []
            for g in st["groups"]:
                cum += sum(ops[i].inc for i in g)
                vals.append(cum)
            st["vals"] = vals
            assert cum < 65000, (chan, cum)
        pos = {e: 0 for e in ENGS}
        for o in ops:
            o.pos = pos[o.eng]
            pos[o.eng] += 1
        waited = {e: {} for e in ENGS}
        need = []
        sig_needed = set()
        for o in ops:
            w = {}
            for d in o.deps:
                p = ops[d]
                if p.chan is not None:
                    if o.chan == p.chan and o.group == p.group:
                        continue
                    s = ("c", p.chan)
                    key = p.group
                else:
                    if p.fn is None:
                        continue
                    s = ("e", p.eng)
                    key = p.pos
                    if p.eng == o.eng:
                        if p.eng == "pe" or not SAME_ENGINE_SYNC:
                            continue
                if s not in w or w[s][0] < key:
                    w[s] = (key, d)
            if o.chan is not None:
                st = self.chan_state[o.chan]
                if o.group > 0 and st["groups"][o.group][0] == o.idx:
                    s = ("c", o.chan)
                    key = o.group - 1
                    if s not in w or w[s][0] < key:
                        w[s] = (key, None)
            lst = []
            for s, (key, d) in w.items():
                if waited[o.eng].get(s, -1) >= key:
                    continue
                waited[o.eng][s] = key
                lst.append((s, key, d))
                if s[0] == "e":
                    sig_needed.add(d)
            need.append(lst)
        sigcount = {e: 0 for e in ENGS}
        for o in ops:
            if o.chan is None and o.idx in sig_needed:
                sigcount[o.eng] += 1
                o.sig = sigcount[o.eng]
            else:
                o.sig = None
        self._sem_ctx = []
        eng_sems = {}
        for e in ENGS:
            n = (sigcount[e] + SEM_ROT - 1) // SEM_ROT
            eng_sems[e] = [self._alloc_sem(f"s_{e}{i}") for i in range(max(n, 1))]
        chan_sems = {}
        for chan in self.chan_state:
            chan_sems[chan] = self._alloc_sem(f"c_{chan}")
        self.nsems = sum(len(v) for v in eng_sems.values()) + len(chan_sems)
        self.counts = dict(pos)

        def eng_wait_target(p):
            k = p.sig - 1
            return eng_sems[p.eng][k // SEM_ROT], (k % SEM_ROT) + 1

        streams = {e: [] for e in ENGS}
        for o in ops:
            streams[o.eng].append(o)

        semv = {}
        ptr = {e: 0 for e in ENGS}
        progress = True
        while progress:
            progress = False
            for e in ENGS:
                while ptr[e] < len(streams[e]):
                    o = streams[e][ptr[e]]
                    ok = True
                    for (s, key, d) in need[o.idx]:
                        if s[0] == "e":
                            sk, val = ("e", ops[d].eng), ops[d].sig
                        else:
                            sk, val = s, self.chan_state[s[1]]["vals"][key]
                        if semv.get(sk, 0) < val:
                            ok = False
                            break
                    if not ok:
                        break
                    if o.chan is not None:
                        semv[("c", o.chan)] = semv.get(("c", o.chan), 0) + o.inc
                    elif o.sig is not None:
                        semv[("e", o.eng)] = semv.get(("e", o.eng), 0) + 1
                        assert semv[("e", o.eng)] == o.sig
                    ptr[e] += 1
                    progress = True
        stuck = []
        for e in ENGS:
            if ptr[e] != len(streams[e]):
                o = streams[e][ptr[e]]
                unsat = []
                for (s_, key, d) in need[o.idx]:
                    if s_[0] == "e":
                        sk, val = ("e", ops[d].eng), ops[d].sig
                    else:
                        sk, val = s_, self.chan_state[s_[1]]["vals"][key]
                    if semv.get(sk, 0) < val:
                        unsat.append((sk, val, semv.get(sk, 0)))
                stuck.append((e, ptr[e], len(streams[e]), o.reads, o.writes, o.chan, unsat))
        assert not stuck, ("DEADLOCK", stuck)

        self.need = need
        self.streams = streams
        def run_stream(e, handle):
            for o in streams[e]:
                for (s, key, d) in need[o.idx]:
                    if s[0] == "e":
                        sem, val = eng_wait_target(ops[d])
                    else:
                        sem = chan_sems[s[1]]
                        val = self.chan_state[s[1]]["vals"][key]
                    handle.wait_ge(sem, val)
                ins = o.fn(handle) if o.fn is not None else None
                if ins is None:
                    assert o.chan is None and o.sig is None, "barrier op cannot signal"
                    continue
                if o.chan is not None:
                    ins.then_inc(chan_sems[o.chan], o.inc)
                elif o.sig is not None:
                    k = o.sig - 1
                    ins.then_inc(eng_sems[o.eng][k // SEM_ROT], 1)

        with nc.Block() as block:
            if streams["pe"]:
                @block.tensor
                def _(h):
                    run_stream("pe", h)
            if streams["act"]:
                @block.scalar
                def _(h):
                    run_stream("act", h)
            if streams["dve"]:
                @block.vector
                def _(h):
                    run_stream("dve", h)
            if streams["pool"]:
                @block.gpsimd
                def _(h):
                    run_stream("pool", h)
            if streams["sp"]:
                @block.sync
                def _(h):
                    run_stream("sp", h)
        for c in reversed(self._sem_ctx):
            c.__exit__(None, None, None)

    def _alloc_sem(self, name):
        c = self.nc.semaphore(name)
        s = c.__enter__()
        self._sem_ctx.append(c)
        return s


class Arena:
    def __init__(self, nc, nbytes, name="arena"):
        self.nc = nc
        self.n32 = nbytes // 4
        self.ctx = nc.sbuf_tensor(name, [128, self.n32], F32)
        self.t = self.ctx.__enter__()
        self.off = 0
        self.marks = []

    def alloc(self, shape, dt):
        esz = 2 if dt == BF16 else 4
        n = 1
        for s in shape:
            n *= s
        nb = (n * esz + 31) // 32 * 32
        a = self.off
        assert a + nb <= self.n32 * 4, ("SBUF arena overflow", a, nb, self.n32 * 4)
        self.off += nb
        ap = self.t[:, a // 4:(a + nb) // 4]
        if dt != F32:
            ap = ap.bitcast(dt)
        ap = ap[:, 0:n]
        if len(shape) == 2:
            ap = ap.rearrange("p (a b) -> p a b", a=shape[0])
        elif len(shape) == 3:
            ap = ap.rearrange("p (a b c) -> p a b c", a=shape[0], b=shape[1])
        return ap

    def mark(self):
        self.marks.append(self.off)

    def release(self):
        self.off = self.marks.pop()

    def close(self):
        self.ctx.__exit__(None, None, None)


class Ctx:
    pass


def make_ctx(nc):
    C = Ctx()
    C.nc = nc
    C.S = Sched(nc)
    C.A = Arena(nc, 207 * 1024)
    C.psctx = nc.psum_tensor("psum_all", [128, 4096], F32)
    C.ps = C.psctx.__enter__()
    C.uid = 0
    return C


def close_ctx(C):
    C.S.finalize()
    C.psctx.__exit__(None, None, None)
    C.A.close()


def bank(C, b, n=512, off=0):
    return C.ps[:, b * 512 + off:b * 512 + off + n]


def emit_ffn(C, T, xT_in, xT_out, wg, wu, wd, gvec, post_gvec=None, tag="f", NXB=2):
    S, A, nc = C.S, C.A, C.nc
    NT = 256
    ntiles = T // NT
    A.mark()
    Wg = A.alloc([8, DFF], BF16)
    Wu = A.alloc([8, DFF], BF16)
    Wd = A.alloc([NFC, D], BF16)
    onesf = A.alloc([128], F32)
    gv = A.alloc([8], F32)
    pgv = A.alloc([8], F32) if post_gvec is not None else None
    epsb = A.alloc([1], F32)
    xt = [A.alloc([8, NT], F32) for _ in range(NXB)]
    hT = [A.alloc([8, NT], BF16) for _ in range(2)]
    aT = A.alloc([NFC, NT], BF16)
    sq = [A.alloc([NT], F32) for _ in range(2)]
    sg = [A.alloc([NT], F32) for _ in range(2)]
    rstd = A.alloc([NT], F32)
    rstd2 = A.alloc([NT], F32)
    pg = [bank(C, 0, NT), bank(C, 1, NT)]
    pu = [bank(C, 2, NT), bank(C, 3, NT)]
    pd = [bank(C, 4, NT), bank(C, 5, NT)]
    pstat = bank(C, 6, NT)
    pstat2 = bank(C, 7, NT)
    R = lambda n: f"{tag}.{n}"

    S.op("pool", lambda h: h.memset(onesf, 1.0 / D), writes=[R("onesf")])
    S.op("pool", lambda h: h.memset(epsb, EPS), writes=[R("epsb")])
    S.dma("sp", R("cst"), lambda h: h.dma_start(out=gv, in_=gvec), writes=[R("gv")])
    if post_gvec is not None:
        S.dma("sp", R("cst"), lambda h: h.dma_start(out=pgv, in_=post_gvec), writes=[R("pgv")])
    xin_v = xT_in.rearrange("c p t -> p c t")
    xout_v = xT_out.rearrange("c p t -> p c t")

    def load(t):
        b = t % NXB
        S.dma(XQ, R(f"xin{b}"), lambda h: h.dma_start(out=xt[b], in_=xin_v[:, :, t * NT:(t + 1) * NT]),
              writes=[R(f"xt{b}")])

    for _t in range(min(NXB, ntiles)):
        load(_t)
    for dc in range(8):
        S.dma("pool", R("wld"), (lambda dc: lambda h: h.dma_start(out=Wg[:, dc, :], in_=wg[dc * 128:(dc + 1) * 128, :]))(dc),
              writes=[R(f"Wg{dc}")])
        S.dma("pool", R("wld"), (lambda dc: lambda h: h.dma_start(out=Wu[:, dc, :], in_=wu[dc * 128:(dc + 1) * 128, :]))(dc),
              writes=[R(f"Wu{dc}")])
    for fc in range(NFC):
        S.dma("pool", R("wld"), (lambda fc: lambda h: h.dma_start(out=Wd[:, fc, :], in_=wd[fc * 128:(fc + 1) * 128, :]))(fc),
              writes=[R(f"Wd{fc}")])

    def stats(xbuf, xres, pst, rs, rsres):
        for c in range(8):
            k = c % 2
            S.op("act", (lambda c, k: lambda h: h.activation(out=sq[k], in_=xbuf[:, c, :], func=AF.Square))(c, k),
                 reads=[xres], writes=[R(f"sq{k}")])
            S.op("pe", (lambda c, k: lambda h: h.matmul(pst, lhsT=onesf, rhs=sq[k], start=(c == 0), stop=(c == 7)))(c, k),
                 reads=[R(f"sq{k}"), R("onesf")], writes=[rsres + ".ps"])
        S.op("act", lambda h: h.activation(out=rs, in_=pst, func=AF.Sqrt, bias=epsb[:, 0:1], scale=1.0),
             reads=[rsres + ".ps", R("epsb")], writes=[rsres])
        S.op("dve", lambda h: h.reciprocal(out=rs, in_=rs), reads=[rsres], writes=[rsres])

    def make_h(t):
        b = t % 2
        xb = t % NXB
        stats(xt[xb], R(f"xt{xb}"), pstat, rstd, R("rstd"))
        for c in range(8):
            S.op("dve", (lambda c: lambda h: h.scalar_tensor_tensor(out=hT[b][:, c, :], in0=xt[xb][:, c, :], scalar=gv[:, c:c + 1],
                                                                   in1=rstd, op0=ALU.mult, op1=ALU.mult))(c),
                 reads=[R(f"xt{xb}"), R("rstd"), R("gv")], writes=[R(f"hT{b}")])

    def gateup(t):
        b = t % 2
        for fc in range(NFC):
            k = fc % 2
            for dc in range(8):
                S.op("pe", (lambda fc, dc, k: lambda h: h.matmul(pg[k], lhsT=Wg[:, dc, fc * 128:(fc + 1) * 128], rhs=hT[b][:, dc, :],
                                                                 start=(dc == 0), stop=(dc == 7)))(fc, dc, k),
                     reads=[R(f"Wg{dc}"), R(f"hT{b}")], writes=[R(f"pg{k}")])
            for dc in range(8):
                S.op("pe", (lambda fc, dc, k: lambda h: h.matmul(pu[k], lhsT=Wu[:, dc, fc * 128:(fc + 1) * 128], rhs=hT[b][:, dc, :],
                                                                 start=(dc == 0), stop=(dc == 7)))(fc, dc, k),
                     reads=[R(f"Wu{dc}"), R(f"hT{b}")], writes=[R(f"pu{k}")])
            S.op("act", (lambda k: lambda h: h.activation(out=sg[k], in_=pg[k], func=AF.Silu))(k),
                 reads=[R(f"pg{k}")], writes=[R(f"sg{k}")])
            S.op("dve", (lambda fc, k: lambda h: h.tensor_tensor(out=aT[:, fc, :], in0=sg[k], in1=pu[k], op=ALU.mult))(fc, k),
                 reads=[R(f"sg{k}"), R(f"pu{k}")], writes=[R(f"aT{fc}")])

    def down(t):
        b = t % NXB
        for oc in range(8):
            k = oc % 2
            for fc in range(NFC):
                S.op("pe", (lambda oc, fc, k: lambda h: h.matmul(pd[k], lhsT=Wd[:, fc, oc * 128:(oc + 1) * 128], rhs=aT[:, fc, :],
                                                                 start=(fc == 0), stop=(fc == NFC - 1)))(oc, fc, k),
                     reads=[R(f"Wd{fc}"), R(f"aT{fc}")], writes=[R(f"pd{k}")])
            S.op("dve", (lambda oc, k: lambda h: h.scalar_tensor_tensor(out=xt[b][:, oc, :], in0=pd[k], scalar=0.5, in1=xt[b][:, oc, :],
                                                                       op0=ALU.mult, op1=ALU.add))(oc, k),
                 reads=[R(f"pd{k}"), R(f"xt{b}")], writes=[R(f"xt{b}")])
        if post_gvec is not None:
            stats(xt[b], R(f"xt{b}"), pstat2, rstd2, R("rstd2"))
            for c in range(8):
                S.op("dve", (lambda c: lambda h: h.scalar_tensor_tensor(out=xt[b][:, c, :], in0=xt[b][:, c, :], scalar=pgv[:, c:c + 1],
                                                                       in1=rstd2, op0=ALU.mult, op1=ALU.mult))(c),
                     reads=[R(f"xt{b}"), R("rstd2"), R("pgv")], writes=[R(f"xt{b}")])
        S.dma(XQ, R(f"xout{b}"), lambda h: h.dma_start(out=xout_v[:, :, t * NT:(t + 1) * NT], in_=xt[b]),
              reads=[R(f"xt{b}")], writes=[R("xout")])

    make_h(0)
    for t in range(ntiles):
        gateup(t)
        if t + 1 < ntiles:
            make_h(t + 1)
        down(t)
        if t + NXB < ntiles:
            load(t + NXB)
    A.release()


_prog_cache = {}


def build_ffn_prog(T, post):
    key = ("ffn", T, post)
    if key in _prog_cache:
        return _prog_cache[key]
    nc = bass.Bass("TRN2", target_bir_lowering=False)
    xin = nc.dram_tensor("xin", [8, 128, T], F32, kind="ExternalInput").ap()
    wg = nc.dram_tensor("wg", [D, DFF], F32, kind="ExternalInput").ap()
    wu = nc.dram_tensor("wu", [D, DFF], F32, kind="ExternalInput").ap()
    wd = nc.dram_tensor("wd", [DFF, D], F32, kind="ExternalInput").ap()
    gvec = nc.dram_tensor("gvec", [128, 8], F32, kind="ExternalInput").ap()
    pg = nc.dram_tensor("pgvec", [128, 8], F32, kind="ExternalInput").ap() if post else None
    xout = nc.dram_tensor("xout", [8, 128, T], F32, kind="ExternalOutput").ap()
    C = make_ctx(nc)
    emit_ffn(C, T, xin, xout, wg, wu, wd, gvec, pg)
    C.S.op("sp", lambda h: None, reads=["f.xout"])
    close_ctx(C)
    _prog_cache[key] = nc
    return nc


def to_T(x2d):
    T = x2d.shape[0]
    return np.ascontiguousarray(x2d.T.reshape(8, 128, T))


def from_T(xT):
    T = xT.shape[2]
    return np.ascontiguousarray(xT.reshape(1024, T).T)


def gain_pc(g):
    return np.ascontiguousarray(g.reshape(8, 128).T).astype(np.float32)


def run_ffn(xT_list, wg, wu, wd, g, post_g=None):
    T = xT_list[0].shape[2]
    nc = build_ffn_prog(T, post_g is not None)
    maps = []
    for xT in xT_list:
        m = {"xin": xT, "wg": wg, "wu": wu, "wd": wd, "gvec": gain_pc(g)}
        if post_g is not None:
            m["pgvec"] = gain_pc(post_g)
        maps.append(m)
    res = run_bass_kernel_spmd(nc, maps, core_ids=list(range(len(maps))))
    return [r["xout"] for r in res.results]


def emit_inproj(C, T, xT_in, w_in, gvec, qg, kg, zT, upT, qT, kT, Vd, tag="i", zoff=0):
    S, A, nc = C.S, C.A, C.nc
    NT = 256
    ntiles = T // NT
    A.mark()
    W = A.alloc([8, DIN], BF16)
    onesf = A.alloc([128], F32)
    blk = A.alloc([128], F32)
    gv = A.alloc([8], F32)
    qgv = A.alloc([1], F32)
    kgv = A.alloc([1], F32)
    epsb = A.alloc([1], F32)
    eps64 = A.alloc([1], F32)
    xt = [A.alloc([8, NT], F32) for _ in range(2)]
    hT = A.alloc([8, NT], BF16)
    sq = [A.alloc([NT], F32) for _ in range(2)]
    rstd = A.alloc([NT], F32)
    sgm = [A.alloc([NT], F32) for _ in range(2)]
    rq = [A.alloc([NT], F32) for _ in range(2)]
    ob = [A.alloc([NT], BF16) for _ in range(4)]
    vb = [A.alloc([512], BF16) for _ in range(2)]
    pu = [bank(C, 0, NT), bank(C, 1, NT), bank(C, 2, NT), bank(C, 3, NT)]
    pv = [bank(C, 4, 512), bank(C, 5, 512)]
    pstat = bank(C, 6, NT)
    pqs = bank(C, 7, NT)
    R = lambda n: f"{tag}.{n}"

    S.op("pool", lambda h: h.memset(onesf, 1.0 / D), writes=[R("onesf")])
    S.op("pool", lambda h: h.memset(blk, 0.0), writes=[R("blk")])
    S.op("pool", lambda h: h.memset(blk[0:64, 0:64], 1.0 / 64), writes=[R("blk")])
    S.op("pool", lambda h: h.memset(blk[64:128, 64:128], 1.0 / 64), writes=[R("blk")])
    S.op("pool", lambda h: h.memset(epsb, EPS), writes=[R("epsb")])
    S.op("pool", lambda h: h.memset(eps64, 64.0 * EPS), writes=[R("eps64")])
    S.dma("pool", R("cst"), lambda h: h.dma_start(out=gv, in_=gvec), writes=[R("gv")])
    S.dma("pool", R("cst"), lambda h: h.dma_start(out=qgv, in_=qg), writes=[R("qgv")])
    S.dma("pool", R("cst"), lambda h: h.dma_start(out=kgv, in_=kg), writes=[R("kgv")])
    xin_v = xT_in.rearrange("c p t -> p c t")

    def load(t):
        b = t % 2
        S.dma(XQ, R(f"xin{b}"), lambda h: h.dma_start(out=xt[b], in_=xin_v[:, :, t * NT:(t + 1) * NT]),
              writes=[R(f"xt{b}")])

    load(0)
    if ntiles > 1:
        load(1)
    for dc in range(8):
        S.dma("pool", R("wld"), (lambda dc: lambda h: h.dma_start(out=W[:, dc, :], in_=w_in[dc * 128:(dc + 1) * 128, :]))(dc),
              writes=[R(f"W{dc}")])

    ocount = [0]

    def out_store(dst_ap, src, srcres):
        k = ocount[0] % 4
        ocount[0] += 1
        return k

    def do_tile(t):
        b = t % 2
        xb, xres = xt[b], R(f"xt{b}")
        tsl = slice(t * NT, (t + 1) * NT)
        for c in range(8):
            k = c % 2
            S.op("act", (lambda c, k: lambda h: h.activation(out=sq[k], in_=xb[:, c, :], func=AF.Square))(c, k),
                 reads=[xres], writes=[R(f"sq{k}")])
            S.op("pe", (lambda c, k: lambda h: h.matmul(pstat, lhsT=onesf, rhs=sq[k], start=(c == 0), stop=(c == 7)))(c, k),
                 reads=[R(f"sq{k}"), R("onesf")], writes=[R("pstat")])
        S.op("act", lambda h: h.activation(out=rstd, in_=pstat, func=AF.Sqrt, bias=epsb[:, 0:1], scale=1.0),
             reads=[R("pstat"), R("epsb")], writes=[R("rstd")])
        S.op("dve", lambda h: h.reciprocal(out=rstd, in_=rstd), reads=[R("rstd")], writes=[R("rstd")])
        for c in range(8):
            S.op("dve", (lambda c: lambda h: h.scalar_tensor_tensor(out=hT[:, c, :], in0=xb[:, c, :], scalar=gv[:, c:c + 1],
                                                                   in1=rstd, op0=ALU.mult, op1=ALU.mult))(c),
                 reads=[xres, R("rstd"), R("gv")], writes=[R("hT")])
        if t + 2 < ntiles:
            pass

        def proj(j, pk):
            for dc in range(8):
                S.op("pe", (lambda dc: lambda h: h.matmul(pu[pk], lhsT=W[:, dc, j * 128:(j + 1) * 128], rhs=hT[:, dc, :],
                                                          start=(dc == 0), stop=(dc == 7)))(dc),
                     reads=[R(f"W{dc}"), R("hT")], writes=[R(f"pu{pk}")])

        def store(dst, k):
            S.dma(XQ, R(f"ost{k}"), lambda h: h.dma_start(out=dst, in_=ob[k]), reads=[R(f"ob{k}")], writes=[R("outs")])

        for j in range(2):
            proj(2 + j, 0)
            proj(j, 1)
            k = ocount[0] % 4
            ocount[0] += 1
            S.op("act", (lambda j: lambda h: h.activation(out=sgm[j], in_=pu[0], func=AF.Sigmoid))(j),
                 reads=[R("pu0")], writes=[R(f"sgm{j}")])
            S.op("dve", (lambda j, k: lambda h: h.tensor_tensor(out=ob[k], in0=sgm[j], in1=pu[1], op=ALU.mult))(j, k),
                 reads=[R(f"sgm{j}"), R("pu1")], writes=[R(f"ob{k}")])
            store(zT[j, :, t * NT + zoff:(t + 1) * NT + zoff], k)
        for j in range(2):
            pk = 2 + j
            proj(4 + j, pk)
            k = ocount[0] % 4
            ocount[0] += 1
            S.op("dve", (lambda pk, k: lambda h: h.tensor_copy(out=ob[k], in_=pu[pk]))(pk, k),
                 reads=[R(f"pu{pk}")], writes=[R(f"ob{k}")])
            store(upT[j, :, t * NT + zoff:(t + 1) * NT + zoff], k)
        for j in range(8):
            pk = j % 4
            r = j % 2
            isq = j < 4
            proj(6 + j, pk)
            k = ocount[0] % 4
            ocount[0] += 1
            S.op("act", (lambda pk, r: lambda h: h.activation(out=sq[r], in_=pu[pk], func=AF.Square))(pk, r),
                 reads=[R(f"pu{pk}")], writes=[R(f"sq{r}")])
            S.op("pe", (lambda r: lambda h: h.matmul(pqs, lhsT=blk, rhs=sq[r], start=True, stop=True))(r),
                 reads=[R(f"sq{r}"), R("blk")], writes=[R("pqs")])
            if isq:
                S.op("act", (lambda r: lambda h: h.activation(out=rq[r], in_=pqs, func=AF.Sqrt, bias=eps64[:, 0:1], scale=64.0))(r),
                     reads=[R("pqs"), R("eps64")], writes=[R(f"rq{r}")])
            else:
                S.op("act", (lambda r: lambda h: h.activation(out=rq[r], in_=pqs, func=AF.Sqrt, bias=epsb[:, 0:1], scale=1.0))(r),
                     reads=[R("pqs"), R("epsb")], writes=[R(f"rq{r}")])
            S.op("dve", (lambda r: lambda h: h.reciprocal(out=rq[r], in_=rq[r]))(r), reads=[R(f"rq{r}")], writes=[R(f"rq{r}")])
            gvv = qgv if isq else kgv
            S.op("dve", (lambda pk, r, k, gvv: lambda h: h.scalar_tensor_tensor(out=ob[k], in0=pu[pk], scalar=gvv[:, 0:1], in1=rq[r],
                                                                               op0=ALU.mult, op1=ALU.mult))(pk, r, k, gvv),
                 reads=[R(f"pu{pk}"), R(f"rq{r}"), R("qgv"), R("kgv")], writes=[R(f"ob{k}")])
            dst = (qT if isq else kT)[j % 4, :, tsl]
            store(dst, k)
        def vpart(s):
            k = s % 2
            for dc in range(8):
                S.op("pe", (lambda dc: lambda h: h.matmul(pv[k], lhsT=hT[:, dc, s * 128:(s + 1) * 128], rhs=W[:, dc, 1792:2304],
                                                          start=(dc == 0), stop=(dc == 7)))(dc),
                     reads=[R(f"W{dc}"), R("hT")], writes=[R(f"pv{k}")])
            S.op("act", lambda h: h.activation(out=vb[k], in_=pv[k], func=AF.Copy), reads=[R(f"pv{k}")], writes=[R(f"vb{k}")])
            cidx = t * (NT // 128) + s
            S.dma(XQ, R(f"vst{k}"), lambda h: h.dma_start(out=Vd[:, :, cidx, :].rearrange("h p e -> p h e"),
                                                          in_=vb[k].rearrange("p (h e) -> p h e", h=4)),
                  reads=[R(f"vb{k}")], writes=[R("outs")])

        for s in range(NT // 128):
            vpart(s)
        if t + 2 < ntiles:
            load(t + 2)

    for t in range(ntiles):
        do_tile(t)
    A.release()


def build_inproj_prog(T):
    key = ("inproj", T)
    if key in _prog_cache:
        return _prog_cache[key]
    nc = bass.Bass("TRN2", target_bir_lowering=False)
    xin = nc.dram_tensor("xin", [8, 128, T], F32, kind="ExternalInput").ap()
    w_in = nc.dram_tensor("w_in", [D, DIN], F32, kind="ExternalInput").ap()
    gvec = nc.dram_tensor("gvec", [128, 8], F32, kind="ExternalInput").ap()
    qg = nc.dram_tensor("qg", [128, 1], F32, kind="ExternalInput").ap()
    kg = nc.dram_tensor("kg", [128, 1], F32, kind="ExternalInput").ap()
    zT = nc.dram_tensor("zT", [2, 128, T], BF16, kind="ExternalOutput").ap()
    upT = nc.dram_tensor("upT", [2, 128, T], BF16, kind="ExternalOutput").ap()
    qT = nc.dram_tensor("qT", [4, 128, T], BF16, kind="ExternalOutput").ap()
    kT = nc.dram_tensor("kT", [4, 128, T], BF16, kind="ExternalOutput").ap()
    Vd = nc.dram_tensor("Vd", [4, 128, T // 128, 128], BF16, kind="ExternalOutput").ap()
    C = make_ctx(nc)
    emit_inproj(C, T, xin, w_in, gvec, qg, kg, zT, upT, qT, kT, Vd)
    C.S.op("pool", lambda h: None, reads=["i.outs"])
    close_ctx(C)
    _prog_cache[key] = nc
    return nc


def emit_attn(C, T, Sq, qT, kTf, Vf, ramp, kposc, lamv, subg, lamc, yT, tag="a", gathered=False):
    S, A, nc = C.S, C.A, C.nc
    NQ = 512
    nqt = T // NQ
    nkc = Sq // 128
    A.mark()
    Kt = A.alloc([Sq], BF16)
    Vt = A.alloc([nkc, 128], BF16)
    kpc = A.alloc([nkc], F32)
    negk = A.alloc([nkc], F32)
    onesb = A.alloc([128], BF16)
    ones128 = A.alloc([128], F32)
    epsb = A.alloc([1], F32)
    lamt = A.alloc([4, 64], F32)
    lprod = A.alloc([2, 64], F32)
    lsum = A.alloc([2], F32)
    neglam = A.alloc([1], F32)
    sgv = A.alloc([1], F32)
    lct = A.alloc([2], F32)
    Qt = [A.alloc([NQ], BF16) for _ in range(2)]
    NF = Sq + T
    FP = 512
    Ft = A.alloc([NF], F32)
    rt = [A.alloc([FP], F32) for _ in range(2)]
    tmp = [A.alloc([2 * NQ], F32) for _ in range(2)]
    Pt = [A.alloc([2 * NQ], BF16) for _ in range(2)]
    r1 = A.alloc([NQ], F32)
    o1 = A.alloc([NQ], F32)
    o2 = A.alloc([NQ], F32)
    sqo = A.alloc([NQ], F32)
    rs = A.alloc([NQ], F32)
    yb = [A.alloc([NQ], BF16) for _ in range(2)]
    R = lambda n: f"{tag}.{n}"
    psS = [C.ps[:, 0:1024], C.ps[:, 1024:2048]]
    acc = [bank(C, 4), bank(C, 5)]
    den = [bank(C, 6), bank(C, 7)]

    S.op("pool", lambda h: h.memset(onesb, 1.0), writes=[R("onesb")])
    S.op("pool", lambda h: h.memset(ones128, 1.0 / 128), writes=[R("ones128")])
    S.op("pool", lambda h: h.memset(epsb, EPS), writes=[R("epsb")])
    S.dma("pool", R("cst"), lambda h: h.dma_start(out=kpc, in_=kposc), writes=[R("kpc")])
    S.dma("pool", R("cst"), lambda h: h.dma_start(out=lamt, in_=lamv), writes=[R("lamt")])
    S.dma("pool", R("cst"), lambda h: h.dma_start(out=sgv, in_=subg), writes=[R("sgv")])
    S.dma("pool", R("cst"), lambda h: h.dma_start(out=lct, in_=lamc), writes=[R("lct")])
    S.op("dve", lambda h: h.tensor_tensor(out=lprod[:, 0, :], in0=lamt[:, 0, :], in1=lamt[:, 1, :], op=ALU.mult),
         reads=[R("lamt")], writes=[R("lprod")])
    S.op("dve", lambda h: h.tensor_tensor(out=lprod[:, 1, :], in0=lamt[:, 2, :], in1=lamt[:, 3, :], op=ALU.mult),
         reads=[R("lamt")], writes=[R("lprod")])
    S.op("dve", lambda h: h.reduce_sum(out=lsum, in_=lprod, axis=mybir.AxisListType.X), reads=[R("lprod")], writes=[R("lsum")])
    S.op("act", lambda h: h.activation(out=lsum, in_=lsum, func=AF.Exp), reads=[R("lsum")], writes=[R("lsum")])
    S.op("dve", lambda h: h.tensor_tensor(out=neglam, in0=lsum[:, 1:2], in1=lsum[:, 0:1], op=ALU.subtract),
         reads=[R("lsum")], writes=[R("neglam")])
    S.op("dve", lambda h: h.tensor_scalar(out=neglam, in0=neglam, scalar1=lct[:, 0:1], scalar2=None, op0=ALU.add),
         reads=[R("neglam"), R("lct")], writes=[R("neglam")])
    S.op("dve", lambda h: h.tensor_scalar(out=sgv, in0=sgv, scalar1=lct[:, 1:2], scalar2=None, op0=ALU.mult),
         reads=[R("sgv"), R("lct")], writes=[R("sgv")])

    cnt = [0]

    def head(hd):
        slope = 2.0 ** (-8.0 * (hd + 1) / 4)
        if gathered:
            S.dma("pool", R("kld"), lambda h: h.dma_start(out=Kt.rearrange("p (r t) -> p r t", r=4),
                                                          in_=kTf[hd].rearrange("r p t -> p r t")), writes=[R("Kt")])
            S.dma("pool", R("vld"), lambda h: h.dma_start(out=Vt.rearrange("p (r c) e -> p r (c e)", r=4),
                                                          in_=Vf[hd].rearrange("r p c e -> p r (c e)")), writes=[R("Vt")])
        else:
            S.dma("pool", R("kld"), lambda h: h.dma_start(out=Kt, in_=kTf[hd]), writes=[R("Kt")])
            S.dma("pool", R("vld"), lambda h: h.dma_start(out=Vt, in_=Vf[hd]), writes=[R("Vt")])
        S.op("dve", lambda h: h.tensor_scalar(out=negk, in0=kpc, scalar1=-slope, scalar2=None, op0=ALU.mult),
             reads=[R("kpc")], writes=[R("negk")])

        def fpiece(pi):
            b = pi % 2
            sl_ = slice(pi * FP, (pi + 1) * FP)
            S.dma("pool", R(f"rld{b}"), lambda h: h.dma_start(out=rt[b], in_=ramp[:, sl_]), writes=[R(f"rt{b}")])
            S.op("act", lambda h: h.activation(out=Ft[:, sl_], in_=rt[b], func=AF.Abs, bias=negk[:, 0:1], scale=slope),
                 reads=[R(f"rt{b}"), R("negk")], writes=[R("Ft")])

        for pi in range(NF // FP):
            fpiece(pi)

        def qtile(qt):
            qb = (hd * nqt + qt) % 2
            qs = slice(qt * NQ, (qt + 1) * NQ)
            S.dma("pool", R(f"qld{qb}"), lambda h: h.dma_start(out=Qt[qb], in_=qT[hd, :, qs]), writes=[R(f"Qt{qb}")])

            def score(kc):
                i = kc % 2
                ks = slice(kc * 128, (kc + 1) * 128)
                S.op("pe", lambda h: h.matmul(psS[i][:, 0:NQ], lhsT=Kt[0:64, ks], rhs=Qt[qb][0:64, :], start=True, stop=True),
                     reads=[R("Kt"), R(f"Qt{qb}")], writes=[R(f"psS{i}a")])
                S.op("pe", lambda h: h.matmul(psS[i][:, NQ:2 * NQ], lhsT=Kt[64:128, ks], rhs=Qt[qb][64:128, :], start=True, stop=True),
                     reads=[R("Kt"), R(f"Qt{qb}")], writes=[R(f"psS{i}b")])
                fo = Sq + qt * NQ - kc * 128
                S.op("dve", lambda h: h.tensor_tensor(out=tmp[i][:, 0:NQ], in0=psS[i][:, 0:NQ], in1=Ft[:, fo:fo + NQ], op=ALU.subtract),
                     reads=[R("Ft"), R(f"psS{i}a")], writes=[R(f"tmp{i}")])
                S.op("dve", lambda h: h.tensor_tensor(out=tmp[i][:, NQ:2 * NQ], in0=psS[i][:, NQ:2 * NQ], in1=Ft[:, fo:fo + NQ], op=ALU.subtract),
                     reads=[R("Ft"), R(f"psS{i}b")], writes=[R(f"tmp{i}")])
                S.op("act", lambda h: h.activation(out=Pt[i], in_=tmp[i], func=AF.Exp), reads=[R(f"tmp{i}")], writes=[R(f"Pt{i}")])

            def accum(kc):
                i = kc % 2
                first, last = (kc == 0), (kc == nkc - 1)
                for m in range(2):
                    S.op("pe", (lambda m: lambda h: h.matmul(acc[m], lhsT=Vt[:, kc, :], rhs=Pt[i][:, m * NQ:(m + 1) * NQ],
                                                             start=first, stop=last))(m),
                         reads=[R("Vt"), R(f"Pt{i}")], writes=[R(f"acc{m}")])
                    S.op("pe", (lambda m: lambda h: h.matmul(den[m], lhsT=onesb, rhs=Pt[i][:, m * NQ:(m + 1) * NQ],
                                                             start=first, stop=last))(m),
                         reads=[R("onesb"), R(f"Pt{i}")], writes=[R(f"den{m}")])

            score(0)
            for kc in range(nkc):
                if kc + 1 < nkc:
                    score(kc + 1)
                accum(kc)
            S.op("dve", lambda h: h.reciprocal(out=r1, in_=den[0]), reads=[R("den0")], writes=[R("r1")])
            S.op("dve", lambda h: h.tensor_tensor(out=o1, in0=acc[0], in1=r1, op=ALU.mult), reads=[R("acc0"), R("r1")], writes=[R("o1")])
            S.op("dve", lambda h: h.reciprocal(out=r1, in_=den[1]), reads=[R("den1"), R("o1")], writes=[R("r1")])
            S.op("dve", lambda h: h.tensor_tensor(out=o2, in0=acc[1], in1=r1, op=ALU.mult), reads=[R("acc1"), R("r1")], writes=[R("o2")])
            S.op("dve", lambda h: h.scalar_tensor_tensor(out=o1, in0=o2, scalar=neglam[:, 0:1], in1=o1, op0=ALU.mult, op1=ALU.add),
                 reads=[R("o2"), R("o1"), R("neglam")], writes=[R("o1")])
            S.op("act", lambda h: h.activation(out=sqo, in_=o1, func=AF.Square), reads=[R("o1")], writes=[R("sqo")])
            S.op("pe", lambda h: h.matmul(den[0], lhsT=ones128, rhs=sqo, start=True, stop=True),
                 reads=[R("ones128"), R("sqo")], writes=[R("den0")])
            S.op("act", lambda h: h.activation(out=rs, in_=den[0], func=AF.Sqrt, bias=epsb[:, 0:1], scale=1.0),
                 reads=[R("den0"), R("epsb")], writes=[R("rs")])
            S.op("dve", lambda h: h.reciprocal(out=rs, in_=rs), reads=[R("rs")], writes=[R("rs")])
            S.op("dve", lambda h: h.scalar_tensor_tensor(out=yb[qb], in0=o1, scalar=sgv[:, 0:1], in1=rs, op0=ALU.mult, op1=ALU.mult),
                 reads=[R("o1"), R("rs"), R("sgv")], writes=[R(f"yb{qb}")])
            S.dma("pool", R(f"yst{qb}"), lambda h: h.dma_start(out=yT[hd, :, qs], in_=yb[qb]), reads=[R(f"yb{qb}")], writes=[R("outs")])

        for qt in range(nqt):
            qtile(qt)

    for hd in range(4):
        head(hd)
    A.release()


def build_attn_prog(T, Sq):
    key = ("attn", T, Sq)
    if key in _prog_cache:
        return _prog_cache[key]
    nc = bass.Bass("TRN2", target_bir_lowering=False)
    qT = nc.dram_tensor("qT", [4, 128, T], BF16, kind="ExternalInput").ap()
    kTf = nc.dram_tensor("kTf", [4, 128, Sq], BF16, kind="ExternalInput").ap()
    Vf = nc.dram_tensor("Vf", [4, 128, Sq // 128, 128], BF16, kind="ExternalInput").ap()
    qpos = nc.dram_tensor("ramp", [128, Sq + T], F32, kind="ExternalInput").ap()
    kposc = nc.dram_tensor("kposc", [128, Sq // 128], F32, kind="ExternalInput").ap()
    lamv = nc.dram_tensor("lamv", [128, 4, 64], F32, kind="ExternalInput").ap()
    subg = nc.dram_tensor("subg", [128, 1], F32, kind="ExternalInput").ap()
    lamc = nc.dram_tensor("lamc", [128, 2], F32, kind="ExternalInput").ap()
    yT = nc.dram_tensor("yT", [4, 128, T], BF16, kind="ExternalOutput").ap()
    C = make_ctx(nc)
    emit_attn(C, T, Sq, qT, kTf, Vf, qpos, kposc, lamv, subg, lamc, yT)
    C.S.op("pool", lambda h: None, reads=["a.outs"])
    close_ctx(C)
    _prog_cache[key] = nc
    return nc


HALO = 16


def emit_mix(C, T, x1T, zTh, upTh, yaT, w_out, convw, cvec, pool_w, invcnt, ident, x2T, tag="m"):
    S, A, nc = C.S, C.A, C.nc
    NT = 256
    ntiles = T // NT
    A.mark()
    Wo = A.alloc([8, D], BF16)
    idt = A.alloc([128], F32)
    cw = A.alloc([2, 31], F32)
    cv = A.alloc([8], F32)
    Dg = A.alloc([2, 31, 128], BF16)
    pst = A.alloc([2, 128], F32)
    PWf = A.alloc([2, 128], BF16)
    PWh = A.alloc([2, 128], BF16)
    ones256 = A.alloc([128], F32)
    epsb = A.alloc([1], F32)
    xt = [A.alloc([8, NT], F32) for _ in range(2)]
    zt = [A.alloc([2, NT + 2 * HALO], BF16) for _ in range(2)]
    ut = [A.alloc([2, NT + 2 * HALO], BF16) for _ in range(2)]
    ic = [A.alloc([2, NT], F32) for _ in range(2)]
    ycat = [A.alloc([8, NT], BF16) for _ in range(2)]
    cz = A.alloc([2, NT], F32)
    sqc = A.alloc([2, NT], F32)
    mean = A.alloc([NT], F32)
    m2 = A.alloc([NT], F32)
    rstd = A.alloc([NT], F32)
    t1 = A.alloc([2, NT], F32)
    pa = A.alloc([NT], F32)
    R = lambda n: f"{tag}.{n}"
    pc = bank(C, 0, NT)
    pm = bank(C, 1, NT)
    pvv = bank(C, 2, NT)
    pA = bank(C, 3, NT)
    pB = bank(C, 4, NT)
    po = [bank(C, 5, NT), bank(C, 6, NT)]

    S.op("pool", lambda h: h.memset(ones256, 1.0 / 256), writes=[R("ones256")])
    S.op("pool", lambda h: h.memset(epsb, EPS), writes=[R("epsb")])
    S.op("pool", lambda h: h.memset(pst, 0.0), writes=[R("pst")])
    S.dma("pool", R("cst"), lambda h: h.dma_start(out=idt, in_=ident), writes=[R("idt")])
    S.dma("pool", R("cst"), lambda h: h.dma_start(out=cw, in_=convw.rearrange("c p k -> p c k")), writes=[R("cw")])
    S.dma("pool", R("cst"), lambda h: h.dma_start(out=cv, in_=cvec), writes=[R("cv")])
    for g in range(4):
        c, r = g // 2, g % 2
        S.dma("pool", R("cst"), (lambda g, c, r: lambda h: h.dma_start(out=pst[64 * r:64 * r + 64, c, 64 * r:64 * r + 64], in_=pool_w[g]))(g, c, r),
              reads=[R("pst")], writes=[R(f"pst{g}")])
    for c in range(2):
        S.op("dve", (lambda c: lambda h: h.tensor_copy(out=PWf[:, c, :], in_=pst[:, c, :]))(c),
             reads=[R("pst0"), R("pst1"), R("pst2"), R("pst3")], writes=[R("PWf")])
        S.op("dve", (lambda c: lambda h: h.tensor_copy(out=PWh[:, c, :], in_=pst[:, c, :]))(c),
             reads=[R("pst0"), R("pst1"), R("pst2"), R("pst3")], writes=[R("PWh")])
        S.op("dve", (lambda c: lambda h: h.memset(PWh[0:64, c, :], 0.0))(c), reads=[R("PWh")], writes=[R("PWh")])
        for k in range(31):
            S.op("dve", (lambda c, k: lambda h: h.tensor_scalar(out=Dg[:, c, k, :], in0=idt, scalar1=cw[:, c, k:k + 1], scalar2=None,
                                                               op0=ALU.mult))(c, k),
                 reads=[R("idt"), R("cw")], writes=[R("Dg")])
    for dc in range(8):
        S.dma("pool", R("wld"), (lambda dc: lambda h: h.dma_start(out=Wo[:, dc, :], in_=w_out[dc * 128:(dc + 1) * 128, :]))(dc),
              writes=[R(f"Wo{dc}")])
    x1v = x1T.rearrange("c p t -> p c t")
    x2v = x2T.rearrange("c p t -> p c t")
    zv = zTh.rearrange("c p t -> p c t")
    uv = upTh.rearrange("c p t -> p c t")
    yav = yaT.rearrange("c p t -> p c t")
    icv = invcnt.rearrange("c p t -> p c t")

    def load(t):
        b = t % 2
        ts_ = slice(t * NT, (t + 1) * NT)
        th = slice(t * NT, (t + 1) * NT + 2 * HALO)
        S.dma(XQ, R(f"ld{b}"), lambda h: h.dma_start(out=xt[b], in_=x1v[:, :, ts_]), writes=[R(f"xt{b}")])
        S.dma(XQ, R(f"ld{b}"), lambda h: h.dma_start(out=zt[b], in_=zv[:, :, th]), writes=[R(f"zt{b}")], newgroup=False)
        S.dma(XQ, R(f"ld{b}"), lambda h: h.dma_start(out=ut[b], in_=uv[:, :, th]), writes=[R(f"ut{b}")], newgroup=False)
        S.dma(XQ, R(f"ld{b}"), lambda h: h.dma_start(out=ic[b], in_=icv[:, :, ts_]), writes=[R(f"ic{b}")], newgroup=False)
        S.dma(XQ, R(f"ld{b}"), lambda h: h.dma_start(out=ycat[b][:, 4:8, :], in_=yav[:, :, ts_]), writes=[R(f"ya{b}")], newgroup=False)

    def do_tile(t):
        b = t % 2
        ts_ = slice(t * NT, (t + 1) * NT)
        for c in range(2):
            for k in range(31):
                S.op("pe", (lambda c, k: lambda h: h.matmul(pc, lhsT=Dg[:, c, k, :], rhs=zt[b][:, c, k + 1:k + 1 + NT],
                                                            start=(k == 0), stop=(k == 30)))(c, k),
                     reads=[R("Dg"), R(f"zt{b}")], writes=[R("pc")])
            S.op("act", (lambda c: lambda h: h.activation(out=cz[:, c, :], in_=pc, func=AF.Identity, bias=cv[:, c:c + 1], scale=1.0))(c),
                 reads=[R("pc"), R("cv")], writes=[R(f"cz{c}")])
            S.op("act", (lambda c: lambda h: h.activation(out=sqc[:, c, :], in_=cz[:, c, :], func=AF.Square))(c),
                 reads=[R(f"cz{c}")], writes=[R(f"sqc{c}")])
        for c in range(2):
            S.op("pe", (lambda c: lambda h: h.matmul(pm, lhsT=ones256, rhs=cz[:, c, :], start=(c == 0), stop=(c == 1)))(c),
                 reads=[R("ones256"), R(f"cz{c}")], writes=[R("pm")])
        for c in range(2):
            S.op("pe", (lambda c: lambda h: h.matmul(pvv, lhsT=ones256, rhs=sqc[:, c, :], start=(c == 0), stop=(c == 1)))(c),
                 reads=[R("ones256"), R(f"sqc{c}")], writes=[R("pvv")])
        S.op("dve", lambda h: h.tensor_copy(out=mean, in_=pm), reads=[R("pm")], writes=[R("mean")])
        S.op("dve", lambda h: h.tensor_tensor(out=m2, in0=mean, in1=mean, op=ALU.mult), reads=[R("mean")], writes=[R("m2")])
        S.op("dve", lambda h: h.tensor_tensor(out=m2, in0=pvv, in1=m2, op=ALU.subtract), reads=[R("pvv"), R("m2")], writes=[R("m2")])
        S.op("act", lambda h: h.activation(out=rstd, in_=m2, func=AF.Sqrt, bias=epsb[:, 0:1], scale=1.0),
             reads=[R("m2"), R("epsb")], writes=[R("rstd")])
        S.op("dve", lambda h: h.reciprocal(out=rstd, in_=rstd), reads=[R("rstd")], writes=[R("rstd")])
        for c in range(2):
            S.op("dve", (lambda c: lambda h: h.tensor_tensor(out=t1[:, c, :], in0=cz[:, c, :], in1=mean, op=ALU.subtract))(c),
                 reads=[R(f"cz{c}"), R("mean")], writes=[R(f"t1{c}")])
            S.op("dve", (lambda c: lambda h: h.scalar_tensor_tensor(out=t1[:, c, :], in0=t1[:, c, :], scalar=cv[:, 2 + c:3 + c], in1=rstd,
                                                                   op0=ALU.mult, op1=ALU.mult))(c),
                 reads=[R(f"t1{c}"), R("rstd"), R("cv")], writes=[R(f"t1{c}")])
            S.op("act", (lambda c: lambda h: h.activation(out=ycat[b][:, c, :], in_=t1[:, c, :], func=AF.Silu, bias=cv[:, 4 + c:5 + c], scale=1.0))(c),
                 reads=[R(f"t1{c}"), R("cv")], writes=[R(f"yc{b}")])
        for c in range(2):
            taps = list(range(-2, 2)) if c == 0 else list(range(-8, 8))
            narrow = (1, ) if c == 0 else (4, )
            for i, tau in enumerate(taps):
                full = (-narrow[0] <= tau <= narrow[0] - 1)
                Wm = PWf if full else PWh
                S.op("pe", (lambda c, tau, Wm, i, n: lambda h: h.matmul(pA, lhsT=Wm[:, c, :], rhs=ut[b][:, c, HALO + tau:HALO + tau + NT],
                                                                        start=(i == 0), stop=(i == n - 1)))(c, tau, Wm, i, len(taps)),
                     reads=[R("PWf"), R("PWh"), R(f"ut{b}")], writes=[R("pA")])
            S.op("pe", (lambda c: lambda h: h.matmul(pB, lhsT=PWf[:, c, :], rhs=ut[b][:, c, HALO:HALO + NT], start=True, stop=True))(c),
                 reads=[R("PWf"), R(f"ut{b}")], writes=[R("pB")])
            S.op("dve", (lambda c: lambda h: h.tensor_tensor(out=pa, in0=pA, in1=ic[b][:, c, :], op=ALU.mult))(c),
                 reads=[R("pA"), R(f"ic{b}")], writes=[R("pa")])
            S.op("dve", (lambda c: lambda h: h.tensor_tensor(out=pa, in0=pa, in1=pB, op=ALU.subtract))(c),
                 reads=[R("pa"), R("pB")], writes=[R("pa")])
            S.op("dve", (lambda c: lambda h: h.tensor_scalar(out=ycat[b][:, 2 + c, :], in0=pa, scalar1=cv[:, 6 + c:7 + c], scalar2=None,
                                                            op0=ALU.mult))(c),
                 reads=[R("pa"), R("cv")], writes=[R(f"yc{b}")])
        for oc in range(8):
            k = oc % 2
            for c in range(8):
                S.op("pe", (lambda oc, c, k: lambda h: h.matmul(po[k], lhsT=Wo[:, c, oc * 128:(oc + 1) * 128], rhs=ycat[b][:, c, :],
                                                                start=(c == 0), stop=(c == 7)))(oc, c, k),
                     reads=[R(f"Wo{c}"), R(f"yc{b}"), R(f"ya{b}")], writes=[R(f"po{k}")])
            S.op("dve", (lambda oc, k: lambda h: h.tensor_tensor(out=xt[b][:, oc, :], in0=po[k], in1=xt[b][:, oc, :], op=ALU.add))(oc, k),
                 reads=[R(f"po{k}"), R(f"xt{b}")], writes=[R(f"xt{b}")])
        S.dma(XQ, R(f"st{b}"), lambda h: h.dma_start(out=x2v[:, :, ts_], in_=xt[b]), reads=[R(f"xt{b}")], writes=[R("outs")])
        if t + 2 < ntiles:
            load(t + 2)

    load(0)
    if ntiles > 1:
        load(1)
    for t in range(ntiles):
        do_tile(t)
    A.release()


def build_mix_prog(T):
    key = ("mix", T)
    if key in _prog_cache:
        return _prog_cache[key]
    nc = bass.Bass("TRN2", target_bir_lowering=False)
    x1T = nc.dram_tensor("x1T", [8, 128, T], F32, kind="ExternalInput").ap()
    zTh = nc.dram_tensor("zTh", [2, 128, T + 2 * HALO], BF16, kind="ExternalInput").ap()
    upTh = nc.dram_tensor("upTh", [2, 128, T + 2 * HALO], BF16, kind="ExternalInput").ap()
    yaT = nc.dram_tensor("yaT", [4, 128, T], BF16, kind="ExternalInput").ap()
    w_out = nc.dram_tensor("w_out", [D, D], F32, kind="ExternalInput").ap()
    convw = nc.dram_tensor("convw", [2, 128, 31], F32, kind="ExternalInput").ap()
    cvec = nc.dram_tensor("cvec", [128, 8], F32, kind="ExternalInput").ap()
    pool_w = nc.dram_tensor("pool_w", [4, 64, 64], F32, kind="ExternalInput").ap()
    invcnt = nc.dram_tensor("invcnt", [2, 128, T], F32, kind="ExternalInput").ap()
    ident = nc.dram_tensor("ident", [128, 128], F32, kind="ExternalInput").ap()
    x2T = nc.dram_tensor("x2T", [8, 128, T], F32, kind="ExternalOutput").ap()
    C = make_ctx(nc)
    emit_mix(C, T, x1T, zTh, upTh, yaT, w_out, convw, cvec, pool_w, invcnt, ident, x2T)
    C.S.op("pool", lambda h: None, reads=["m.outs"])
    close_ctx(C)
    _prog_cache[key] = nc
    return nc


def alibi_ramp(r, T, Sq):
    return np.ascontiguousarray(np.broadcast_to((np.arange(Sq + T, dtype=np.float32) - Sq + r * T)[None], (128, Sq + T)))


def pool_invcnt(pos0, T, Sq):
    t = np.arange(pos0, pos0 + T)
    out = np.zeros((2, 128, T), np.float32)
    for g, w in enumerate((2, 4, 8, 16)):
        lo = np.clip(t - w // 2, 0, Sq)
        hi = np.clip(t + w // 2, 0, Sq)
        out[g // 2, (g % 2) * 64:(g % 2) * 64 + 64, :] = (1.0 / (hi - lo).astype(np.float32))[None, :]
    return out


def _launch(nc, maps):
    res = run_bass_kernel_spmd(nc, maps, core_ids=list(range(len(maps))))
    return res.results


def kernel_unfused(x, ffn1_norm, ffn1_w_gate, ffn1_w_up, ffn1_w_down, mix_norm, w_in,
                   conv_dw, conv_dw_bias, conv_ln_gain, conv_ln_bias, pool_w, pool_scale,
                   q_norm, k_norm, lambda_q1, lambda_k1, lambda_q2, lambda_k2, attn_subln,
                   w_out, ffn2_norm, ffn2_w_gate, ffn2_w_up, ffn2_w_down, post_norm, _launch=_launch):
    import math
    f32 = np.float32
    x = np.asarray(x, f32)
    B, Sq, _ = x.shape
    T = Sq // 4
    ncore = 4 * B
    xT = [to_T(x[c // 4, (c % 4) * T:(c % 4 + 1) * T]) for c in range(ncore)]
    ident = np.eye(128, dtype=f32)
    kposc = (np.arange(Sq // 128, dtype=f32)[None] * 128 + np.arange(128, dtype=f32)[:, None]).copy()
    qpos = [alibi_ramp(c % 4, T, Sq) for c in range(ncore)]
    invc = [pool_invcnt((c % 4) * T, T, Sq) for c in range(ncore)]
    A = lambda a: np.ascontiguousarray(np.asarray(a, f32))
    for l in range(2):
        lambda_init = 0.8 - 0.6 * math.exp(-0.3 * l)
        nc = build_ffn_prog(T, False)
        g1 = gain_pc(A(ffn1_norm[l]))
        r = _launch(nc, [{"xin": xT[c], "wg": A(ffn1_w_gate[l]), "wu": A(ffn1_w_up[l]), "wd": A(ffn1_w_down[l]), "gvec": g1}
                         for c in range(ncore)])
        x1T = [r[c]["xout"] for c in range(ncore)]
        nc = build_inproj_prog(T)
        qg = np.tile(A(q_norm[l]), 2)[:, None].copy()
        kg = np.tile(A(k_norm[l]), 2)[:, None].copy()
        r = _launch(nc, [{"xin": x1T[c], "w_in": A(w_in[l]), "gvec": gain_pc(A(mix_norm[l])), "qg": qg, "kg": kg}
                         for c in range(ncore)])
        kTf, Vf, zh, uh = [], [], [], []
        for b in range(B):
            cs = range(4 * b, 4 * b + 4)
            kTf.append(np.concatenate([r[c]["kT"] for c in cs], axis=2))
            Vf.append(np.concatenate([r[c]["Vd"] for c in cs], axis=2))
            zf = np.concatenate([r[c]["zT"] for c in cs], axis=2)
            uf = np.concatenate([r[c]["upT"] for c in cs], axis=2)
            zh.append(np.pad(zf, ((0, 0), (0, 0), (HALO, HALO))))
            uh.append(np.pad(uf, ((0, 0), (0, 0), (HALO, HALO))))
        qTl = [r[c]["qT"] for c in range(ncore)]
        nc = build_attn_prog(T, Sq)
        lamv = np.ascontiguousarray(np.broadcast_to(np.stack([A(lambda_q1[l]), A(lambda_k1[l]), A(lambda_q2[l]), A(lambda_k2[l])])[None],
                                                    (128, 4, 64)))
        lamc = np.ascontiguousarray(np.broadcast_to(np.array([-lambda_init, 1.0 - lambda_init], f32)[None], (128, 2)))
        subg = A(attn_subln[l])[:, None].copy()
        r = _launch(nc, [{"qT": qTl[c], "kTf": kTf[c // 4], "Vf": Vf[c // 4], "ramp": qpos[c], "kposc": kposc, "lamv": lamv,
                          "subg": subg, "lamc": lamc} for c in range(ncore)])
        yaT = [r[c]["yT"] for c in range(ncore)]
        nc = build_mix_prog(T)
        cb, lg, lb, psc = A(conv_dw_bias[l]), A(conv_ln_gain[l]), A(conv_ln_bias[l]), A(pool_scale[l])
        cvec = np.stack([cb[:128], cb[128:], lg[:128], lg[128:], lb[:128], lb[128:], psc[:128], psc[128:]], 1).astype(f32)
        convw = np.ascontiguousarray(A(conv_dw[l]).T.reshape(2, 128, 31))
        maps = []
        for c in range(ncore):
            o = (c % 4) * T
            maps.append({"x1T": x1T[c], "zTh": np.ascontiguousarray(zh[c // 4][:, :, o:o + T + 2 * HALO]),
                         "upTh": np.ascontiguousarray(uh[c // 4][:, :, o:o + T + 2 * HALO]), "yaT": yaT[c], "w_out": A(w_out[l]),
                         "convw": convw, "cvec": cvec, "pool_w": A(pool_w[l]), "invcnt": invc[c], "ident": ident})
        r = _launch(nc, maps)
        x2T = [r[c]["x2T"] for c in range(ncore)]
        nc = build_ffn_prog(T, True)
        r = _launch(nc, [{"xin": x2T[c], "wg": A(ffn2_w_gate[l]), "wu": A(ffn2_w_up[l]), "wd": A(ffn2_w_down[l]),
                          "gvec": gain_pc(A(ffn2_norm[l])), "pgvec": gain_pc(A(post_norm[l]))} for c in range(ncore)])
        xT = [r[c]["xout"] for c in range(ncore)]
    out = np.empty((B, Sq, D), f32)
    for c in range(ncore):
        out[c // 4, (c % 4) * T:(c % 4 + 1) * T] = from_T(xT[c])
    return out


LAYERS_PER_LAUNCH = 2
LAYER_W = ("wg1", "wu1", "wd1", "w_in", "w_out", "wg2", "wu2", "wd2")


def emit_exchange(C, T, kT, Vd, kTg, Vg, zTh, upTh, Ed, Eg, selL, selR, tag="x"):
    S, A, nc = C.S, C.A, C.nc
    R = lambda n: f"{tag}.{n}"
    groups = [[0, 1, 2, 3], [4, 5, 6, 7]]
    A.mark()
    H = HALO
    eg = A.alloc([4, 4, 2 * H], BF16)
    hl = A.alloc([4, H], BF16)
    hr = A.alloc([4, H], BF16)
    sl = A.alloc([4], F32)
    sr = A.alloc([4], F32)
    S.dma("pool", R("cst"), lambda h: h.dma_start(out=sl, in_=selL), writes=[R("sl")])
    S.dma("pool", R("cst"), lambda h: h.dma_start(out=sr, in_=selR), writes=[R("sr")])
    for j, src in enumerate((zTh, upTh)):
        for c in range(2):
            S.dma("pool", R("edge"), (lambda j, c, src: lambda h: h.dma_start(out=Ed[2 * j + c, :, 0:H], in_=src[c, :, H:2 * H]))(j, c, src),
                  writes=[R(f"Ed{j}{c}a")], newgroup=(j == 0 and c == 0))
            S.dma("pool", R("edge"), (lambda j, c, src: lambda h: h.dma_start(out=Ed[2 * j + c, :, H:2 * H], in_=src[c, :, T:T + H]))(j, c, src),
                  writes=[R(f"Ed{j}{c}b")], newgroup=False)
    for hd in range(4):
        S.dma("pool", R("cc"), (lambda hd: lambda h: h.collective_compute("AllGather", ALU.bypass, replica_groups=groups,
                                                                          ins=[kT[hd]], outs=[kTg[hd].rearrange("r p t -> (r p) t")]))(hd),
              writes=[R(f"kTg{hd}")], inc=1)
        S.dma("pool", R("cc"), (lambda hd: lambda h: h.collective_compute("AllGather", ALU.bypass, replica_groups=groups,
                                                                          ins=[Vd[hd].rearrange("p c e -> p (c e)")],
                                                                          outs=[Vg[hd].rearrange("r p c e -> (r p) (c e)")]))(hd),
              writes=[R(f"Vg{hd}")], inc=1)
    S.dma("pool", R("cc"), lambda h: h.collective_compute("AllGather", ALU.bypass, replica_groups=groups,
                                                          ins=[Ed.rearrange("j p e -> (j p) e")], outs=[Eg.rearrange("r j p e -> (r j p) e")]),
          reads=[R(f"Ed{j}{c}{x}") for j in range(2) for c in range(2) for x in "ab"], writes=[R("Eg")], inc=1)
    S.dma("pool", R("egl"), lambda h: h.dma_start(out=eg, in_=Eg.rearrange("r j p e -> p r j e")), reads=[R("Eg")], writes=[R("eg")])
    for r in range(4):
        if r == 0:
            S.op("dve", lambda h: h.tensor_scalar(out=hl, in0=eg[:, 0, :, H:2 * H], scalar1=sl[:, 0:1], scalar2=None, op0=ALU.mult),
                 reads=[R("eg"), R("sl")], writes=[R("hl")])
            S.op("dve", lambda h: h.tensor_scalar(out=hr, in0=eg[:, 0, :, 0:H], scalar1=sr[:, 0:1], scalar2=None, op0=ALU.mult),
                 reads=[R("eg"), R("sr")], writes=[R("hr")])
        else:
            S.op("dve", (lambda r: lambda h: h.scalar_tensor_tensor(out=hl, in0=eg[:, r, :, H:2 * H], scalar=sl[:, r:r + 1], in1=hl,
                                                                   op0=ALU.mult, op1=ALU.add))(r),
                 reads=[R("eg"), R("sl"), R("hl")], writes=[R("hl")])
            S.op("dve", (lambda r: lambda h: h.scalar_tensor_tensor(out=hr, in0=eg[:, r, :, 0:H], scalar=sr[:, r:r + 1], in1=hr,
                                                                   op0=ALU.mult, op1=ALU.add))(r),
                 reads=[R("eg"), R("sr"), R("hr")], writes=[R("hr")])
    for j, dst in enumerate((zTh, upTh)):
        S.dma("pool", R("hst"), (lambda j, dst: lambda h: h.dma_start(out=dst[:, :, 0:H].rearrange("c p e -> p c e"),
                                                                    in_=hl[:, 2 * j:2 * j + 2, :]))(j, dst),
              reads=[R("hl")], writes=[R(f"haloL{j}")], newgroup=(j == 0))
        S.dma("pool", R("hst"), (lambda j, dst: lambda h: h.dma_start(out=dst[:, :, T + H:T + 2 * H].rearrange("c p e -> p c e"),
                                                                    in_=hr[:, 2 * j:2 * j + 2, :]))(j, dst),
              reads=[R("hr")], writes=[R(f"haloR{j}")], newgroup=False)
    A.release()


def build_fused_prog(T, Sq, nlayers=2):
    key = ("fused", T, Sq, nlayers)
    if key in _prog_cache:
        return _prog_cache[key]
    nc = bass.Bass("TRN2", target_bir_lowering=False)
    EI = lambda name, shape, dt=F32: nc.dram_tensor(name, shape, dt, kind="ExternalInput").ap()
    IN = lambda name, shape, dt: nc.dram_tensor(name, shape, dt).ap()
    xin = EI("xin", [8, 128, T])
    wshape = {"wg1": [D, DFF], "wu1": [D, DFF], "wd1": [DFF, D], "w_in": [D, DIN], "w_out": [D, D],
              "wg2": [D, DFF], "wu2": [D, DFF], "wd2": [DFF, D]}
    Wt = [{k: EI(f"{k}_{l}", wshape[k]) for k in LAYER_W} for l in range(nlayers)]
    P = []
    for l in range(nlayers):
        P.append({"g1": EI(f"g1_{l}", [128, 8]), "gm": EI(f"gm_{l}", [128, 8]), "g2": EI(f"g2_{l}", [128, 8]),
                  "gp": EI(f"gp_{l}", [128, 8]), "qg": EI(f"qg_{l}", [128, 1]), "kg": EI(f"kg_{l}", [128, 1]),
                  "lamv": EI(f"lamv_{l}", [128, 4, 64]), "subg": EI(f"subg_{l}", [128, 1]), "lamc": EI(f"lamc_{l}", [128, 2]),
                  "convw": EI(f"convw_{l}", [2, 128, 31]), "cvec": EI(f"cvec_{l}", [128, 8]), "pool_w": EI(f"poolw_{l}", [4, 64, 64])})
    qpos = EI("ramp", [128, Sq + T])
    kposc = EI("kposc", [128, Sq // 128])
    invcnt = EI("invcnt", [2, 128, T])
    ident = EI("ident", [128, 128])
    selL = EI("selL", [128, 4])
    selR = EI("selR", [128, 4])
    xout = nc.dram_tensor("xout", [8, 128, T], F32, kind="ExternalOutput").ap()
    xa = IN("xa", [8, 128, T], F32)
    xb = IN("xb", [8, 128, T], F32)
    xc = IN("xc", [8, 128, T], F32)
    zTh = IN("zTh", [2, 128, T + 2 * HALO], BF16)
    upTh = IN("upTh", [2, 128, T + 2 * HALO], BF16)
    qT = IN("qT", [4, 128, T], BF16)
    kT = IN("kT", [4, 128, T], BF16)
    Vd = IN("Vd", [4, 128, T // 128, 128], BF16)
    kTg = IN("kTg", [4, 4, 128, T], BF16)
    Vg = IN("Vg", [4, 4, 128, T // 128, 128], BF16)
    Ed = IN("Ed", [4, 128, 2 * HALO], BF16)
    Eg = IN("Eg", [4, 4, 128, 2 * HALO], BF16)
    yaT = IN("yaT", [4, 128, T], BF16)
    C = make_ctx(nc)
    S = C.S
    cur = xin
    for l in range(nlayers):
        w, p = Wt[l], P[l]
        emit_ffn(C, T, cur, xa, w["wg1"], w["wu1"], w["wd1"], p["g1"], None, tag="f")
        S.barrier()
        emit_inproj(C, T, xa, w["w_in"], p["gm"], p["qg"], p["kg"], zTh, upTh, qT, kT, Vd, tag="i", zoff=HALO)
        S.barrier()
        emit_exchange(C, T, kT, Vd, kTg, Vg, zTh, upTh, Ed, Eg, selL, selR, tag="x")
        S.barrier()
        emit_attn(C, T, Sq, qT, kTg, Vg, qpos, kposc, p["lamv"], p["subg"], p["lamc"], yaT, tag="a", gathered=True)
        S.barrier()
        emit_mix(C, T, xa, zTh, upTh, yaT, w["w_out"], p["convw"], p["cvec"], p["pool_w"], invcnt, ident, xb, tag="m")
        S.barrier()
        dst = xc if l < nlayers - 1 else xout
        emit_ffn(C, T, xb, dst, w["wg2"], w["wu2"], w["wd2"], p["g2"], p["gp"], tag="f")
        S.barrier()
        cur = xc
    close_ctx(C)
    _prog_cache[key] = nc
    return nc


def fused_inputs(c, T, Sq, xT, weights, params):
    f32 = np.float32
    r = c % 4
    m = {"xin": xT, "ramp": alibi_ramp(r, T, Sq),
         "kposc": (np.arange(Sq // 128, dtype=f32)[None] * 128 + np.arange(128, dtype=f32)[:, None]).copy(),
         "invcnt": pool_invcnt(r * T, T, Sq), "ident": np.eye(128, dtype=f32)}
    sl = np.zeros((128, 4), f32)
    sr = np.zeros((128, 4), f32)
    if r > 0:
        sl[:, r - 1] = 1.0
    if r < 3:
        sr[:, r + 1] = 1.0
    m["selL"], m["selR"] = sl, sr
    m.update(weights)
    m.update(params)
    return m


def kernel_fused(inputs, launch):
    import math
    f32 = np.float32
    A = lambda a: np.ascontiguousarray(np.asarray(a, f32))
    x = A(inputs["x"])
    B, Sq, _ = x.shape
    T = Sq // 4
    ncore = 4 * B
    weights, params = {}, {}
    names = {"wg1": "ffn1_w_gate", "wu1": "ffn1_w_up", "wd1": "ffn1_w_down", "w_in": "w_in", "w_out": "w_out",
             "wg2": "ffn2_w_gate", "wu2": "ffn2_w_up", "wd2": "ffn2_w_down"}
    for l in range(2):
        lambda_init = 0.8 - 0.6 * math.exp(-0.3 * l)
        for k, src in names.items():
            weights[f"{k}_{l}"] = A(inputs[src][l])
        params[f"g1_{l}"] = gain_pc(A(inputs["ffn1_norm"][l]))
        params[f"gm_{l}"] = gain_pc(A(inputs["mix_norm"][l]))
        params[f"g2_{l}"] = gain_pc(A(inputs["ffn2_norm"][l]))
        params[f"gp_{l}"] = gain_pc(A(inputs["post_norm"][l]))
        params[f"qg_{l}"] = np.tile(A(inputs["q_norm"][l]), 2)[:, None].copy()
        params[f"kg_{l}"] = np.tile(A(inputs["k_norm"][l]), 2)[:, None].copy()
        params[f"lamv_{l}"] = np.ascontiguousarray(np.broadcast_to(
            np.stack([A(inputs["lambda_q1"][l]), A(inputs["lambda_k1"][l]), A(inputs["lambda_q2"][l]), A(inputs["lambda_k2"][l])])[None],
            (128, 4, 64)))
        params[f"subg_{l}"] = A(inputs["attn_subln"][l])[:, None].copy()
        params[f"lamc_{l}"] = np.ascontiguousarray(np.broadcast_to(np.array([-lambda_init, 1.0 - lambda_init], f32)[None], (128, 2)))
        params[f"convw_{l}"] = np.ascontiguousarray(A(inputs["conv_dw"][l]).T.reshape(2, 128, 31))
        cb, lg, lb, psc = (A(inputs[n][l]) for n in ("conv_dw_bias", "conv_ln_gain", "conv_ln_bias", "pool_scale"))
        params[f"cvec_{l}"] = np.stack([cb[:128], cb[128:], lg[:128], lg[128:], lb[:128], lb[128:], psc[:128], psc[128:]], 1).astype(f32)
        params[f"poolw_{l}"] = A(inputs["pool_w"][l])
    xT = [to_T(x[c // 4, (c % 4) * T:(c % 4 + 1) * T]) for c in range(ncore)]
    if LAYERS_PER_LAUNCH == 2:
        nc = build_fused_prog(T, Sq, 2)
        r = launch(nc, [fused_inputs(c, T, Sq, xT[c], weights, params) for c in range(ncore)])
        xT = [r[c]["xout"] for c in range(ncore)]
    else:
        nc = build_fused_prog(T, Sq, 1)
        for l in range(2):
            wl = {k[:-2] + "_0": v for k, v in weights.items() if k.endswith(f"_{l}")}
            pl = {k[:-2] + "_0": v for k, v in params.items() if k.endswith(f"_{l}")}
            r = launch(nc, [fused_inputs(c, T, Sq, xT[c], wl, pl) for c in range(ncore)])
            xT = [r[c]["xout"] for c in range(ncore)]
    out = np.empty((B, Sq, D), f32)
    for c in range(ncore):
        out[c // 4, (c % 4) * T:(c % 4 + 1) * T] = from_T(xT[c])
    return out


def kernel(**inputs):
    return kernel_fused(inputs, _launch)
```

```python
import numpy as np
import ml_dtypes
import concourse.bass as bass
import concourse.mybir as mybir
from concourse.bass_utils import run_bass_kernel_spmd

F32 = mybir.dt.float32
BF16 = mybir.dt.bfloat16
AF = mybir.ActivationFunctionType
ALU = mybir.AluOpType

D = 1024
DFF = 2816
NFC = DFF // 128
DIN = 2304
S_LEN = 16384
TPC = 4096
NCORES = 8
EPS = 1e-6

ENGS = ("pe", "act", "dve", "pool", "sp")
SAME_ENGINE_SYNC = True
SEM_ROT = 24000
XQ = "pool"


class _Op:
    __slots__ = ("eng", "fn", "reads", "writes", "chan", "group", "deps", "sig",
                 "idx", "pos", "xdeps", "inc")


class Sched:
    def __init__(self, nc):
        self.nc = nc
        self.ops = []
        self.chan_state = {}
        self.last_on = {}

    def op(self, eng, fn, reads=(), writes=(), xdeps=()):
        o = _Op()
        o.eng, o.fn, o.reads, o.writes = eng, fn, tuple(reads), tuple(writes)
        o.chan = None
        o.group = None
        o.xdeps = tuple(xdeps)
        o.idx = len(self.ops)
        self.ops.append(o)
        self.last_on[eng] = o.idx
        return o

    def dma(self, eng, chan, fn, reads=(), writes=(), newgroup=True, inc=16):
        o = self.op(eng, fn, reads, writes)
        o.chan = chan
        o.inc = inc
        st = self.chan_state.setdefault(chan, {"groups": []})
        if newgroup or not st["groups"]:
            st["groups"].append([])
        st["groups"][-1].append(o.idx)
        o.group = len(st["groups"]) - 1
        return o

    def barrier(self):
        lasts = [i for i in self.last_on.values()]
        for st in self.chan_state.values():
            if st["groups"]:
                lasts.append(st["groups"][-1][-1])
        for e in ENGS:
            self.op(e, None, xdeps=lasts)

    def finalize(self):
        nc = self.nc
        ops = self.ops
        last_w = {}
        readers = {}
        for o in ops:
            deps = set(o.xdeps)
            for r in o.reads:
                if r in last_w:
                    deps.add(last_w[r])
            for w in o.writes:
                if w in last_w:
                    deps.add(last_w[w])
                for rd in readers.get(w, ()):
                    deps.add(rd)
            deps.discard(o.idx)
            o.deps = deps
            for r in o.reads:
                readers.setdefault(r, []).append(o.idx)
            for w in o.writes:
                last_w[w] = o.idx
                readers[w] = []
        for chan, st in self.chan_state.items():
            cum = 0
            vals = []
            for g in st["groups"]:
                cum += sum(ops[i].inc for i in g)
                vals.append(cum)
            st["vals"] = vals
            assert cum < 65000, (chan, cum)
        pos = {e: 0 for e in ENGS}
        for o in ops:
            o.pos = pos[o.eng]
            pos[o.eng] += 1
        waited = {e: {} for e in ENGS}
        need = []
        sig_needed = set()
        for o in ops:
            w = {}
            for d in o.deps:
                p = ops[d]
                if p.chan is not None:
                    if o.chan == p.chan and o.group == p.group:
                        continue
                    s = ("c", p.chan)
                    key = p.group
                else:
                    if p.fn is None:
                        continue
                    s = ("e", p.eng)
                    key = p.pos
                    if p.eng == o.eng:
                        if p.eng == "pe" or not SAME_ENGINE_SYNC:
                            continue
                if s not in w or w[s][0] < key:
                    w[s] = (key, d)
            if o.chan is not None:
                st = self.chan_state[o.chan]
                if o.group > 0 and st["groups"][o.group][0] == o.idx:
                    s = ("c", o.chan)
                    key = o.group - 1
                    if s not in w or w[s][0] < key:
                        w[s] = (key, None)
            lst = []
            for s, (key, d) in w.items():
                if waited[o.eng].get(s, -1) >= key:
                    continue
                waited[o.eng][s] = key
                lst.append((s, key, d))
                if s[0] == "e":
                    sig_needed.add(d)
            need.append(lst)
        sigcount = {e: 0 for e in ENGS}
        for o in ops:
            if o.chan is None and o.idx in sig_needed:
                sigcount[o.eng] += 1
                o.sig = sigcount[o.eng]
            else:
                o.sig = None
        self._sem_ctx = []
        eng_sems = {}
        for e in ENGS:
            n = (sigcount[e] + SEM_ROT - 1) // SEM_ROT
            eng_sems[e] = [self._alloc_sem(f"s_{e}{i}") for i in range(max(n, 1))]
        chan_sems = {}
        for chan in self.chan_state:
            chan_sems[chan] = self._alloc_sem(f"c_{chan}")
        self.nsems = sum(len(v) for v in eng_sems.values()) + len(chan_sems)
        self.counts = dict(pos)

        def eng_wait_target(p):
            k = p.sig - 1
            return eng_sems[p.eng][k // SEM_ROT], (k % SEM_ROT) + 1

        streams = {e: [] for e in ENGS}
        for o in ops:
            streams[o.eng].append(o)

        semv = {}
        ptr = {e: 0 for e in ENGS}
        progress = True
        while progress:
            progress = False
            for e in ENGS:
                while ptr[e] < len(streams[e]):
                    o = streams[e][ptr[e]]
                    ok = True
                    for (s, key, d) in need[o.idx]:
                        if s[0] == "e":
                            sk, val = ("e", ops[d].eng), ops[d].sig
                        else:
                            sk, val = s, self.chan_state[s[1]]["vals"][key]
                        if semv.get(sk, 0) < val:
                            ok = False
                            break
                    if not ok:
                        break
                    if o.chan is not None:
                        semv[("c", o.chan)] = semv.get(("c", o.chan), 0) + o.inc
                    elif o.sig is not None:
                        semv[("e", o.eng)] = semv.get(("e", o.eng), 0) + 1
                        assert semv[("e", o.eng)] == o.sig
                    ptr[e] += 1
                    progress = True
        stuck = []
        for e in ENGS:
            if ptr[e] != len(streams[e]):
                o = streams[e][ptr[e]]
                unsat = []
                for (s_, key, d) in need[o.idx]:
                    if s_[0] == "e":
                        sk, val = ("e", ops[d].eng), ops[d].sig
                    else:
                        sk, val = s_, self.chan_state[s_[1]]["vals"][key]
                    if semv.get(sk, 0) < val:
                        unsat.append((sk, val, semv.get(sk, 0)))
                stuck.append((e, ptr[e], len(streams[e]), o.reads, o.writes, o.chan, unsat))
        assert not stuck, ("DEADLOCK", stuck)

        self.need = need
        self.streams = streams
        def run_stream(e, handle):
            for o in streams[e]:
                for (s, key, d) in need[o.idx]:
                    if s[0] == "e":
                        sem, val = eng_wait_target(ops[d])
                    else:
                        sem = chan_sems[s[1]]
                        val = self.chan_state[s[1]]["vals"][key]
                    handle.wait_ge(sem, val)
                ins = o.fn(handle) if o.fn is not None else None
                if ins is None:
                    assert o.chan is None and o.sig is None, "barrier op cannot signal"
                    continue
                if o.chan is not None:
                    ins.then_inc(chan_sems[o.chan], o.inc)
                elif o.sig is not None:
                    k = o.sig - 1
                    ins.then_inc(eng_sems[o.eng][k // SEM_ROT], 1)

        with nc.Block() as block:
            if streams["pe"]:
                @block.tensor
                def _(h):
                    run_stream("pe", h)
            if streams["act"]:
                @block.scalar
                def _(h):
                    run_stream("act", h)
            if streams["dve"]:
                @block.vector
                def _(h):
                    run_stream("dve", h)
            if streams["pool"]:
                @block.gpsimd
                def _(h):
                    run_stream("pool", h)
            if streams["sp"]:
                @block.sync
                def _(h):
                    run_stream("sp", h)
        for c in reversed(self._sem_ctx):
            c.__exit__(None, None, None)

    def _alloc_sem(self, name):
        c = self.nc.semaphore(name)
        s = c.__enter__()
        self._sem_ctx.append(c)
        return s


class Arena:
    def __init__(self, nc, nbytes, name="arena"):
        self.nc = nc
        self.n32 = nbytes // 4
        self.ctx = nc.sbuf_tensor(name, [128, self.n32], F32)
        self.t = self.ctx.__enter__()
        self.off = 0
        self.marks = []

    def alloc(self, shape, dt):
        esz = 2 if dt == BF16 else 4
        n = 1
        for s in shape:
            n *= s
        nb = (n * esz + 31) // 32 * 32
        a = self.off
        assert a + nb <= self.n32 * 4, ("SBUF arena overflow", a, nb, self.n32 * 4)
        self.off += nb
        ap = self.t[:, a // 4:(a + nb) // 4]
        if dt != F32:
            ap = ap.bitcast(dt)
        ap = ap[:, 0:n]
        if len(shape) == 2:
            ap = ap.rearrange("p (a b) -> p a b", a=shape[0])
        elif len(shape) == 3:
            ap = ap.rearrange("p (a b c) -> p a b c", a=shape[0], b=shape[1])
        return ap

    def mark(self):
        self.marks.append(self.off)

    def release(self):
        self.off = self.marks.pop()

    def close(self):
        self.ctx.__exit__(None, None, None)


class Ctx:
    pass


def make_ctx(nc):
    C = Ctx()
    C.nc = nc
    C.S = Sched(nc)
    C.A = Arena(nc, 207 * 1024)
    C.psctx = nc.psum_tensor("psum_all", [128, 4096], F32)
    C.ps = C.psctx.__enter__()
    C.uid = 0
    return C


def close_ctx(C):
    C.S.finalize()
    C.psctx.__exit__(None, None, None)
    C.A.close()


def bank(C, b, n=512, off=0):
    return C.ps[:, b * 512 + off:b * 512 + off + n]


def emit_ffn(C, T, xT_in, xT_out, wg, wu, wd, gvec, post_gvec=None, tag="f", NXB=2):
    S, A, nc = C.S, C.A, C.nc
    NT = 256
    ntiles = T // NT
    A.mark()
    Wg = A.alloc([8, DFF], BF16)
    Wu = A.alloc([8, DFF], BF16)
    Wd = A.alloc([NFC, D], BF16)
    onesf = A.alloc([128], F32)
    gv = A.alloc([8], F32)
    pgv = A.alloc([8], F32) if post_gvec is not None else None
    epsb = A.alloc([1], F32)
    xt = [A.alloc([8, NT], F32) for _ in range(NXB)]
    hT = [A.alloc([8, NT], BF16) for _ in range(2)]
    aT = A.alloc([NFC, NT], BF16)
    sq = [A.alloc([NT], F32) for _ in range(2)]
    sg = [A.alloc([NT], F32) for _ in range(2)]
    rstd = A.alloc([NT], F32)
    rstd2 = A.alloc([NT], F32)
    pg = [bank(C, 0, NT), bank(C, 1, NT)]
    pu = [bank(C, 2, NT), bank(C, 3, NT)]
    pd = [bank(C, 4, NT), bank(C, 5, NT)]
    pstat = bank(C, 6, NT)
    pstat2 = bank(C, 7, NT)
    R = lambda n: f"{tag}.{n}"

    S.op("pool", lambda h: h.memset(onesf, 1.0 / D), writes=[R("onesf")])
    S.op("pool", lambda h: h.memset(epsb, EPS), writes=[R("epsb")])
    S.dma("sp", R("cst"), lambda h: h.dma_start(out=gv, in_=gvec), writes=[R("gv")])
    if post_gvec is not None:
        S.dma("sp", R("cst"), lambda h: h.dma_start(out=pgv, in_=post_gvec), writes=[R("pgv")])
    xin_v = xT_in.rearrange("c p t -> p c t")
    xout_v = xT_out.rearrange("c p t -> p c t")

    def load(t):
        b = t % NXB
        S.dma(XQ, R(f"xin{b}"), lambda h: h.dma_start(out=xt[b], in_=xin_v[:, :, t * NT:(t + 1) * NT]),
              writes=[R(f"xt{b}")])

    for _t in range(min(NXB, ntiles)):
        load(_t)
    for dc in range(8):
        S.dma("pool", R("wld"), (lambda dc: lambda h: h.dma_start(out=Wg[:, dc, :], in_=wg[dc * 128:(dc + 1) * 128, :]))(dc),
              writes=[R(f"Wg{dc}")])
        S.dma("pool", R("wld"), (lambda dc: lambda h: h.dma_start(out=Wu[:, dc, :], in_=wu[dc * 128:(dc + 1) * 128, :]))(dc),
              writes=[R(f"Wu{dc}")])
    for fc in range(NFC):
        S.dma("pool", R("wld"), (lambda fc: lambda h: h.dma_start(out=Wd[:, fc, :], in_=wd[fc * 128:(fc + 1) * 128, :]))(fc),
              writes=[R(f"Wd{fc}")])

    def stats(xbuf, xres, pst, rs, rsres):
        for c in range(8):
            k = c % 2
            S.op("act", (lambda c, k: lambda h: h.activation(out=sq[k], in_=xbuf[:, c, :], func=AF.Square))(c, k),
                 reads=[xres], writes=[R(f"sq{k}")])
            S.op("pe", (lambda c, k: lambda h: h.matmul(pst, lhsT=onesf, rhs=sq[k], start=(c == 0), stop=(c == 7)))(c, k),
                 reads=[R(f"sq{k}"), R("onesf")], writes=[rsres + ".ps"])
        S.op("act", lambda h: h.activation(out=rs, in_=pst, func=AF.Sqrt, bias=epsb[:, 0:1], scale=1.0),
             reads=[rsres + ".ps", R("epsb")], writes=[rsres])
        S.op("dve", lambda h: h.reciprocal(out=rs, in_=rs), reads=[rsres], writes=[rsres])

    def make_h(t):
        b = t % 2
        xb = t % NXB
        stats(xt[xb], R(f"xt{xb}"), pstat, rstd, R("rstd"))
        for c in range(8):
            S.op("dve", (lambda c: lambda h: h.scalar_tensor_tensor(out=hT[b][:, c, :], in0=xt[xb][:, c, :], scalar=gv[:, c:c + 1],
                                                                   in1=rstd, op0=ALU.mult, op1=ALU.mult))(c),
                 reads=[R(f"xt{xb}"), R("rstd"), R("gv")], writes=[R(f"hT{b}")])

    def gateup(t):
        b = t % 2
        for fc in range(NFC):
            k = fc % 2
            for dc in range(8):
                S.op("pe", (lambda fc, dc, k: lambda h: h.matmul(pg[k], lhsT=Wg[:, dc, fc * 128:(fc + 1) * 128], rhs=hT[b][:, dc, :],
                                                                 start=(dc == 0), stop=(dc == 7)))(fc, dc, k),
                     reads=[R(f"Wg{dc}"), R(f"hT{b}")], writes=[R(f"pg{k}")])
            for dc in range(8):
                S.op("pe", (lambda fc, dc, k: lambda h: h.matmul(pu[k], lhsT=Wu[:, dc, fc * 128:(fc + 1) * 128], rhs=hT[b][:, dc, :],
                                                                 start=(dc == 0), stop=(dc == 7)))(fc, dc, k),
                     reads=[R(f"Wu{dc}"), R(f"hT{b}")], writes=[R(f"pu{k}")])
            S.op("act", (lambda k: lambda h: h.activation(out=sg[k], in_=pg[k], func=AF.Silu))(k),
                 reads=[R(f"pg{k}")], writes=[R(f"sg{k}")])
            S.op("dve", (lambda fc, k: lambda h: h.tensor_tensor(out=aT[:, fc, :], in0=sg[k], in1=pu[k], op=ALU.mult))(fc, k),
                 reads=[R(f"sg{k}"), R(f"pu{k}")], writes=[R(f"aT{fc}")])

    def down(t):
        b = t % NXB
        for oc in range(8):
            k = oc % 2
            for fc in range(NFC):
                S.op("pe", (lambda oc, fc, k: lambda h: h.matmul(pd[k], lhsT=Wd[:, fc, oc * 128:(oc + 1) * 128], rhs=aT[:, fc, :],
                                                                 start=(fc == 0), stop=(fc == NFC - 1)))(oc, fc, k),
                     reads=[R(f"Wd{fc}"), R(f"aT{fc}")], writes=[R(f"pd{k}")])
            S.op("dve", (lambda oc, k: lambda h: h.scalar_tensor_tensor(out=xt[b][:, oc, :], in0=pd[k], scalar=0.5, in1=xt[b][:, oc, :],
                                                                       op0=ALU.mult, op1=ALU.add))(oc, k),
                 reads=[R(f"pd{k}"), R(f"xt{b}")], writes=[R(f"xt{b}")])
        if post_gvec is not None:
            stats(xt[b], R(f"xt{b}"), pstat2, rstd2, R("rstd2"))
            for c in range(8):
                S.op("dve", (lambda c: lambda h: h.scalar_tensor_tensor(out=xt[b][:, c, :], in0=xt[b][:, c, :], scalar=pgv[:, c:c + 1],
                                                                       in1=rstd2, op0=ALU.mult, op1=ALU.mult))(c),
                     reads=[R(f"xt{b}"), R("rstd2"), R("pgv")], writes=[R(f"xt{b}")])
        S.dma(XQ, R(f"xout{b}"), lambda h: h.dma_start(out=xout_v[:, :, t * NT:(t + 1) * NT], in_=xt[b]),
              reads=[R(f"xt{b}")], writes=[R("xout")])

    make_h(0)
    for t in range(ntiles):
        gateup(t)
        if t + 1 < ntiles:
            make_h(t + 1)
        down(t)
        if t + NXB < ntiles:
            load(t + NXB)
    A.release()


_prog_cache = {}


def build_ffn_prog(T, post):
    key = ("ffn", T, post)
    if key in _prog_cache:
        return _prog_cache[key]
    nc = bass.Bass("TRN2", target_bir_lowering=False)
    xin = nc.dram_tensor("xin", [8, 128, T], F32, kind="ExternalInput").ap()
    wg = nc.dram_tensor("wg", [D, DFF], F32, kind="ExternalInput").ap()
    wu = nc.dram_tensor("wu", [D, DFF], F32, kind="ExternalInput").ap()
    wd = nc.dram_tensor("wd", [DFF, D], F32, kind="ExternalInput").ap()
    gvec = nc.dram_tensor("gvec", [128, 8], F32, kind="ExternalInput").ap()
    pg = nc.dram_tensor("pgvec", [128, 8], F32, kind="ExternalInput").ap() if post else None
    xout = nc.dram_tensor("xout", [8, 128, T], F32, kind="ExternalOutput").ap()
    C = make_ctx(nc)
    emit_ffn(C, T, xin, xout, wg, wu, wd, gvec, pg)
    C.S.op("sp", lambda h: None, reads=["f.xout"])
    close_ctx(C)
    _prog_cache[key] = nc
    return nc


def to_T(x2d):
    T = x2d.shape[0]
    return np.ascontiguousarray(x2d.T.reshape(8, 128, T))


def from_T(xT):
    T = xT.shape[2]
    return np.ascontiguousarray(xT.reshape(1024, T).T)


def gain_pc(g):
    return np.ascontiguousarray(g.reshape(8, 128).T).astype(np.float32)


def run_ffn(xT_list, wg, wu, wd, g, post_g=None):
    T = xT_list[0].shape[2]
    nc = build_ffn_prog(T, post_g is not None)
    maps = []
    for xT in xT_list:
        m = {"xin": xT, "wg": wg, "wu": wu, "wd": wd, "gvec": gain_pc(g)}
        if post_g is not None:
            m["pgvec"] = gain_pc(post_g)
        maps.append(m)
    res = run_bass_kernel_spmd(nc, maps, core_ids=list(range(len(maps))))
    return [r["xout"] for r in res.results]


def emit_inproj(C, T, xT_in, w_in, gvec, qg, kg, zT, upT, qT, kT, Vd, tag="i", zoff=0):
    S, A, nc = C.S, C.A, C.nc
    NT = 256
    ntiles = T // NT
    A.mark()
    W = A.alloc([8, DIN], BF16)
    onesf = A.alloc([128], F32)
    blk = A.alloc([128], F32)
    gv = A.alloc([8], F32)
    qgv = A.alloc([1], F32)
    kgv = A.alloc([1], F32)
    epsb = A.alloc([1], F32)
    eps64 = A.alloc([1], F32)
    xt = [A.alloc([8, NT], F32) for _ in range(2)]
    hT = A.alloc([8, NT], BF16)
    sq = [A.alloc([NT], F32) for _ in range(2)]
    rstd = A.alloc([NT], F32)
    sgm = [A.alloc([NT], F32) for _ in range(2)]
    rq = [A.alloc([NT], F32) for _ in range(2)]
    ob = [A.alloc([NT], BF16) for _ in range(4)]
    vb = [A.alloc([512], BF16) for _ in range(2)]
    pu = [bank(C, 0, NT), bank(C, 1, NT), bank(C, 2, NT), bank(C, 3, NT)]
    pv = [bank(C, 4, 512), bank(C, 5, 512)]
    pstat = bank(C, 6, NT)
    pqs = bank(C, 7, NT)
    R = lambda n: f"{tag}.{n}"

    S.op("pool", lambda h: h.memset(onesf, 1.0 / D), writes=[R("onesf")])
    S.op("pool", lambda h: h.memset(blk, 0.0), writes=[R("blk")])
    S.op("pool", lambda h: h.memset(blk[0:64, 0:64], 1.0 / 64), writes=[R("blk")])
    S.op("pool", lambda h: h.memset(blk[64:128, 64:128], 1.0 / 64), writes=[R("blk")])
    S.op("pool", lambda h: h.memset(epsb, EPS), writes=[R("epsb")])
    S.op("pool", lambda h: h.memset(eps64, 64.0 * EPS), writes=[R("eps64")])
    S.dma("pool", R("cst"), lambda h: h.dma_start(out=gv, in_=gvec), writes=[R("gv")])
    S.dma("pool", R("cst"), lambda h: h.dma_start(out=qgv, in_=qg), writes=[R("qgv")])
    S.dma("pool", R("cst"), lambda h: h.dma_start(out=kgv, in_=kg), writes=[R("kgv")])
    xin_v = xT_in.rearrange("c p t -> p c t")

    def load(t):
        b = t % 2
        S.dma(XQ, R(f"xin{b}"), lambda h: h.dma_start(out=xt[b], in_=xin_v[:, :, t * NT:(t + 1) * NT]),
              writes=[R(f"xt{b}")])

    load(0)
    if ntiles > 1:
        load(1)
    for dc in range(8):
        S.dma("pool", R("wld"), (lambda dc: lambda h: h.dma_start(out=W[:, dc, :], in_=w_in[dc * 128:(dc + 1) * 128, :]))(dc),
              writes=[R(f"W{dc}")])

    ocount = [0]

    def out_store(dst_ap, src, srcres):
        k = ocount[0] % 4
        ocount[0] += 1
        return k

    def do_tile(t):
        b = t % 2
        xb, xres = xt[b], R(f"xt{b}")
        tsl = slice(t * NT, (t + 1) * NT)
        for c in range(8):
            k = c % 2
            S.op("act", (lambda c, k: lambda h: h.activation(out=sq[k], in_=xb[:, c, :], func=AF.Square))(c, k),
                 reads=[xres], writes=[R(f"sq{k}")])
            S.op("pe", (lambda c, k: lambda h: h.matmul(pstat, lhsT=onesf, rhs=sq[k], start=(c == 0), stop=(c == 7)))(c, k),
                 reads=[R(f"sq{k}"), R("onesf")], writes=[R("pstat")])
        S.op("act", lambda h: h.activation(out=rstd, in_=pstat, func=AF.Sqrt, bias=epsb[:, 0:1], scale=1.0),
             reads=[R("pstat"), R("epsb")], writes=[R("rstd")])
        S.op("dve", lambda h: h.reciprocal(out=rstd, in_=rstd), reads=[R("rstd")], writes=[R("rstd")])
        for c in range(8):
            S.op("dve", (lambda c: lambda h: h.scalar_tensor_tensor(out=hT[:, c, :], in0=xb[:, c, :], scalar=gv[:, c:c + 1],
                                                                   in1=rstd, op0=ALU.mult, op1=ALU.mult))(c),
                 reads=[xres, R("rstd"), R("gv")], writes=[R("hT")])
        if t + 2 < ntiles:
            pass

        def proj(j, pk):
            for dc in range(8):
                S.op("pe", (lambda dc: lambda h: h.matmul(pu[pk], lhsT=W[:, dc, j * 128:(j + 1) * 128], rhs=hT[:, dc, :],
                                                          start=(dc == 0), stop=(dc == 7)))(dc),
                     reads=[R(f"W{dc}"), R("hT")], writes=[R(f"pu{pk}")])

        def store(dst, k):
            S.dma(XQ, R(f"ost{k}"), lambda h: h.dma_start(out=dst, in_=ob[k]), reads=[R(f"ob{k}")], writes=[R("outs")])

        for j in range(2):
            proj(2 + j, 0)
            proj(j, 1)
            k = ocount[0] % 4
            ocount[0] += 1
            S.op("act", (lambda j: lambda h: h.activation(out=sgm[j], in_=pu[0], func=AF.Sigmoid))(j),
                 reads=[R("pu0")], writes=[R(f"sgm{j}")])
            S.op("dve", (lambda j, k: lambda h: h.tensor_tensor(out=ob[k], in0=sgm[j], in1=pu[1], op=ALU.mult))(j, k),
                 reads=[R(f"sgm{j}"), R("pu1")], writes=[R(f"ob{k}")])
            store(zT[j, :, t * NT + zoff:(t + 1) * NT + zoff], k)
        for j in range(2):
            pk = 2 + j
            proj(4 + j, pk)
            k = ocount[0] % 4
            ocount[0] += 1
            S.op("dve", (lambda pk, k: lambda h: h.tensor_copy(out=ob[k], in_=pu[pk]))(pk, k),
                 reads=[R(f"pu{pk}")], writes=[R(f"ob{k}")])
            store(upT[j, :, t * NT + zoff:(t + 1) * NT + zoff], k)
        for j in range(8):
            pk = j % 4
            r = j % 2
            isq = j < 4
            proj(6 + j, pk)
            k = ocount[0] % 4
            ocount[0] += 1
            S.op("act", (lambda pk, r: lambda h: h.activation(out=sq[r], in_=pu[pk], func=AF.Square))(pk, r),
                 reads=[R(f"pu{pk}")], writes=[R(f"sq{r}")])
            S.op("pe", (lambda r: lambda h: h.matmul(pqs, lhsT=blk, rhs=sq[r], start=True, stop=True))(r),
                 reads=[R(f"sq{r}"), R("blk")], writes=[R("pqs")])
            if isq:
                S.op("act", (lambda r: lambda h: h.activation(out=rq[r], in_=pqs, func=AF.Sqrt, bias=eps64[:, 0:1], scale=64.0))(r),
                     reads=[R("pqs"), R("eps64")], writes=[R(f"rq{r}")])
            else:
                S.op("act", (lambda r: lambda h: h.activation(out=rq[r], in_=pqs, func=AF.Sqrt, bias=epsb[:, 0:1], scale=1.0))(r),
                     reads=[R("pqs"), R("epsb")], writes=[R(f"rq{r}")])
            S.op("dve", (lambda r: lambda h: h.reciprocal(out=rq[r], in_=rq[r]))(r), reads=[R(f"rq{r}")], writes=[R(f"rq{r}")])
            gvv = qgv if isq else kgv
            S.op("dve", (lambda pk, r, k, gvv: lambda h: h.scalar_tensor_tensor(out=ob[k], in0=pu[pk], scalar=gvv[:, 0:1], in1=rq[r],
                                                                               op0=ALU.mult, op1=ALU.mult))(pk, r, k, gvv),
                 reads=[R(f"pu{pk}"), R(f"rq{r}"), R("qgv"), R("kgv")], writes=[R(f"ob{k}")])
            dst = (qT if isq else kT)[j % 4, :, tsl]
            store(dst, k)
        def vpart(s):
            k = s % 2
            for dc in range(8):
                S.op("pe", (lambda dc: lambda h: h.matmul(pv[k], lhsT=hT[:, dc, s * 128:(s + 1) * 128], rhs=W[:, dc, 1792:2304],
                                                          start=(dc == 0), stop=(dc == 7)))(dc),
                     reads=[R(f"W{dc}"), R("hT")], writes=[R(f"pv{k}")])
            S.op("act", lambda h: h.activation(out=vb[k], in_=pv[k], func=AF.Copy), reads=[R(f"pv{k}")], writes=[R(f"vb{k}")])
            cidx = t * (NT // 128) + s
            S.dma(XQ, R(f"vst{k}"), lambda h: h.dma_start(out=Vd[:, :, cidx, :].rearrange("h p e -> p h e"),
                                                          in_=vb[k].rearrange("p (h e) -> p h e", h=4)),
                  reads=[R(f"vb{k}")], writes=[R("outs")])

        for s in range(NT // 128):
            vpart(s)
        if t + 2 < ntiles:
            load(t + 2)

    for t in range(ntiles):
        do_tile(t)
    A.release()


def build_inproj_prog(T):
    key = ("inproj", T)
    if key in _prog_cache:
        return _prog_cache[key]
    nc = bass.Bass("TRN2", target_bir_lowering=False)
    xin = nc.dram_tensor("xin", [8, 128, T], F32, kind="ExternalInput").ap()
    w_in = nc.dram_tensor("w_in", [D, DIN], F32, kind="ExternalInput").ap()
    gvec = nc.dram_tensor("gvec", [128, 8], F32, kind="ExternalInput").ap()
    qg = nc.dram_tensor("qg", [128, 1], F32, kind="ExternalInput").ap()
    kg = nc.dram_tensor("kg", [128, 1], F32, kind="ExternalInput").ap()
    zT = nc.dram_tensor("zT", [2, 128, T], BF16, kind="ExternalOutput").ap()
    upT = nc.dram_tensor("upT", [2, 128, T], BF16, kind="ExternalOutput").ap()
    qT = nc.dram_tensor("qT", [4, 128, T], BF16, kind="ExternalOutput").ap()
    kT = nc.dram_tensor("kT", [4, 128, T], BF16, kind="ExternalOutput").ap()
    Vd = nc.dram_tensor("Vd", [4, 128, T // 128, 128], BF16, kind="ExternalOutput").ap()
    C = make_ctx(nc)
    emit_inproj(C, T, xin, w_in, gvec, qg, kg, zT, upT, qT, kT, Vd)
    C.S.op("pool", lambda h: None, reads=["i.outs"])
    close_ctx(C)
    _prog_cache[key] = nc
    return nc


def emit_attn(C, T, Sq, qT, kTf, Vf, ramp, kposc, lamv, subg, lamc, yT, tag="a", gathered=False):
    S, A, nc = C.S, C.A, C.nc
    NQ = 512
    nqt = T // NQ
    nkc = Sq // 128
    A.mark()
    Kt = A.alloc([Sq], BF16)
    Vt = A.alloc([nkc, 128], BF16)
    kpc = A.alloc([nkc], F32)
    negk = A.alloc([nkc], F32)
    onesb = A.alloc([128], BF16)
    ones128 = A.alloc([128], F32)
    epsb = A.alloc([1], F32)
    lamt = A.alloc([4, 64], F32)
    lprod = A.alloc([2, 64], F32)
    lsum = A.alloc([2], F32)
    neglam = A.alloc([1], F32)
    sgv = A.alloc([1], F32)
    lct = A.alloc([2], F32)
    Qt = [A.alloc([NQ], BF16) for _ in range(2)]
    NF = Sq + T
    FP = 512
    Ft = A.alloc([NF], F32)
    rt = [A.alloc([FP], F32) for _ in range(2)]
    tmp = [A.alloc([2 * NQ], F32) for _ in range(2)]
    Pt = [A.alloc([2 * NQ], BF16) for _ in range(2)]
    r1 = A.alloc([NQ], F32)
    o1 = A.alloc([NQ], F32)
    o2 = A.alloc([NQ], F32)
    sqo = A.alloc([NQ], F32)
    rs = A.alloc([NQ], F32)
    yb = [A.alloc([NQ], BF16) for _ in range(2)]
    R = lambda n: f"{tag}.{n}"
    psS = [C.ps[:, 0:1024], C.ps[:, 1024:2048]]
    acc = [bank(C, 4), bank(C, 5)]
    den = [bank(C, 6), bank(C, 7)]

    S.op("pool", lambda h: h.memset(onesb, 1.0), writes=[R("onesb")])
    S.op("pool", lambda h: h.memset(ones128, 1.0 / 128), writes=[R("ones128")])
    S.op("pool", lambda h: h.memset(epsb, EPS), writes=[R("epsb")])
    S.dma("pool", R("cst"), lambda h: h.dma_start(out=kpc, in_=kposc), writes=[R("kpc")])
    S.dma("pool", R("cst"), lambda h: h.dma_start(out=lamt, in_=lamv), writes=[R("lamt")])
    S.dma("pool", R("cst"), lambda h: h.dma_start(out=sgv, in_=subg), writes=[R("sgv")])
    S.dma("pool", R("cst"), lambda h: h.dma_start(out=lct, in_=lamc), writes=[R("lct")])
    S.op("dve", lambda h: h.tensor_tensor(out=lprod[:, 0, :], in0=lamt[:, 0, :], in1=lamt[:, 1, :], op=ALU.mult),
         reads=[R("lamt")], writes=[R("lprod")])
    S.op("dve", lambda h: h.tensor_tensor(out=lprod[:, 1, :], in0=lamt[:, 2, :], in1=lamt[:, 3, :], op=ALU.mult),
         reads=[R("lamt")], writes=[R("lprod")])
    S.op("dve", lambda h: h.reduce_sum(out=lsum, in_=lprod, axis=mybir.AxisListType.X), reads=[R("lprod")], writes=[R("lsum")])
    S.op("act", lambda h: h.activation(out=lsum, in_=lsum, func=AF.Exp), reads=[R("lsum")], writes=[R("lsum")])
    S.op("dve", lambda h: h.tensor_tensor(out=neglam, in0=lsum[:, 1:2], in1=lsum[:, 0:1], op=ALU.subtract),
         reads=[R("lsum")], writes=[R("neglam")])
    S.op("dve", lambda h: h.tensor_scalar(out=neglam, in0=neglam, scalar1=lct[:, 0:1], scalar2=None, op0=ALU.add),
         reads=[R("neglam"), R("lct")], writes=[R("neglam")])
    S.op("dve", lambda h: h.tensor_scalar(out=sgv, in0=sgv, scalar1=lct[:, 1:2], scalar2=None, op0=ALU.mult),
         reads=[R("sgv"), R("lct")], writes=[R("sgv")])

    cnt = [0]

    def head(hd):
        slope = 2.0 ** (-8.0 * (hd + 1) / 4)
        if gathered:
            S.dma("pool", R("kld"), lambda h: h.dma_start(out=Kt.rearrange("p (r t) -> p r t", r=4),
                                                          in_=kTf[hd].rearrange("r p t -> p r t")), writes=[R("Kt")])
            S.dma("pool", R("vld"), lambda h: h.dma_start(out=Vt.rearrange("p (r c) e -> p r (c e)", r=4),
                                                          in_=Vf[hd].rearrange("r p c e -> p r (c e)")), writes=[R("Vt")])
        else:
            S.dma("pool", R("kld"), lambda h: h.dma_start(out=Kt, in_=kTf[hd]), writes=[R("Kt")])
            S.dma("pool", R("vld"), lambda h: h.dma_start(out=Vt, in_=Vf[hd]), writes=[R("Vt")])
        S.op("dve", lambda h: h.tensor_scalar(out=negk, in0=kpc, scalar1=-slope, scalar2=None, op0=ALU.mult),
             reads=[R("kpc")], writes=[R("negk")])

        def fpiece(pi):
            b = pi % 2
            sl_ = slice(pi * FP, (pi + 1) * FP)
            S.dma("pool", R(f"rld{b}"), lambda h: h.dma_start(out=rt[b], in_=ramp[:, sl_]), writes=[R(f"rt{b}")])
            S.op("act", lambda h: h.activation(out=Ft[:, sl_], in_=rt[b], func=AF.Abs, bias=negk[:, 0:1], scale=slope),
                 reads=[R(f"rt{b}"), R("negk")], writes=[R("Ft")])

        for pi in range(NF // FP):
            fpiece(pi)

        def qtile(qt):
            qb = (hd * nqt + qt) % 2
            qs = slice(qt * NQ, (qt + 1) * NQ)
            S.dma("pool", R(f"qld{qb}"), lambda h: h.dma_start(out=Qt[qb], in_=qT[hd, :, qs]), writes=[R(f"Qt{qb}")])

            def score(kc):
                i = kc % 2
                ks = slice(kc * 128, (kc + 1) * 128)
                S.op("pe", lambda h: h.matmul(psS[i][:, 0:NQ], lhsT=Kt[0:64, ks], rhs=Qt[qb][0:64, :], start=True, stop=True),
                     reads=[R("Kt"), R(f"Qt{qb}")], writes=[R(f"psS{i}a")])
                S.op("pe", lambda h: h.matmul(psS[i][:, NQ:2 * NQ], lhsT=Kt[64:128, ks], rhs=Qt[qb][64:128, :], start=True, stop=True),
                     reads=[R("Kt"), R(f"Qt{qb}")], writes=[R(f"psS{i}b")])
                fo = Sq + qt * NQ - kc * 128
                for m, ab in ((0, "a"), (1, "b")):
                    S.op("dve", (lambda m: lambda h: h.tensor_tensor(out=tmp[i][:, m * NQ:(m + 1) * NQ], in0=psS[i][:, m * NQ:(m + 1) * NQ],
                                                                     in1=Ft[:, fo:fo + NQ], op=ALU.subtract))(m),
                         reads=[R("Ft"), R(f"psS{i}{ab}")], writes=[R(f"tmp{i}{ab}")])
                    S.op("act", (lambda m: lambda h: h.activation(out=Pt[i][:, m * NQ:(m + 1) * NQ], in_=tmp[i][:, m * NQ:(m + 1) * NQ],
                                                                  func=AF.Exp))(m),
                         reads=[R(f"tmp{i}{ab}")], writes=[R(f"Pt{i}{ab}")])

            def accum(kc):
                i = kc % 2
                first, last = (kc == 0), (kc == nkc - 1)
                for m, ab in ((0, "a"), (1, "b")):
                    S.op("pe", (lambda m: lambda h: h.matmul(acc[m], lhsT=Vt[:, kc, :], rhs=Pt[i][:, m * NQ:(m + 1) * NQ],
                                                             start=first, stop=last))(m),
                         reads=[R("Vt"), R(f"Pt{i}{ab}")], writes=[R(f"acc{m}")])
                    S.op("pe", (lambda m: lambda h: h.matmul(den[m], lhsT=onesb, rhs=Pt[i][:, m * NQ:(m + 1) * NQ],
                                                             start=first, stop=last))(m),
                         reads=[R("onesb"), R(f"Pt{i}{ab}")], writes=[R(f"den{m}")])

            score(0)
            for kc in range(nkc):
                if kc + 1 < nkc:
                    score(kc + 1)
                accum(kc)
            S.op("dve", lambda h: h.reciprocal(out=r1, in_=den[0]), reads=[R("den0")], writes=[R("r1")])
            S.op("dve", lambda h: h.tensor_tensor(out=o1, in0=acc[0], in1=r1, op=ALU.mult), reads=[R("acc0"), R("r1")], writes=[R("o1")])
            S.op("dve", lambda h: h.reciprocal(out=r1, in_=den[1]), reads=[R("den1"), R("o1")], writes=[R("r1")])
            S.op("dve", lambda h: h.tensor_tensor(out=o2, in0=acc[1], in1=r1, op=ALU.mult), reads=[R("acc1"), R("r1")], writes=[R("o2")])
            S.op("dve", lambda h: h.scalar_tensor_tensor(out=o1, in0=o2, scalar=neglam[:, 0:1], in1=o1, op0=ALU.mult, op1=ALU.add),
                 reads=[R("o2"), R("o1"), R("neglam")], writes=[R("o1")])
            S.op("act", lambda h: h.activation(out=sqo, in_=o1, func=AF.Square), reads=[R("o1")], writes=[R("sqo")])
            S.op("pe", lambda h: h.matmul(den[0], lhsT=ones128, rhs=sqo, start=True, stop=True),
                 reads=[R("ones128"), R("sqo")], writes=[R("den0")])
            S.op("act", lambda h: h.activation(out=rs, in_=den[0], func=AF.Sqrt, bias=epsb[:, 0:1], scale=1.0),
                 reads=[R("den0"), R("epsb")], writes=[R("rs")])
            S.op("dve", lambda h: h.reciprocal(out=rs, in_=rs), reads=[R("rs")], writes=[R("rs")])
            S.op("dve", lambda h: h.scalar_tensor_tensor(out=yb[qb], in0=o1, scalar=sgv[:, 0:1], in1=rs, op0=ALU.mult, op1=ALU.mult),
                 reads=[R("o1"), R("rs"), R("sgv")], writes=[R(f"yb{qb}")])
            S.dma("pool", R(f"yst{qb}"), lambda h: h.dma_start(out=yT[hd, :, qs], in_=yb[qb]), reads=[R(f"yb{qb}")], writes=[R("outs")])

        for qt in range(nqt):
            qtile(qt)

    for hd in range(4):
        head(hd)
    A.release()


def build_attn_prog(T, Sq):
    key = ("attn", T, Sq)
    if key in _prog_cache:
        return _prog_cache[key]
    nc = bass.Bass("TRN2", target_bir_lowering=False)
    qT = nc.dram_tensor("qT", [4, 128, T], BF16, kind="ExternalInput").ap()
    kTf = nc.dram_tensor("kTf", [4, 128, Sq], BF16, kind="ExternalInput").ap()
    Vf = nc.dram_tensor("Vf", [4, 128, Sq // 128, 128], BF16, kind="ExternalInput").ap()
    qpos = nc.dram_tensor("ramp", [128, Sq + T], F32, kind="ExternalInput").ap()
    kposc = nc.dram_tensor("kposc", [128, Sq // 128], F32, kind="ExternalInput").ap()
    lamv = nc.dram_tensor("lamv", [128, 4, 64], F32, kind="ExternalInput").ap()
    subg = nc.dram_tensor("subg", [128, 1], F32, kind="ExternalInput").ap()
    lamc = nc.dram_tensor("lamc", [128, 2], F32, kind="ExternalInput").ap()
    yT = nc.dram_tensor("yT", [4, 128, T], BF16, kind="ExternalOutput").ap()
    C = make_ctx(nc)
    emit_attn(C, T, Sq, qT, kTf, Vf, qpos, kposc, lamv, subg, lamc, yT)
    C.S.op("pool", lambda h: None, reads=["a.outs"])
    close_ctx(C)
    _prog_cache[key] = nc
    return nc


HALO = 16


def emit_mix(C, T, x1T, zTh, upTh, yaT, w_out, convw, cvec, pool_w, invcnt, ident, x2T, tag="m"):
    S, A, nc = C.S, C.A, C.nc
    NT = 256
    ntiles = T // NT
    A.mark()
    Wo = A.alloc([8, D], BF16)
    idt = A.alloc([128], F32)
    cw = A.alloc([2, 31], F32)
    cv = A.alloc([8], F32)
    Dg = A.alloc([2, 31, 128], BF16)
    pst = A.alloc([2, 128], F32)
    PWf = A.alloc([2, 128], BF16)
    PWh = A.alloc([2, 128], BF16)
    ones256 = A.alloc([128], F32)
    epsb = A.alloc([1], F32)
    xt = [A.alloc([8, NT], F32) for _ in range(2)]
    zt = [A.alloc([2, NT + 2 * HALO], BF16) for _ in range(2)]
    ut = [A.alloc([2, NT + 2 * HALO], BF16) for _ in range(2)]
    ic = [A.alloc([2, NT], F32) for _ in range(2)]
    ycat = [A.alloc([8, NT], BF16) for _ in range(2)]
    cz = A.alloc([2, NT], F32)
    sqc = A.alloc([2, NT], F32)
    mean = A.alloc([NT], F32)
    m2 = A.alloc([NT], F32)
    rstd = A.alloc([NT], F32)
    t1 = A.alloc([2, NT], F32)
    pa = A.alloc([NT], F32)
    R = lambda n: f"{tag}.{n}"
    pc = bank(C, 0, NT)
    pm = bank(C, 1, NT)
    pvv = bank(C, 2, NT)
    pA = bank(C, 3, NT)
    pB = bank(C, 4, NT)
    po = [bank(C, 5, NT), bank(C, 6, NT)]

    S.op("pool", lambda h: h.memset(ones256, 1.0 / 256), writes=[R("ones256")])
    S.op("pool", lambda h: h.memset(epsb, EPS), writes=[R("epsb")])
    S.op("pool", lambda h: h.memset(pst, 0.0), writes=[R("pst")])
    S.dma("pool", R("cst"), lambda h: h.dma_start(out=idt, in_=ident), writes=[R("idt")])
    S.dma("pool", R("cst"), lambda h: h.dma_start(out=cw, in_=convw.rearrange("c p k -> p c k")), writes=[R("cw")])
    S.dma("pool", R("cst"), lambda h: h.dma_start(out=cv, in_=cvec), writes=[R("cv")])
    for g in range(4):
        c, r = g // 2, g % 2
        S.dma("pool", R("cst"), (lambda g, c, r: lambda h: h.dma_start(out=pst[64 * r:64 * r + 64, c, 64 * r:64 * r + 64], in_=pool_w[g]))(g, c, r),
              reads=[R("pst")], writes=[R(f"pst{g}")])
    for c in range(2):
        S.op("dve", (lambda c: lambda h: h.tensor_copy(out=PWf[:, c, :], in_=pst[:, c, :]))(c),
             reads=[R("pst0"), R("pst1"), R("pst2"), R("pst3")], writes=[R("PWf")])
        S.op("dve", (lambda c: lambda h: h.tensor_copy(out=PWh[:, c, :], in_=pst[:, c, :]))(c),
             reads=[R("pst0"), R("pst1"), R("pst2"), R("pst3")], writes=[R("PWh")])
        S.op("dve", (lambda c: lambda h: h.memset(PWh[0:64, c, :], 0.0))(c), reads=[R("PWh")], writes=[R("PWh")])
        for k in range(31):
            S.op("dve", (lambda c, k: lambda h: h.tensor_scalar(out=Dg[:, c, k, :], in0=idt, scalar1=cw[:, c, k:k + 1], scalar2=None,
                                                               op0=ALU.mult))(c, k),
                 reads=[R("idt"), R("cw")], writes=[R("Dg")])
    for dc in range(8):
        S.dma("pool", R("wld"), (lambda dc: lambda h: h.dma_start(out=Wo[:, dc, :], in_=w_out[dc * 128:(dc + 1) * 128, :]))(dc),
              writes=[R(f"Wo{dc}")])
    x1v = x1T.rearrange("c p t -> p c t")
    x2v = x2T.rearrange("c p t -> p c t")
    zv = zTh.rearrange("c p t -> p c t")
    uv = upTh.rearrange("c p t -> p c t")
    yav = yaT.rearrange("c p t -> p c t")
    icv = invcnt.rearrange("c p t -> p c t")

    def load(t):
        b = t % 2
        ts_ = slice(t * NT, (t + 1) * NT)
        th = slice(t * NT, (t + 1) * NT + 2 * HALO)
        S.dma(XQ, R(f"ld{b}"), lambda h: h.dma_start(out=xt[b], in_=x1v[:, :, ts_]), writes=[R(f"xt{b}")])
        S.dma(XQ, R(f"ld{b}"), lambda h: h.dma_start(out=zt[b], in_=zv[:, :, th]), writes=[R(f"zt{b}")], newgroup=False)
        S.dma(XQ, R(f"ld{b}"), lambda h: h.dma_start(out=ut[b], in_=uv[:, :, th]), writes=[R(f"ut{b}")], newgroup=False)
        S.dma(XQ, R(f"ld{b}"), lambda h: h.dma_start(out=ic[b], in_=icv[:, :, ts_]), writes=[R(f"ic{b}")], newgroup=False)
        S.dma(XQ, R(f"ld{b}"), lambda h: h.dma_start(out=ycat[b][:, 4:8, :], in_=yav[:, :, ts_]), writes=[R(f"ya{b}")], newgroup=False)

    def do_tile(t):
        b = t % 2
        ts_ = slice(t * NT, (t + 1) * NT)
        for c in range(2):
            for k in range(31):
                S.op("pe", (lambda c, k: lambda h: h.matmul(pc, lhsT=Dg[:, c, k, :], rhs=zt[b][:, c, k + 1:k + 1 + NT],
                                                            start=(k == 0), stop=(k == 30)))(c, k),
                     reads=[R("Dg"), R(f"zt{b}")], writes=[R("pc")])
            S.op("act", (lambda c: lambda h: h.activation(out=cz[:, c, :], in_=pc, func=AF.Identity, bias=cv[:, c:c + 1], scale=1.0))(c),
                 reads=[R("pc"), R("cv")], writes=[R(f"cz{c}")])
            S.op("act", (lambda c: lambda h: h.activation(out=sqc[:, c, :], in_=cz[:, c, :], func=AF.Square))(c),
                 reads=[R(f"cz{c}")], writes=[R(f"sqc{c}")])
        for c in range(2):
            S.op("pe", (lambda c: lambda h: h.matmul(pm, lhsT=ones256, rhs=cz[:, c, :], start=(c == 0), stop=(c == 1)))(c),
                 reads=[R("ones256"), R(f"cz{c}")], writes=[R("pm")])
        for c in range(2):
            S.op("pe", (lambda c: lambda h: h.matmul(pvv, lhsT=ones256, rhs=sqc[:, c, :], start=(c == 0), stop=(c == 1)))(c),
                 reads=[R("ones256"), R(f"sqc{c}")], writes=[R("pvv")])
        S.op("dve", lambda h: h.tensor_copy(out=mean, in_=pm), reads=[R("pm")], writes=[R("mean")])
        S.op("dve", lambda h: h.tensor_tensor(out=m2, in0=mean, in1=mean, op=ALU.mult), reads=[R("mean")], writes=[R("m2")])
        S.op("dve", lambda h: h.tensor_tensor(out=m2, in0=pvv, in1=m2, op=ALU.subtract), reads=[R("pvv"), R("m2")], writes=[R("m2")])
        S.op("act", lambda h: h.activation(out=rstd, in_=m2, func=AF.Sqrt, bias=epsb[:, 0:1], scale=1.0),
             reads=[R("m2"), R("epsb")], writes=[R("rstd")])
        S.op("dve", lambda h: h.reciprocal(out=rstd, in_=rstd), reads=[R("rstd")], writes=[R("rstd")])
        for c in range(2):
            S.op("dve", (lambda c: lambda h: h.tensor_tensor(out=t1[:, c, :], in0=cz[:, c, :], in1=mean, op=ALU.subtract))(c),
                 reads=[R(f"cz{c}"), R("mean")], writes=[R(f"t1{c}")])
            S.op("dve", (lambda c: lambda h: h.scalar_tensor_tensor(out=t1[:, c, :], in0=t1[:, c, :], scalar=cv[:, 2 + c:3 + c], in1=rstd,
                                                                   op0=ALU.mult, op1=ALU.mult))(c),
                 reads=[R(f"t1{c}"), R("rstd"), R("cv")], writes=[R(f"t1{c}")])
            S.op("act", (lambda c: lambda h: h.activation(out=ycat[b][:, c, :], in_=t1[:, c, :], func=AF.Silu, bias=cv[:, 4 + c:5 + c], scale=1.0))(c),
                 reads=[R(f"t1{c}"), R("cv")], writes=[R(f"yc{b}")])
        for c in range(2):
            taps = list(range(-2, 2)) if c == 0 else list(range(-8, 8))
            narrow = (1, ) if c == 0 else (4, )
            for i, tau in enumerate(taps):
                full = (-narrow[0] <= tau <= narrow[0] - 1)
                Wm = PWf if full else PWh
                S.op("pe", (lambda c, tau, Wm, i, n: lambda h: h.matmul(pA, lhsT=Wm[:, c, :], rhs=ut[b][:, c, HALO + tau:HALO + tau + NT],
                                                                        start=(i == 0), stop=(i == n - 1)))(c, tau, Wm, i, len(taps)),
                     reads=[R("PWf"), R("PWh"), R(f"ut{b}")], writes=[R("pA")])
            S.op("pe", (lambda c: lambda h: h.matmul(pB, lhsT=PWf[:, c, :], rhs=ut[b][:, c, HALO:HALO + NT], start=True, stop=True))(c),
                 reads=[R("PWf"), R(f"ut{b}")], writes=[R("pB")])
            S.op("dve", (lambda c: lambda h: h.tensor_tensor(out=pa, in0=pA, in1=ic[b][:, c, :], op=ALU.mult))(c),
                 reads=[R("pA"), R(f"ic{b}")], writes=[R("pa")])
            S.op("dve", (lambda c: lambda h: h.tensor_tensor(out=pa, in0=pa, in1=pB, op=ALU.subtract))(c),
                 reads=[R("pa"), R("pB")], writes=[R("pa")])
            S.op("dve", (lambda c: lambda h: h.tensor_scalar(out=ycat[b][:, 2 + c, :], in0=pa, scalar1=cv[:, 6 + c:7 + c], scalar2=None,
                                                            op0=ALU.mult))(c),
                 reads=[R("pa"), R("cv")], writes=[R(f"yc{b}")])
        for oc in range(8):
            k = oc % 2
            for c in range(8):
                S.op("pe", (lambda oc, c, k: lambda h: h.matmul(po[k], lhsT=Wo[:, c, oc * 128:(oc + 1) * 128], rhs=ycat[b][:, c, :],
                                                                start=(c == 0), stop=(c == 7)))(oc, c, k),
                     reads=[R(f"Wo{c}"), R(f"yc{b}"), R(f"ya{b}")], writes=[R(f"po{k}")])
            S.op("dve", (lambda oc, k: lambda h: h.tensor_tensor(out=xt[b][:, oc, :], in0=po[k], in1=xt[b][:, oc, :], op=ALU.add))(oc, k),
                 reads=[R(f"po{k}"), R(f"xt{b}")], writes=[R(f"xt{b}")])
        S.dma(XQ, R(f"st{b}"), lambda h: h.dma_start(out=x2v[:, :, ts_], in_=xt[b]), reads=[R(f"xt{b}")], writes=[R("outs")])
        if t + 2 < ntiles:
            load(t + 2)

    load(0)
    if ntiles > 1:
        load(1)
    for t in range(ntiles):
        do_tile(t)
    A.release()


def build_mix_prog(T):
    key = ("mix", T)
    if key in _prog_cache:
        return _prog_cache[key]
    nc = bass.Bass("TRN2", target_bir_lowering=False)
    x1T = nc.dram_tensor("x1T", [8, 128, T], F32, kind="ExternalInput").ap()
    zTh = nc.dram_tensor("zTh", [2, 128, T + 2 * HALO], BF16, kind="ExternalInput").ap()
    upTh = nc.dram_tensor("upTh", [2, 128, T + 2 * HALO], BF16, kind="ExternalInput").ap()
    yaT = nc.dram_tensor("yaT", [4, 128, T], BF16, kind="ExternalInput").ap()
    w_out = nc.dram_tensor("w_out", [D, D], F32, kind="ExternalInput").ap()
    convw = nc.dram_tensor("convw", [2, 128, 31], F32, kind="ExternalInput").ap()
    cvec = nc.dram_tensor("cvec", [128, 8], F32, kind="ExternalInput").ap()
    pool_w = nc.dram_tensor("pool_w", [4, 64, 64], F32, kind="ExternalInput").ap()
    invcnt = nc.dram_tensor("invcnt", [2, 128, T], F32, kind="ExternalInput").ap()
    ident = nc.dram_tensor("ident", [128, 128], F32, kind="ExternalInput").ap()
    x2T = nc.dram_tensor("x2T", [8, 128, T], F32, kind="ExternalOutput").ap()
    C = make_ctx(nc)
    emit_mix(C, T, x1T, zTh, upTh, yaT, w_out, convw, cvec, pool_w, invcnt, ident, x2T)
    C.S.op("pool", lambda h: None, reads=["m.outs"])
    close_ctx(C)
    _prog_cache[key] = nc
    return nc


def alibi_ramp(r, T, Sq):
    return np.ascontiguousarray(np.broadcast_to((np.arange(Sq + T, dtype=np.float32) - Sq + r * T)[None], (128, Sq + T)))


def pool_invcnt(pos0, T, Sq):
    t = np.arange(pos0, pos0 + T)
    out = np.zeros((2, 128, T), np.float32)
    for g, w in enumerate((2, 4, 8, 16)):
        lo = np.clip(t - w // 2, 0, Sq)
        hi = np.clip(t + w // 2, 0, Sq)
        out[g // 2, (g % 2) * 64:(g % 2) * 64 + 64, :] = (1.0 / (hi - lo).astype(np.float32))[None, :]
    return out


def _launch(nc, maps):
    res = run_bass_kernel_spmd(nc, maps, core_ids=list(range(len(maps))))
    return res.results


def kernel_unfused(x, ffn1_norm, ffn1_w_gate, ffn1_w_up, ffn1_w_down, mix_norm, w_in,
                   conv_dw, conv_dw_bias, conv_ln_gain, conv_ln_bias, pool_w, pool_scale,
                   q_norm, k_norm, lambda_q1, lambda_k1, lambda_q2, lambda_k2, attn_subln,
                   w_out, ffn2_norm, ffn2_w_gate, ffn2_w_up, ffn2_w_down, post_norm, _launch=_launch):
    import math
    f32 = np.float32
    x = np.asarray(x, f32)
    B, Sq, _ = x.shape
    T = Sq // 4
    ncore = 4 * B
    xT = [to_T(x[c // 4, (c % 4) * T:(c % 4 + 1) * T]) for c in range(ncore)]
    ident = np.eye(128, dtype=f32)
    kposc = (np.arange(Sq // 128, dtype=f32)[None] * 128 + np.arange(128, dtype=f32)[:, None]).copy()
    qpos = [alibi_ramp(c % 4, T, Sq) for c in range(ncore)]
    invc = [pool_invcnt((c % 4) * T, T, Sq) for c in range(ncore)]
    A = lambda a: np.ascontiguousarray(np.asarray(a, f32))
    for l in range(2):
        lambda_init = 0.8 - 0.6 * math.exp(-0.3 * l)
        nc = build_ffn_prog(T, False)
        g1 = gain_pc(A(ffn1_norm[l]))
        r = _launch(nc, [{"xin": xT[c], "wg": A(ffn1_w_gate[l]), "wu": A(ffn1_w_up[l]), "wd": A(ffn1_w_down[l]), "gvec": g1}
                         for c in range(ncore)])
        x1T = [r[c]["xout"] for c in range(ncore)]
        nc = build_inproj_prog(T)
        qg = np.tile(A(q_norm[l]), 2)[:, None].copy()
        kg = np.tile(A(k_norm[l]), 2)[:, None].copy()
        r = _launch(nc, [{"xin": x1T[c], "w_in": A(w_in[l]), "gvec": gain_pc(A(mix_norm[l])), "qg": qg, "kg": kg}
                         for c in range(ncore)])
        kTf, Vf, zh, uh = [], [], [], []
        for b in range(B):
            cs = range(4 * b, 4 * b + 4)
            kTf.append(np.concatenate([r[c]["kT"] for c in cs], axis=2))
            Vf.append(np.concatenate([r[c]["Vd"] for c in cs], axis=2))
            zf = np.concatenate([r[c]["zT"] for c in cs], axis=2)
            uf = np.concatenate([r[c]["upT"] for c in cs], axis=2)
            zh.append(np.pad(zf, ((0, 0), (0, 0), (HALO, HALO))))
            uh.append(np.pad(uf, ((0, 0), (0, 0), (HALO, HALO))))
        qTl = [r[c]["qT"] for c in range(ncore)]
        nc = build_attn_prog(T, Sq)
        lamv = np.ascontiguousarray(np.broadcast_to(np.stack([A(lambda_q1[l]), A(lambda_k1[l]), A(lambda_q2[l]), A(lambda_k2[l])])[None],
                                                    (128, 4, 64)))
        lamc = np.ascontiguousarray(np.broadcast_to(np.array([-lambda_init, 1.0 - lambda_init], f32)[None], (128, 2)))
        subg = A(attn_subln[l])[:, None].copy()
        r = _launch(nc, [{"qT": qTl[c], "kTf": kTf[c // 4], "Vf": Vf[c // 4], "ramp": qpos[c], "kposc": kposc, "lamv": lamv,
                          "subg": subg, "lamc": lamc} for c in range(ncore)])
        yaT = [r[c]["yT"] for c in range(ncore)]
        nc = build_mix_prog(T)
        cb, lg, lb, psc = A(conv_dw_bias[l]), A(conv_ln_gain[l]), A(conv_ln_bias[l]), A(pool_scale[l])
        cvec = np.stack([cb[:128], cb[128:], lg[:128], lg[128:], lb[:128], lb[128:], psc[:128], psc[128:]], 1).astype(f32)
        convw = np.ascontiguousarray(A(conv_dw[l]).T.reshape(2, 128, 31))
        maps = []
        for c in range(ncore):
            o = (c % 4) * T
            maps.append({"x1T": x1T[c], "zTh": np.ascontiguousarray(zh[c // 4][:, :, o:o + T + 2 * HALO]),
                         "upTh": np.ascontiguousarray(uh[c // 4][:, :, o:o + T + 2 * HALO]), "yaT": yaT[c], "w_out": A(w_out[l]),
                         "convw": convw, "cvec": cvec, "pool_w": A(pool_w[l]), "invcnt": invc[c], "ident": ident})
        r = _launch(nc, maps)
        x2T = [r[c]["x2T"] for c in range(ncore)]
        nc = build_ffn_prog(T, True)
        r = _launch(nc, [{"xin": x2T[c], "wg": A(ffn2_w_gate[l]), "wu": A(ffn2_w_up[l]), "wd": A(ffn2_w_down[l]),
                          "gvec": gain_pc(A(ffn2_norm[l])), "pgvec": gain_pc(A(post_norm[l]))} for c in range(ncore)])
        xT = [r[c]["xout"] for c in range(ncore)]
    out = np.empty((B, Sq, D), f32)
    for c in range(ncore):
        out[c // 4, (c % 4) * T:(c % 4 + 1) * T] = from_T(xT[c])
    return out


LAYERS_PER_LAUNCH = 2
LAYER_W = ("wg1", "wu1", "wd1", "w_in", "w_out", "wg2", "wu2", "wd2")


def emit_exchange(C, T, kT, Vd, kTg, Vg, zTh, upTh, Ed, Eg, selL, selR, tag="x"):
    S, A, nc = C.S, C.A, C.nc
    R = lambda n: f"{tag}.{n}"
    groups = [[0, 1, 2, 3], [4, 5, 6, 7]]
    A.mark()
    H = HALO
    eg = A.alloc([4, 4, 2 * H], BF16)
    hl = A.alloc([4, H], BF16)
    hr = A.alloc([4, H], BF16)
    sl = A.alloc([4], F32)
    sr = A.alloc([4], F32)
    S.dma("pool", R("cst"), lambda h: h.dma_start(out=sl, in_=selL), writes=[R("sl")])
    S.dma("pool", R("cst"), lambda h: h.dma_start(out=sr, in_=selR), writes=[R("sr")])
    for j, src in enumerate((zTh, upTh)):
        for c in range(2):
            S.dma("pool", R("edge"), (lambda j, c, src: lambda h: h.dma_start(out=Ed[2 * j + c, :, 0:H], in_=src[c, :, H:2 * H]))(j, c, src),
                  writes=[R(f"Ed{j}{c}a")], newgroup=(j == 0 and c == 0))
            S.dma("pool", R("edge"), (lambda j, c, src: lambda h: h.dma_start(out=Ed[2 * j + c, :, H:2 * H], in_=src[c, :, T:T + H]))(j, c, src),
                  writes=[R(f"Ed{j}{c}b")], newgroup=False)
    for hd in range(4):
        S.dma("pool", R("cc"), (lambda hd: lambda h: h.collective_compute("AllGather", ALU.bypass, replica_groups=groups,
                                                                          ins=[kT[hd]], outs=[kTg[hd].rearrange("r p t -> (r p) t")]))(hd),
              writes=[R(f"kTg{hd}")], inc=1)
        S.dma("pool", R("cc"), (lambda hd: lambda h: h.collective_compute("AllGather", ALU.bypass, replica_groups=groups,
                                                                          ins=[Vd[hd].rearrange("p c e -> p (c e)")],
                                                                          outs=[Vg[hd].rearrange("r p c e -> (r p) (c e)")]))(hd),
              writes=[R(f"Vg{hd}")], inc=1)
    S.dma("pool", R("cc"), lambda h: h.collective_compute("AllGather", ALU.bypass, replica_groups=groups,
                                                          ins=[Ed.rearrange("j p e -> (j p) e")], outs=[Eg.rearrange("r j p e -> (r j p) e")]),
          reads=[R(f"Ed{j}{c}{x}") for j in range(2) for c in range(2) for x in "ab"], writes=[R("Eg")], inc=1)
    S.dma("pool", R("egl"), lambda h: h.dma_start(out=eg, in_=Eg.rearrange("r j p e -> p r j e")), reads=[R("Eg")], writes=[R("eg")])
    for r in range(4):
        if r == 0:
            S.op("dve", lambda h: h.tensor_scalar(out=hl, in0=eg[:, 0, :, H:2 * H], scalar1=sl[:, 0:1], scalar2=None, op0=ALU.mult),
                 reads=[R("eg"), R("sl")], writes=[R("hl")])
            S.op("dve", lambda h: h.tensor_scalar(out=hr, in0=eg[:, 0, :, 0:H], scalar1=sr[:, 0:1], scalar2=None, op0=ALU.mult),
                 reads=[R("eg"), R("sr")], writes=[R("hr")])
        else:
            S.op("dve", (lambda r: lambda h: h.scalar_tensor_tensor(out=hl, in0=eg[:, r, :, H:2 * H], scalar=sl[:, r:r + 1], in1=hl,
                                                                   op0=ALU.mult, op1=ALU.add))(r),
                 reads=[R("eg"), R("sl"), R("hl")], writes=[R("hl")])
            S.op("dve", (lambda r: lambda h: h.scalar_tensor_tensor(out=hr, in0=eg[:, r, :, 0:H], scalar=sr[:, r:r + 1], in1=hr,
                                                                   op0=ALU.mult, op1=ALU.add))(r),
                 reads=[R("eg"), R("sr"), R("hr")], writes=[R("hr")])
    for j, dst in enumerate((zTh, upTh)):
        S.dma("pool", R("hst"), (lambda j, dst: lambda h: h.dma_start(out=dst[:, :, 0:H].rearrange("c p e -> p c e"),
                                                                    in_=hl[:, 2 * j:2 * j + 2, :]))(j, dst),
              reads=[R("hl")], writes=[R(f"haloL{j}")], newgroup=(j == 0))
        S.dma("pool", R("hst"), (lambda j, dst: lambda h: h.dma_start(out=dst[:, :, T + H:T + 2 * H].rearrange("c p e -> p c e"),
                                                                    in_=hr[:, 2 * j:2 * j + 2, :]))(j, dst),
              reads=[R("hr")], writes=[R(f"haloR{j}")], newgroup=False)
    A.release()


def build_fused_prog(T, Sq, nlayers=2):
    key = ("fused", T, Sq, nlayers)
    if key in _prog_cache:
        return _prog_cache[key]
    nc = bass.Bass("TRN2", target_bir_lowering=False)
    EI = lambda name, shape, dt=F32: nc.dram_tensor(name, shape, dt, kind="ExternalInput").ap()
    IN = lambda name, shape, dt: nc.dram_tensor(name, shape, dt).ap()
    xin = EI("xin", [8, 128, T])
    wshape = {"wg1": [D, DFF], "wu1": [D, DFF], "wd1": [DFF, D], "w_in": [D, DIN], "w_out": [D, D],
              "wg2": [D, DFF], "wu2": [D, DFF], "wd2": [DFF, D]}
    Wt = [{k: EI(f"{k}_{l}", wshape[k]) for k in LAYER_W} for l in range(nlayers)]
    P = []
    for l in range(nlayers):
        P.append({"g1": EI(f"g1_{l}", [128, 8]), "gm": EI(f"gm_{l}", [128, 8]), "g2": EI(f"g2_{l}", [128, 8]),
                  "gp": EI(f"gp_{l}", [128, 8]), "qg": EI(f"qg_{l}", [128, 1]), "kg": EI(f"kg_{l}", [128, 1]),
                  "lamv": EI(f"lamv_{l}", [128, 4, 64]), "subg": EI(f"subg_{l}", [128, 1]), "lamc": EI(f"lamc_{l}", [128, 2]),
                  "convw": EI(f"convw_{l}", [2, 128, 31]), "cvec": EI(f"cvec_{l}", [128, 8]), "pool_w": EI(f"poolw_{l}", [4, 64, 64])})
    qpos = EI("ramp", [128, Sq + T])
    kposc = EI("kposc", [128, Sq // 128])
    invcnt = EI("invcnt", [2, 128, T])
    ident = EI("ident", [128, 128])
    selL = EI("selL", [128, 4])
    selR = EI("selR", [128, 4])
    xout = nc.dram_tensor("xout", [8, 128, T], F32, kind="ExternalOutput").ap()
    xa = IN("xa", [8, 128, T], F32)
    xb = IN("xb", [8, 128, T], F32)
    xc = IN("xc", [8, 128, T], F32)
    zTh = IN("zTh", [2, 128, T + 2 * HALO], BF16)
    upTh = IN("upTh", [2, 128, T + 2 * HALO], BF16)
    qT = IN("qT", [4, 128, T], BF16)
    kT = IN("kT", [4, 128, T], BF16)
    Vd = IN("Vd", [4, 128, T // 128, 128], BF16)
    kTg = IN("kTg", [4, 4, 128, T], BF16)
    Vg = IN("Vg", [4, 4, 128, T // 128, 128], BF16)
    Ed = IN("Ed", [4, 128, 2 * HALO], BF16)
    Eg = IN("Eg", [4, 4, 128, 2 * HALO], BF16)
    yaT = IN("yaT", [4, 128, T], BF16)
    C = make_ctx(nc)
    S = C.S
    cur = xin
    for l in range(nlayers):
        w, p = Wt[l], P[l]
        emit_ffn(C, T, cur, xa, w["wg1"], w["wu1"], w["wd1"], p["g1"], None, tag="f")
        S.barrier()
        emit_inproj(C, T, xa, w["w_in"], p["gm"], p["qg"], p["kg"], zTh, upTh, qT, kT, Vd, tag="i", zoff=HALO)
        S.barrier()
        emit_exchange(C, T, kT, Vd, kTg, Vg, zTh, upTh, Ed, Eg, selL, selR, tag="x")
        S.barrier()
        emit_attn(C, T, Sq, qT, kTg, Vg, qpos, kposc, p["lamv"], p["subg"], p["lamc"], yaT, tag="a", gathered=True)
        S.barrier()
        emit_mix(C, T, xa, zTh, upTh, yaT, w["w_out"], p["convw"], p["cvec"], p["pool_w"], invcnt, ident, xb, tag="m")
        S.barrier()
        dst = xc if l < nlayers - 1 else xout
        emit_ffn(C, T, xb, dst, w["wg2"], w["wu2"], w["wd2"], p["g2"], p["gp"], tag="f")
        S.barrier()
        cur = xc
    close_ctx(C)
    _prog_cache[key] = nc
    return nc


def fused_inputs(c, T, Sq, xT, weights, params):
    f32 = np.float32
    r = c % 4
    m = {"xin": xT, "ramp": alibi_ramp(r, T, Sq),
         "kposc": (np.arange(Sq // 128, dtype=f32)[None] * 128 + np.arange(128, dtype=f32)[:, None]).copy(),
         "invcnt": pool_invcnt(r * T, T, Sq), "ident": np.eye(128, dtype=f32)}
    sl = np.zeros((128, 4), f32)
    sr = np.zeros((128, 4), f32)
    if r > 0:
        sl[:, r - 1] = 1.0
    if r < 3:
        sr[:, r + 1] = 1.0
    m["selL"], m["selR"] = sl, sr
    m.update(weights)
    m.update(params)
    return m


def kernel_fused(inputs, launch):
    import math
    f32 = np.float32
    A = lambda a: np.ascontiguousarray(np.asarray(a, f32))
    x = A(inputs["x"])
    B, Sq, _ = x.shape
    T = Sq // 4
    ncore = 4 * B
    weights, params = {}, {}
    names = {"wg1": "ffn1_w_gate", "wu1": "ffn1_w_up", "wd1": "ffn1_w_down", "w_in": "w_in", "w_out": "w_out",
             "wg2": "ffn2_w_gate", "wu2": "ffn2_w_up", "wd2": "ffn2_w_down"}
    for l in range(2):
        lambda_init = 0.8 - 0.6 * math.exp(-0.3 * l)
        for k, src in names.items():
            weights[f"{k}_{l}"] = A(inputs[src][l])
        params[f"g1_{l}"] = gain_pc(A(inputs["ffn1_norm"][l]))
        params[f"gm_{l}"] = gain_pc(A(inputs["mix_norm"][l]))
        params[f"g2_{l}"] = gain_pc(A(inputs["ffn2_norm"][l]))
        params[f"gp_{l}"] = gain_pc(A(inputs["post_norm"][l]))
        params[f"qg_{l}"] = np.tile(A(inputs["q_norm"][l]), 2)[:, None].copy()
        params[f"kg_{l}"] = np.tile(A(inputs["k_norm"][l]), 2)[:, None].copy()
        params[f"lamv_{l}"] = np.ascontiguousarray(np.broadcast_to(
            np.stack([A(inputs["lambda_q1"][l]), A(inputs["lambda_k1"][l]), A(inputs["lambda_q2"][l]), A(inputs["lambda_k2"][l])])[None],
            (128, 4, 64)))
        params[f"subg_{l}"] = A(inputs["attn_subln"][l])[:, None].copy()
        params[f"lamc_{l}"] = np.ascontiguousarray(np.broadcast_to(np.array([-lambda_init, 1.0 - lambda_init], f32)[None], (128, 2)))
        params[f"convw_{l}"] = np.ascontiguousarray(A(inputs["conv_dw"][l]).T.reshape(2, 128, 31))
        cb, lg, lb, psc = (A(inputs[n][l]) for n in ("conv_dw_bias", "conv_ln_gain", "conv_ln_bias", "pool_scale"))
        params[f"cvec_{l}"] = np.stack([cb[:128], cb[128:], lg[:128], lg[128:], lb[:128], lb[128:], psc[:128], psc[128:]], 1).astype(f32)
        params[f"poolw_{l}"] = A(inputs["pool_w"][l])
    xT = [to_T(x[c // 4, (c % 4) * T:(c % 4 + 1) * T]) for c in range(ncore)]
    if LAYERS_PER_LAUNCH == 2:
        nc = build_fused_prog(T, Sq, 2)
        r = launch(nc, [fused_inputs(c, T, Sq, xT[c], weights, params) for c in range(ncore)])
        xT = [r[c]["xout"] for c in range(ncore)]
    else:
        nc = build_fused_prog(T, Sq, 1)
        for l in range(2):
            wl = {k[:-2] + "_0": v for k, v in weights.items() if k.endswith(f"_{l}")}
            pl = {k[:-2] + "_0": v for k, v in params.items() if k.endswith(f"_{l}")}
            r = launch(nc, [fused_inputs(c, T, Sq, xT[c], wl, pl) for c in range(ncore)])
            xT = [r[c]["xout"] for c in range(ncore)]
    out = np.empty((B, Sq, D), f32)
    for c in range(ncore):
        out[c // 4, (c % 4) * T:(c % 4 + 1) * T] = from_T(xT[c])
    return out


def kernel(**inputs):
    return kernel_fused(inputs, _launch)
```

```python
import numpy as np
import ml_dtypes
import concourse.bass as bass
import concourse.mybir as mybir
from concourse.bass_utils import run_bass_kernel_spmd

F32 = mybir.dt.float32
BF16 = mybir.dt.bfloat16
AF = mybir.ActivationFunctionType
ALU = mybir.AluOpType

D = 1024
DFF = 2816
NFC = DFF // 128
DIN = 2304
S_LEN = 16384
TPC = 4096
NCORES = 8
EPS = 1e-6

ENGS = ("pe", "act", "dve", "pool", "sp")
SAME_ENGINE_SYNC = True
SEM_ROT = 24000
XQ = "pool"


class _Op:
    __slots__ = ("eng", "fn", "reads", "writes", "chan", "group", "deps", "sig",
                 "idx", "pos", "xdeps", "inc")


class Sched:
    def __init__(self, nc):
        self.nc = nc
        self.ops = []
        self.chan_state = {}
        self.last_on = {}

    def op(self, eng, fn, reads=(), writes=(), xdeps=()):
        o = _Op()
        o.eng, o.fn, o.reads, o.writes = eng, fn, tuple(reads), tuple(writes)
        o.chan = None
        o.group = None
        o.xdeps = tuple(xdeps)
        o.idx = len(self.ops)
        self.ops.append(o)
        self.last_on[eng] = o.idx
        return o

    def dma(self, eng, chan, fn, reads=(), writes=(), newgroup=True, inc=16):
        o = self.op(eng, fn, reads, writes)
        o.chan = chan
        o.inc = inc
        st = self.chan_state.setdefault(chan, {"groups": []})
        if newgroup or not st["groups"]:
            st["groups"].append([])
        st["groups"][-1].append(o.idx)
        o.group = len(st["groups"]) - 1
        return o

    def barrier(self):
        lasts = [i for i in self.last_on.values()]
        for st in self.chan_state.values():
            if st["groups"]:
                lasts.append(st["groups"][-1][-1])
        for e in ENGS:
            self.op(e, None, xdeps=lasts)

    def finalize(self):
        nc = self.nc
        ops = self.ops
        last_w = {}
        readers = {}
        for o in ops:
            deps = set(o.xdeps)
            for r in o.reads:
                if r in last_w:
                    deps.add(last_w[r])
            for w in o.writes:
                if w in last_w:
                    deps.add(last_w[w])
                for rd in readers.get(w, ()):
                    deps.add(rd)
            deps.discard(o.idx)
            o.deps = deps
            for r in o.reads:
                readers.setdefault(r, []).append(o.idx)
            for w in o.writes:
                last_w[w] = o.idx
                readers[w] = []
        for chan, st in self.chan_state.items():
            cum = 0
            vals = []
            for g in st["groups"]:
                cum += sum(ops[i].inc for i in g)
                vals.append(cum)
            st["vals"] = vals
            assert cum < 65000, (chan, cum)
        pos = {e: 0 for e in ENGS}
        for o in ops:
            o.pos = pos[o.eng]
            pos[o.eng] += 1
        waited = {e: {} for e in ENGS}
        need = []
        sig_needed = set()
        for o in ops:
            w = {}
            for d in o.deps:
                p = ops[d]
                if p.chan is not None:
                    if o.chan == p.chan and o.group == p.group:
                        continue
                    s = ("c", p.chan)
                    key = p.group
                else:
                    if p.fn is None:
                        continue
                    s = ("e", p.eng)
                    key = p.pos
                    if p.eng == o.eng:
                        if p.eng == "pe" or not SAME_ENGINE_SYNC:
                            continue
                if s not in w or w[s][0] < key:
                    w[s] = (key, d)
            if o.chan is not None:
                st = self.chan_state[o.chan]
                if o.group > 0 and st["groups"][o.group][0] == o.idx:
                    s = ("c", o.chan)
                    key = o.group - 1
                    if s not in w or w[s][0] < key:
                        w[s] = (key, None)
            lst = []
            for s, (key, d) in w.items():
                if waited[o.eng].get(s, -1) >= key:
                    continue
                waited[o.eng][s] = key
                lst.append((s, key, d))
                if s[0] == "e":
                    sig_needed.add(d)
            need.append(lst)
        sigcount = {e: 0 for e in ENGS}
        for o in ops:
            if o.chan is None and o.idx in sig_needed:
                sigcount[o.eng] += 1
                o.sig = sigcount[o.eng]
            else:
                o.sig = None
        self._sem_ctx = []
        eng_sems = {}
        for e in ENGS:
            n = (sigcount[e] + SEM_ROT - 1) // SEM_ROT
            eng_sems[e] = [self._alloc_sem(f"s_{e}{i}") for i in range(max(n, 1))]
        chan_sems = {}
        for chan in self.chan_state:
            chan_sems[chan] = self._alloc_sem(f"c_{chan}")
        self.nsems = sum(len(v) for v in eng_sems.values()) + len(chan_sems)
        self.counts = dict(pos)

        def eng_wait_target(p):
            k = p.sig - 1
            return eng_sems[p.eng][k // SEM_ROT], (k % SEM_ROT) + 1

        streams = {e: [] for e in ENGS}
        for o in ops:
            streams[o.eng].append(o)

        semv = {}
        ptr = {e: 0 for e in ENGS}
        progress = True
        while progress:
            progress = False
            for e in ENGS:
                while ptr[e] < len(streams[e]):
                    o = streams[e][ptr[e]]
                    ok = True
                    for (s, key, d) in need[o.idx]:
                        if s[0] == "e":
                            sk, val = ("e", ops[d].eng), ops[d].sig
                        else:
                            sk, val = s, self.chan_state[s[1]]["vals"][key]
                        if semv.get(sk, 0) < val:
                            ok = False
                            break
                    if not ok:
                        break
                    if o.chan is not None:
                        semv[("c", o.chan)] = semv.get(("c", o.chan), 0) + o.inc
                    elif o.sig is not None:
                        semv[("e", o.eng)] = semv.get(("e", o.eng), 0) + 1
                        assert semv[("e", o.eng)] == o.sig
                    ptr[e] += 1
                    progress = True
        stuck = []
        for e in ENGS:
            if ptr[e] != len(streams[e]):
                o = streams[e][ptr[e]]
                unsat = []
                for (s_, key, d) in need[o.idx]:
                    if s_[0] == "e":
                        sk, val = ("e", ops[d].eng), ops[d].sig
                    else:
                        sk, val = s_, self.chan_state[s_[1]]["vals"][key]
                    if semv.get(sk, 0) < val:
                        unsat.append((sk, val, semv.get(sk, 0)))
                stuck.append((e, ptr[e], len(streams[e]), o.reads, o.writes, o.chan, unsat))
        assert not stuck, ("DEADLOCK", stuck)

        self.need = need
        self.streams = streams
        def run_stream(e, handle):
            for o in streams[e]:
                for (s, key, d) in need[o.idx]:
                    if s[0] == "e":
                        sem, val = eng_wait_target(ops[d])
                    else:
                        sem = chan_sems[s[1]]
                        val = self.chan_state[s[1]]["vals"][key]
                    handle.wait_ge(sem, val)
                ins = o.fn(handle) if o.fn is not None else None
                if ins is None:
                    assert o.chan is None and o.sig is None, "barrier op cannot signal"
                    continue
                if o.chan is not None:
                    ins.then_inc(chan_sems[o.chan], o.inc)
                elif o.sig is not None:
                    k = o.sig - 1
                    ins.then_inc(eng_sems[o.eng][k // SEM_ROT], 1)

        with nc.Block() as block:
            if streams["pe"]:
                @block.tensor
                def _(h):
                    run_stream("pe", h)
            if streams["act"]:
                @block.scalar
                def _(h):
                    run_stream("act", h)
            if streams["dve"]:
                @block.vector
                def _(h):
                    run_stream("dve", h)
            if streams["pool"]:
                @block.gpsimd
                def _(h):
                    run_stream("pool", h)
            if streams["sp"]:
                @block.sync
                def _(h):
                    run_stream("sp", h)
        for c in reversed(self._sem_ctx):
            c.__exit__(None, None, None)

    def _alloc_sem(self, name):
        c = self.nc.semaphore(name)
        s = c.__enter__()
        self._sem_ctx.append(c)
        return s


class Arena:
    def __init__(self, nc, nbytes, name="arena"):
        self.nc = nc
        self.n32 = nbytes // 4
        self.ctx = nc.sbuf_tensor(name, [128, self.n32], F32)
        self.t = self.ctx.__enter__()
        self.off = 0
        self.marks = []

    def alloc(self, shape, dt):
        esz = 2 if dt == BF16 else 4
        n = 1
        for s in shape:
            n *= s
        nb = (n * esz + 31) // 32 * 32
        a = self.off
        assert a + nb <= self.n32 * 4, ("SBUF arena overflow", a, nb, self.n32 * 4)
        self.off += nb
        ap = self.t[:, a // 4:(a + nb) // 4]
        if dt != F32:
            ap = ap.bitcast(dt)
        ap = ap[:, 0:n]
        if len(shape) == 2:
            ap = ap.rearrange("p (a b) -> p a b", a=shape[0])
        elif len(shape) == 3:
            ap = ap.rearrange("p (a b c) -> p a b c", a=shape[0], b=shape[1])
        return ap

    def mark(self):
        self.marks.append(self.off)

    def release(self):
        self.off = self.marks.pop()

    def close(self):
        self.ctx.__exit__(None, None, None)


class Ctx:
    pass


def make_ctx(nc):
    C = Ctx()
    C.nc = nc
    C.S = Sched(nc)
    C.A = Arena(nc, 207 * 1024)
    C.psctx = nc.psum_tensor("psum_all", [128, 4096], F32)
    C.ps = C.psctx.__enter__()
    C.uid = 0
    return C


def close_ctx(C):
    C.S.finalize()
    C.psctx.__exit__(None, None, None)
    C.A.close()


def bank(C, b, n=512, off=0):
    return C.ps[:, b * 512 + off:b * 512 + off + n]


def emit_ffn(C, T, xT_in, xT_out, wg, wu, wd, gvec, post_gvec=None, tag="f", NXB=2):
    S, A, nc = C.S, C.A, C.nc
    NT = 256
    ntiles = T // NT
    A.mark()
    Wg = A.alloc([8, DFF], BF16)
    Wu = A.alloc([8, DFF], BF16)
    Wd = A.alloc([NFC, D], BF16)
    onesf = A.alloc([128], F32)
    gv = A.alloc([8], F32)
    pgv = A.alloc([8], F32) if post_gvec is not None else None
    epsb = A.alloc([1], F32)
    xt = [A.alloc([8, NT], F32) for _ in range(NXB)]
    hT = [A.alloc([8, NT], BF16) for _ in range(2)]
    aT = A.alloc([NFC, NT], BF16)
    sq = [A.alloc([NT], F32) for _ in range(2)]
    sg = [A.alloc([NT], F32) for _ in range(2)]
    rstd = A.alloc([NT], F32)
    rstd2 = A.alloc([NT], F32)
    pg = [bank(C, 0, NT), bank(C, 1, NT)]
    pu = [bank(C, 2, NT), bank(C, 3, NT)]
    pd = [bank(C, 4, NT), bank(C, 5, NT)]
    pstat = bank(C, 6, NT)
    pstat2 = bank(C, 7, NT)
    R = lambda n: f"{tag}.{n}"

    S.op("pool", lambda h: h.memset(onesf, 1.0 / D), writes=[R("onesf")])
    S.op("pool", lambda h: h.memset(epsb, EPS), writes=[R("epsb")])
    S.dma("sp", R("cst"), lambda h: h.dma_start(out=gv, in_=gvec), writes=[R("gv")])
    if post_gvec is not None:
        S.dma("sp", R("cst"), lambda h: h.dma_start(out=pgv, in_=post_gvec), writes=[R("pgv")])
    xin_v = xT_in.rearrange("c p t -> p c t")
    xout_v = xT_out.rearrange("c p t -> p c t")

    def load(t):
        b = t % NXB
        S.dma(XQ, R(f"xin{b}"), lambda h: h.dma_start(out=xt[b], in_=xin_v[:, :, t * NT:(t + 1) * NT]),
              writes=[R(f"xt{b}")])

    for _t in range(min(NXB, ntiles)):
        load(_t)
    for dc in range(8):
        S.dma("pool", R("wld"), (lambda dc: lambda h: h.dma_start(out=Wg[:, dc, :], in_=wg[dc * 128:(dc + 1) * 128, :]))(dc),
              writes=[R(f"Wg{dc}")])
        S.dma("pool", R("wld"), (lambda dc: lambda h: h.dma_start(out=Wu[:, dc, :], in_=wu[dc * 128:(dc + 1) * 128, :]))(dc),
              writes=[R(f"Wu{dc}")])
    for fc in range(NFC):
        S.dma("pool", R("wld"), (lambda fc: lambda h: h.dma_start(out=Wd[:, fc, :], in_=wd[fc * 128:(fc + 1) * 128, :]))(fc),
              writes=[R(f"Wd{fc}")])

    def stats(xbuf, xres, pst, rs, rsres):
        for c in range(8):
            k = c % 2
            S.op("act", (lambda c, k: lambda h: h.activation(out=sq[k], in_=xbuf[:, c, :], func=AF.Square))(c, k),
                 reads=[xres], writes=[R(f"sq{k}")])
            S.op("pe", (lambda c, k: lambda h: h.matmul(pst, lhsT=onesf, rhs=sq[k], start=(c == 0), stop=(c == 7)))(c, k),
                 reads=[R(f"sq{k}"), R("onesf")], writes=[rsres + ".ps"])
        S.op("act", lambda h: h.activation(out=rs, in_=pst, func=AF.Sqrt, bias=epsb[:, 0:1], scale=1.0),
             reads=[rsres + ".ps", R("epsb")], writes=[rsres])
        S.op("dve", lambda h: h.reciprocal(out=rs, in_=rs), reads=[rsres], writes=[rsres])

    def make_h(t):
        b = t % 2
        xb = t % NXB
        stats(xt[xb], R(f"xt{xb}"), pstat, rstd, R("rstd"))
        for c in range(8):
            S.op("dve", (lambda c: lambda h: h.scalar_tensor_tensor(out=hT[b][:, c, :], in0=xt[xb][:, c, :], scalar=gv[:, c:c + 1],
                                                                   in1=rstd, op0=ALU.mult, op1=ALU.mult))(c),
                 reads=[R(f"xt{xb}"), R("rstd"), R("gv")], writes=[R(f"hT{b}")])

    def gateup(t):
        b = t % 2
        for fc in range(NFC):
            k = fc % 2
            for dc in range(8):
                S.op("pe", (lambda fc, dc, k: lambda h: h.matmul(pg[k], lhsT=Wg[:, dc, fc * 128:(fc + 1) * 128], rhs=hT[b][:, dc, :],
                                                                 start=(dc == 0), stop=(dc == 7)))(fc, dc, k),
                     reads=[R(f"Wg{dc}"), R(f"hT{b}")], writes=[R(f"pg{k}")])
            for dc in range(8):
                S.op("pe", (lambda fc, dc, k: lambda h: h.matmul(pu[k], lhsT=Wu[:, dc, fc * 128:(fc + 1) * 128], rhs=hT[b][:, dc, :],
                                                                 start=(dc == 0), stop=(dc == 7)))(fc, dc, k),
                     reads=[R(f"Wu{dc}"), R(f"hT{b}")], writes=[R(f"pu{k}")])
            S.op("act", (lambda k: lambda h: h.activation(out=sg[k], in_=pg[k], func=AF.Silu))(k),
                 reads=[R(f"pg{k}")], writes=[R(f"sg{k}")])
            S.op("dve", (lambda fc, k: lambda h: h.tensor_tensor(out=aT[:, fc, :], in0=sg[k], in1=pu[k], op=ALU.mult))(fc, k),
                 reads=[R(f"sg{k}"), R(f"pu{k}")], writes=[R(f"aT{fc}")])

    def down(t):
        b = t % NXB
        for oc in range(8):
            k = oc % 2
            for fc in range(NFC):
                S.op("pe", (lambda oc, fc, k: lambda h: h.matmul(pd[k], lhsT=Wd[:, fc, oc * 128:(oc + 1) * 128], rhs=aT[:, fc, :],
                                                                 start=(fc == 0), stop=(fc == NFC - 1)))(oc, fc, k),
                     reads=[R(f"Wd{fc}"), R(f"aT{fc}")], writes=[R(f"pd{k}")])
            S.op("dve", (lambda oc, k: lambda h: h.scalar_tensor_tensor(out=xt[b][:, oc, :], in0=pd[k], scalar=0.5, in1=xt[b][:, oc, :],
                                                                       op0=ALU.mult, op1=ALU.add))(oc, k),
                 reads=[R(f"pd{k}"), R(f"xt{b}")], writes=[R(f"xt{b}")])
        if post_gvec is not None:
            stats(xt[b], R(f"xt{b}"), pstat2, rstd2, R("rstd2"))
            for c in range(8):
                S.op("dve", (lambda c: lambda h: h.scalar_tensor_tensor(out=xt[b][:, c, :], in0=xt[b][:, c, :], scalar=pgv[:, c:c + 1],
                                                                       in1=rstd2, op0=ALU.mult, op1=ALU.mult))(c),
                     reads=[R(f"xt{b}"), R("rstd2"), R("pgv")], writes=[R(f"xt{b}")])
        S.dma(XQ, R(f"xout{b}"), lambda h: h.dma_start(out=xout_v[:, :, t * NT:(t + 1) * NT], in_=xt[b]),
              reads=[R(f"xt{b}")], writes=[R("xout")])

    make_h(0)
    for t in range(ntiles):
        gateup(t)
        if t + 1 < ntiles:
            make_h(t + 1)
        down(t)
        if t + NXB < ntiles:
            load(t + NXB)
    A.release()


_prog_cache = {}


def build_ffn_prog(T, post):
    key = ("ffn", T, post)
    if key in _prog_cache:
        return _prog_cache[key]
    nc = bass.Bass("TRN2", target_bir_lowering=False)
    xin = nc.dram_tensor("xin", [8, 128, T], F32, kind="ExternalInput").ap()
    wg = nc.dram_tensor("wg", [D, DFF], F32, kind="ExternalInput").ap()
    wu = nc.dram_tensor("wu", [D, DFF], F32, kind="ExternalInput").ap()
    wd = nc.dram_tensor("wd", [DFF, D], F32, kind="ExternalInput").ap()
    gvec = nc.dram_tensor("gvec", [128, 8], F32, kind="ExternalInput").ap()
    pg = nc.dram_tensor("pgvec", [128, 8], F32, kind="ExternalInput").ap() if post else None
    xout = nc.dram_tensor("xout", [8, 128, T], F32, kind="ExternalOutput").ap()
    C = make_ctx(nc)
    emit_ffn(C, T, xin, xout, wg, wu, wd, gvec, pg)
    C.S.op("sp", lambda h: None, reads=["f.xout"])
    close_ctx(C)
    _prog_cache[key] = nc
    return nc


def to_T(x2d):
    T = x2d.shape[0]
    return np.ascontiguousarray(x2d.T.reshape(8, 128, T))


def from_T(xT):
    T = xT.shape[2]
    return np.ascontiguousarray(xT.reshape(1024, T).T)


def gain_pc(g):
    return np.ascontiguousarray(g.reshape(8, 128).T).astype(np.float32)


def run_ffn(xT_list, wg, wu, wd, g, post_g=None):
    T = xT_list[0].shape[2]
    nc = build_ffn_prog(T, post_g is not None)
    maps = []
    for xT in xT_list:
        m = {"xin": xT, "wg": wg, "wu": wu, "wd": wd, "gvec": gain_pc(g)}
        if post_g is not None:
            m["pgvec"] = gain_pc(post_g)
        maps.append(m)
    res = run_bass_kernel_spmd(nc, maps, core_ids=list(range(len(maps))))
    return [r["xout"] for r in res.results]


def emit_inproj(C, T, xT_in, w_in, gvec, qg, kg, zT, upT, qT, kT, Vd, tag="i", zoff=0):
    S, A, nc = C.S, C.A, C.nc
    NT = 256
    ntiles = T // NT
    A.mark()
    W = A.alloc([8, DIN], BF16)
    onesf = A.alloc([128], F32)
    blk = A.alloc([128], F32)
    gv = A.alloc([8], F32)
    qgv = A.alloc([1], F32)
    kgv = A.alloc([1], F32)
    epsb = A.alloc([1], F32)
    eps64 = A.alloc([1], F32)
    xt = [A.alloc([8, NT], F32) for _ in range(2)]
    hT = A.alloc([8, NT], BF16)
    sq = [A.alloc([NT], F32) for _ in range(2)]
    rstd = A.alloc([NT], F32)
    sgm = [A.alloc([NT], F32) for _ in range(2)]
    rq = [A.alloc([NT], F32) for _ in range(2)]
    ob = [A.alloc([NT], BF16) for _ in range(4)]
    vb = [A.alloc([512], BF16) for _ in range(2)]
    pu = [bank(C, 0, NT), bank(C, 1, NT), bank(C, 2, NT), bank(C, 3, NT)]
    pv = [bank(C, 4, 512), bank(C, 5, 512)]
    pstat = bank(C, 6, NT)
    pqs = bank(C, 7, NT)
    R = lambda n: f"{tag}.{n}"

    S.op("pool", lambda h: h.memset(onesf, 1.0 / D), writes=[R("onesf")])
    S.op("pool", lambda h: h.memset(blk, 0.0), writes=[R("blk")])
    S.op("pool", lambda h: h.memset(blk[0:64, 0:64], 1.0 / 64), writes=[R("blk")])
    S.op("pool", lambda h: h.memset(blk[64:128, 64:128], 1.0 / 64), writes=[R("blk")])
    S.op("pool", lambda h: h.memset(epsb, EPS), writes=[R("epsb")])
    S.op("pool", lambda h: h.memset(eps64, 64.0 * EPS), writes=[R("eps64")])
    S.dma("pool", R("cst"), lambda h: h.dma_start(out=gv, in_=gvec), writes=[R("gv")])
    S.dma("pool", R("cst"), lambda h: h.dma_start(out=qgv, in_=qg), writes=[R("qgv")])
    S.dma("pool", R("cst"), lambda h: h.dma_start(out=kgv, in_=kg), writes=[R("kgv")])
    xin_v = xT_in.rearrange("c p t -> p c t")

    def load(t):
        b = t % 2
        S.dma(XQ, R(f"xin{b}"), lambda h: h.dma_start(out=xt[b], in_=xin_v[:, :, t * NT:(t + 1) * NT]),
              writes=[R(f"xt{b}")])

    load(0)
    if ntiles > 1:
        load(1)
    for dc in range(8):
        S.dma("pool", R("wld"), (lambda dc: lambda h: h.dma_start(out=W[:, dc, :], in_=w_in[dc * 128:(dc + 1) * 128, :]))(dc),
              writes=[R(f"W{dc}")])

    ocount = [0]

    def out_store(dst_ap, src, srcres):
        k = ocount[0] % 4
        ocount[0] += 1
        return k

    def do_tile(t):
        b = t % 2
        xb, xres = xt[b], R(f"xt{b}")
        tsl = slice(t * NT, (t + 1) * NT)
        for c in range(8):
            k = c % 2
            S.op("act", (lambda c, k: lambda h: h.activation(out=sq[k], in_=xb[:, c, :], func=AF.Square))(c, k),
                 reads=[xres], writes=[R(f"sq{k}")])
            S.op("pe", (lambda c, k: lambda h: h.matmul(pstat, lhsT=onesf, rhs=sq[k], start=(c == 0), stop=(c == 7)))(c, k),
                 reads=[R(f"sq{k}"), R("onesf")], writes=[R("pstat")])
        S.op("act", lambda h: h.activation(out=rstd, in_=pstat, func=AF.Sqrt, bias=epsb[:, 0:1], scale=1.0),
             reads=[R("pstat"), R("epsb")], writes=[R("rstd")])
        S.op("dve", lambda h: h.reciprocal(out=rstd, in_=rstd), reads=[R("rstd")], writes=[R("rstd")])
        for c in range(8):
            S.op("dve", (lambda c: lambda h: h.scalar_tensor_tensor(out=hT[:, c, :], in0=xb[:, c, :], scalar=gv[:, c:c + 1],
                                                                   in1=rstd, op0=ALU.mult, op1=ALU.mult))(c),
                 reads=[xres, R("rstd"), R("gv")], writes=[R("hT")])
        if t + 2 < ntiles:
            pass

        def proj(j, pk):
            for dc in range(8):
                S.op("pe", (lambda dc: lambda h: h.matmul(pu[pk], lhsT=W[:, dc, j * 128:(j + 1) * 128], rhs=hT[:, dc, :],
                                                          start=(dc == 0), stop=(dc == 7)))(dc),
                     reads=[R(f"W{dc}"), R("hT")], writes=[R(f"pu{pk}")])

        def store(dst, k):
            S.dma(XQ, R(f"ost{k}"), lambda h: h.dma_start(out=dst, in_=ob[k]), reads=[R(f"ob{k}")], writes=[R("outs")])

        for j in range(2):
            proj(2 + j, 0)
            proj(j, 1)
            k = ocount[0] % 4
            ocount[0] += 1
            S.op("act", (lambda j: lambda h: h.activation(out=sgm[j], in_=pu[0], func=AF.Sigmoid))(j),
                 reads=[R("pu0")], writes=[R(f"sgm{j}")])
            S.op("dve", (lambda j, k: lambda h: h.tensor_tensor(out=ob[k], in0=sgm[j], in1=pu[1], op=ALU.mult))(j, k),
                 reads=[R(f"sgm{j}"), R("pu1")], writes=[R(f"ob{k}")])
            store(zT[j, :, t * NT + zoff:(t + 1) * NT + zoff], k)
        for j in range(2):
            pk = 2 + j
            proj(4 + j, pk)
            k = ocount[0] % 4
            ocount[0] += 1
            S.op("dve", (lambda pk, k: lambda h: h.tensor_copy(out=ob[k], in_=pu[pk]))(pk, k),
                 reads=[R(f"pu{pk}")], writes=[R(f"ob{k}")])
            store(upT[j, :, t * NT + zoff:(t + 1) * NT + zoff], k)
        for j in range(8):
            pk = j % 4
            r = j % 2
            isq = j < 4
            proj(6 + j, pk)
            k = ocount[0] % 4
            ocount[0] += 1
            S.op("act", (lambda pk, r: lambda h: h.activation(out=sq[r], in_=pu[pk], func=AF.Square))(pk, r),
                 reads=[R(f"pu{pk}")], writes=[R(f"sq{r}")])
            S.op("pe", (lambda r: lambda h: h.matmul(pqs, lhsT=blk, rhs=sq[r], start=True, stop=True))(r),
                 reads=[R(f"sq{r}"), R("blk")], writes=[R("pqs")])
            if isq:
                S.op("act", (lambda r: lambda h: h.activation(out=rq[r], in_=pqs, func=AF.Sqrt, bias=eps64[:, 0:1], scale=64.0))(r),
                     reads=[R("pqs"), R("eps64")], writes=[R(f"rq{r}")])
            else:
                S.op("act", (lambda r: lambda h: h.activation(out=rq[r], in_=pqs, func=AF.Sqrt, bias=epsb[:, 0:1], scale=1.0))(r),
                     reads=[R("pqs"), R("epsb")], writes=[R(f"rq{r}")])
            S.op("dve", (lambda r: lambda h: h.reciprocal(out=rq[r], in_=rq[r]))(r), reads=[R(f"rq{r}")], writes=[R(f"rq{r}")])
            gvv = qgv if isq else kgv
            S.op("dve", (lambda pk, r, k, gvv: lambda h: h.scalar_tensor_tensor(out=ob[k], in0=pu[pk], scalar=gvv[:, 0:1], in1=rq[r],
                                                                               op0=ALU.mult, op1=ALU.mult))(pk, r, k, gvv),
                 reads=[R(f"pu{pk}"), R(f"rq{r}"), R("qgv"), R("kgv")], writes=[R(f"ob{k}")])
            dst = (qT if isq else kT)[j % 4, :, tsl]
            store(dst, k)
        def vpart(s):
            k = s % 2
            for dc in range(8):
                S.op("pe", (lambda dc: lambda h: h.matmul(pv[k], lhsT=hT[:, dc, s * 128:(s + 1) * 128], rhs=W[:, dc, 1792:2304],
                                                          start=(dc == 0), stop=(dc == 7)))(dc),
                     reads=[R(f"W{dc}"), R("hT")], writes=[R(f"pv{k}")])
            S.op("act", lambda h: h.activation(out=vb[k], in_=pv[k], func=AF.Copy), reads=[R(f"pv{k}")], writes=[R(f"vb{k}")])
            cidx = t * (NT // 128) + s
            S.dma(XQ, R(f"vst{k}"), lambda h: h.dma_start(out=Vd[:, :, cidx, :].rearrange("h p e -> p h e"),
                                                          in_=vb[k].rearrange("p (h e) -> p h e", h=4)),
                  reads=[R(f"vb{k}")], writes=[R("outs")])

        for s in range(NT // 128):
            vpart(s)
        if t + 2 < ntiles:
            load(t + 2)

    for t in range(ntiles):
        do_tile(t)
    A.release()


def build_inproj_prog(T):
    key = ("inproj", T)
    if key in _prog_cache:
        return _prog_cache[key]
    nc = bass.Bass("TRN2", target_bir_lowering=False)
    xin = nc.dram_tensor("xin", [8, 128, T], F32, kind="ExternalInput").ap()
    w_in = nc.dram_tensor("w_in", [D, DIN], F32, kind="ExternalInput").ap()
    gvec = nc.dram_tensor("gvec", [128, 8], F32, kind="ExternalInput").ap()
    qg = nc.dram_tensor("qg", [128, 1], F32, kind="ExternalInput").ap()
    kg = nc.dram_tensor("kg", [128, 1], F32, kind="ExternalInput").ap()
    zT = nc.dram_tensor("zT", [2, 128, T], BF16, kind="ExternalOutput").ap()
    upT = nc.dram_tensor("upT", [2, 128, T], BF16, kind="ExternalOutput").ap()
    qT = nc.dram_tensor("qT", [4, 128, T], BF16, kind="ExternalOutput").ap()
    kT = nc.dram_tensor("kT", [4, 128, T], BF16, kind="ExternalOutput").ap()
    Vd = nc.dram_tensor("Vd", [4, 128, T // 128, 128], BF16, kind="ExternalOutput").ap()
    C = make_ctx(nc)
    emit_inproj(C, T, xin, w_in, gvec, qg, kg, zT, upT, qT, kT, Vd)
    C.S.op("pool", lambda h: None, reads=["i.outs"])
    close_ctx(C)
    _prog_cache[key] = nc
    return nc


def emit_attn(C, T, Sq, qT, kTf, Vf, ramp, kposc, lamv, subg, lamc, yT, tag="a", gathered=False):
    S, A, nc = C.S, C.A, C.nc
    NQ = 512
    nqt = T // NQ
    nkc = Sq // 128
    A.mark()
    Kt = A.alloc([Sq], BF16)
    Vt = A.alloc([nkc, 128], BF16)
    kpc = A.alloc([nkc], F32)
    negk = A.alloc([nkc], F32)
    onesb = A.alloc([128], BF16)
    ones128 = A.alloc([128], F32)
    epsb = A.alloc([1], F32)
    lamt = A.alloc([4, 64], F32)
    lprod = A.alloc([2, 64], F32)
    lsum = A.alloc([2], F32)
    neglam = A.alloc([1], F32)
    sgv = A.alloc([1], F32)
    lct = A.alloc([2], F32)
    Qt = [A.alloc([NQ], BF16) for _ in range(2)]
    NF = Sq + T
    FP = 512
    Ft = A.alloc([NF], F32)
    rt = [A.alloc([FP], F32) for _ in range(2)]
    tmp = [A.alloc([2 * NQ], F32) for _ in range(2)]
    Pt = [A.alloc([2 * NQ], BF16) for _ in range(3)]
    r1 = A.alloc([NQ], F32)
    o1 = A.alloc([NQ], F32)
    o2 = A.alloc([NQ], F32)
    sqo = A.alloc([NQ], F32)
    rs = A.alloc([NQ], F32)
    yb = [A.alloc([NQ], BF16) for _ in range(2)]
    R = lambda n: f"{tag}.{n}"
    psS = [C.ps[:, 0:1024], C.ps[:, 1024:2048]]
    acc = [bank(C, 4), bank(C, 5)]
    den = [bank(C, 6), bank(C, 7)]

    S.op("pool", lambda h: h.memset(onesb, 1.0), writes=[R("onesb")])
    S.op("pool", lambda h: h.memset(ones128, 1.0 / 128), writes=[R("ones128")])
    S.op("pool", lambda h: h.memset(epsb, EPS), writes=[R("epsb")])
    S.dma("pool", R("cst"), lambda h: h.dma_start(out=kpc, in_=kposc), writes=[R("kpc")])
    S.dma("pool", R("cst"), lambda h: h.dma_start(out=lamt, in_=lamv), writes=[R("lamt")])
    S.dma("pool", R("cst"), lambda h: h.dma_start(out=sgv, in_=subg), writes=[R("sgv")])
    S.dma("pool", R("cst"), lambda h: h.dma_start(out=lct, in_=lamc), writes=[R("lct")])
    S.op("dve", lambda h: h.tensor_tensor(out=lprod[:, 0, :], in0=lamt[:, 0, :], in1=lamt[:, 1, :], op=ALU.mult),
         reads=[R("lamt")], writes=[R("lprod")])
    S.op("dve", lambda h: h.tensor_tensor(out=lprod[:, 1, :], in0=lamt[:, 2, :], in1=lamt[:, 3, :], op=ALU.mult),
         reads=[R("lamt")], writes=[R("lprod")])
    S.op("dve", lambda h: h.reduce_sum(out=lsum, in_=lprod, axis=mybir.AxisListType.X), reads=[R("lprod")], writes=[R("lsum")])
    S.op("act", lambda h: h.activation(out=lsum, in_=lsum, func=AF.Exp), reads=[R("lsum")], writes=[R("lsum")])
    S.op("dve", lambda h: h.tensor_tensor(out=neglam, in0=lsum[:, 1:2], in1=lsum[:, 0:1], op=ALU.subtract),
         reads=[R("lsum")], writes=[R("neglam")])
    S.op("dve", lambda h: h.tensor_scalar(out=neglam, in0=neglam, scalar1=lct[:, 0:1], scalar2=None, op0=ALU.add),
         reads=[R("neglam"), R("lct")], writes=[R("neglam")])
    S.op("dve", lambda h: h.tensor_scalar(out=sgv, in0=sgv, scalar1=lct[:, 1:2], scalar2=None, op0=ALU.mult),
         reads=[R("sgv"), R("lct")], writes=[R("sgv")])

    cnt = [0]

    def head(hd):
        slope = 2.0 ** (-8.0 * (hd + 1) / 4)
        if gathered:
            S.dma("pool", R("kld"), lambda h: h.dma_start(out=Kt.rearrange("p (r t) -> p r t", r=4),
                                                          in_=kTf[hd].rearrange("r p t -> p r t")), writes=[R("Kt")])
            S.dma("pool", R("vld"), lambda h: h.dma_start(out=Vt.rearrange("p (r c) e -> p r (c e)", r=4),
                                                          in_=Vf[hd].rearrange("r p c e -> p r (c e)")), writes=[R("Vt")])
        else:
            S.dma("pool", R("kld"), lambda h: h.dma_start(out=Kt, in_=kTf[hd]), writes=[R("Kt")])
            S.dma("pool", R("vld"), lambda h: h.dma_start(out=Vt, in_=Vf[hd]), writes=[R("Vt")])
        S.op("dve", lambda h: h.tensor_scalar(out=negk, in0=kpc, scalar1=-slope, scalar2=None, op0=ALU.mult),
             reads=[R("kpc")], writes=[R("negk")])

        def fpiece(pi):
            b = pi % 2
            sl_ = slice(pi * FP, (pi + 1) * FP)
            S.dma("pool", R(f"rld{b}"), lambda h: h.dma_start(out=rt[b], in_=ramp[:, sl_]), writes=[R(f"rt{b}")])
            S.op("act", lambda h: h.activation(out=Ft[:, sl_], in_=rt[b], func=AF.Abs, bias=negk[:, 0:1], scale=slope),
                 reads=[R(f"rt{b}"), R("negk")], writes=[R("Ft")])

        for pi in range(NF // FP):
            fpiece(pi)

        def qtile(qt):
            qb = (hd * nqt + qt) % 2
            qs = slice(qt * NQ, (qt + 1) * NQ)
            S.dma("pool", R(f"qld{qb}"), lambda h: h.dma_start(out=Qt[qb], in_=qT[hd, :, qs]), writes=[R(f"Qt{qb}")])

            def score(kc):
                i = kc % 2
                j = kc % 3
                ks = slice(kc * 128, (kc + 1) * 128)
                S.op("pe", lambda h: h.matmul(psS[i][:, 0:NQ], lhsT=Kt[0:64, ks], rhs=Qt[qb][0:64, :], start=True, stop=True),
                     reads=[R("Kt"), R(f"Qt{qb}")], writes=[R(f"psS{i}a")])
                S.op("pe", lambda h: h.matmul(psS[i][:, NQ:2 * NQ], lhsT=Kt[64:128, ks], rhs=Qt[qb][64:128, :], start=True, stop=True),
                     reads=[R("Kt"), R(f"Qt{qb}")], writes=[R(f"psS{i}b")])
                fo = Sq + qt * NQ - kc * 128
                for m, ab in ((0, "a"), (1, "b")):
                    S.op("dve", (lambda m: lambda h: h.tensor_tensor(out=tmp[i][:, m * NQ:(m + 1) * NQ], in0=psS[i][:, m * NQ:(m + 1) * NQ],
                                                                     in1=Ft[:, fo:fo + NQ], op=ALU.subtract))(m),
                         reads=[R("Ft"), R(f"psS{i}{ab}")], writes=[R(f"tmp{i}{ab}")])
                    S.op("act", (lambda m: lambda h: h.activation(out=Pt[j][:, m * NQ:(m + 1) * NQ], in_=tmp[i][:, m * NQ:(m + 1) * NQ],
                                                                  func=AF.Exp))(m),
                         reads=[R(f"tmp{i}{ab}")], writes=[R(f"Pt{j}{ab}")])

            def accum(kc):
                i = kc % 3
                first, last = (kc == 0), (kc == nkc - 1)
                for m, ab in ((0, "a"), (1, "b")):
                    S.op("pe", (lambda m: lambda h: h.matmul(acc[m], lhsT=Vt[:, kc, :], rhs=Pt[i][:, m * NQ:(m + 1) * NQ],
                                                             start=first, stop=last))(m),
                         reads=[R("Vt"), R(f"Pt{i}{ab}")], writes=[R(f"acc{m}")])
                    S.op("pe", (lambda m: lambda h: h.matmul(den[m], lhsT=onesb, rhs=Pt[i][:, m * NQ:(m + 1) * NQ],
                                                             start=first, stop=last))(m),
                         reads=[R("onesb"), R(f"Pt{i}{ab}")], writes=[R(f"den{m}")])

            score(0)
            if nkc > 1:
                score(1)
            for kc in range(nkc):
                if kc + 2 < nkc:
                    score(kc + 2)
                accum(kc)
            S.op("dve", lambda h: h.reciprocal(out=r1, in_=den[0]), reads=[R("den0")], writes=[R("r1")])
            S.op("dve", lambda h: h.tensor_tensor(out=o1, in0=acc[0], in1=r1, op=ALU.mult), reads=[R("acc0"), R("r1")], writes=[R("o1")])
            S.op("dve", lambda h: h.reciprocal(out=r1, in_=den[1]), reads=[R("den1"), R("o1")], writes=[R("r1")])
            S.op("dve", lambda h: h.tensor_tensor(out=o2, in0=acc[1], in1=r1, op=ALU.mult), reads=[R("acc1"), R("r1")], writes=[R("o2")])
            S.op("dve", lambda h: h.scalar_tensor_tensor(out=o1, in0=o2, scalar=neglam[:, 0:1], in1=o1, op0=ALU.mult, op1=ALU.add),
                 reads=[R("o2"), R("o1"), R("neglam")], writes=[R("o1")])
            S.op("act", lambda h: h.activation(out=sqo, in_=o1, func=AF.Square), reads=[R("o1")], writes=[R("sqo")])
            S.op("pe", lambda h: h.matmul(den[0], lhsT=ones128, rhs=sqo, start=True, stop=True),
                 reads=[R("ones128"), R("sqo")], writes=[R("den0")])
            S.op("act", lambda h: h.activation(out=rs, in_=den[0], func=AF.Sqrt, bias=epsb[:, 0:1], scale=1.0),
                 reads=[R("den0"), R("epsb")], writes=[R("rs")])
            S.op("dve", lambda h: h.reciprocal(out=rs, in_=rs), reads=[R("rs")], writes=[R("rs")])
            S.op("dve", lambda h: h.scalar_tensor_tensor(out=yb[qb], in0=o1, scalar=sgv[:, 0:1], in1=rs, op0=ALU.mult, op1=ALU.mult),
                 reads=[R("o1"), R("rs"), R("sgv")], writes=[R(f"yb{qb}")])
            S.dma("pool", R(f"yst{qb}"), lambda h: h.dma_start(out=yT[hd, :, qs], in_=yb[qb]), reads=[R(f"yb{qb}")], writes=[R("outs")])

        for qt in range(nqt):
            qtile(qt)

    for hd in range(4):
        head(hd)
    A.release()


def build_attn_prog(T, Sq):
    key = ("attn", T, Sq)
    if key in _prog_cache:
        return _prog_cache[key]
    nc = bass.Bass("TRN2", target_bir_lowering=False)
    qT = nc.dram_tensor("qT", [4, 128, T], BF16, kind="ExternalInput").ap()
    kTf = nc.dram_tensor("kTf", [4, 128, Sq], BF16, kind="ExternalInput").ap()
    Vf = nc.dram_tensor("Vf", [4, 128, Sq // 128, 128], BF16, kind="ExternalInput").ap()
    qpos = nc.dram_tensor("ramp", [128, Sq + T], F32, kind="ExternalInput").ap()
    kposc = nc.dram_tensor("kposc", [128, Sq // 128], F32, kind="ExternalInput").ap()
    lamv = nc.dram_tensor("lamv", [128, 4, 64], F32, kind="ExternalInput").ap()
    subg = nc.dram_tensor("subg", [128, 1], F32, kind="ExternalInput").ap()
    lamc = nc.dram_tensor("lamc", [128, 2], F32, kind="ExternalInput").ap()
    yT = nc.dram_tensor("yT", [4, 128, T], BF16, kind="ExternalOutput").ap()
    C = make_ctx(nc)
    emit_attn(C, T, Sq, qT, kTf, Vf, qpos, kposc, lamv, subg, lamc, yT)
    C.S.op("pool", lambda h: None, reads=["a.outs"])
    close_ctx(C)
    _prog_cache[key] = nc
    return nc


HALO = 16


def emit_mix(C, T, x1T, zTh, upTh, yaT, w_out, convw, cvec, pool_w, invcnt, ident, x2T, tag="m"):
    S, A, nc = C.S, C.A, C.nc
    NT = 256
    ntiles = T // NT
    A.mark()
    Wo = A.alloc([8, D], BF16)
    idt = A.alloc([128], F32)
    cw = A.alloc([2, 31], F32)
    cv = A.alloc([8], F32)
    Dg = A.alloc([2, 31, 128], BF16)
    pst = A.alloc([2, 128], F32)
    PWf = A.alloc([2, 128], BF16)
    PWh = A.alloc([2, 128], BF16)
    ones256 = A.alloc([128], F32)
    epsb = A.alloc([1], F32)
    xt = [A.alloc([8, NT], F32) for _ in range(2)]
    zt = [A.alloc([2, NT + 2 * HALO], BF16) for _ in range(2)]
    ut = [A.alloc([2, NT + 2 * HALO], BF16) for _ in range(2)]
    ic = [A.alloc([2, NT], F32) for _ in range(2)]
    ycat = [A.alloc([8, NT], BF16) for _ in range(2)]
    cz = A.alloc([2, NT], F32)
    sqc = A.alloc([2, NT], F32)
    mean = A.alloc([NT], F32)
    m2 = A.alloc([NT], F32)
    rstd = A.alloc([NT], F32)
    t1 = A.alloc([2, NT], F32)
    pa = A.alloc([NT], F32)
    R = lambda n: f"{tag}.{n}"
    pc = bank(C, 0, NT)
    pm = bank(C, 1, NT)
    pvv = bank(C, 2, NT)
    pA = bank(C, 3, NT)
    pB = bank(C, 4, NT)
    po = [bank(C, 5, NT), bank(C, 6, NT)]

    S.op("pool", lambda h: h.memset(ones256, 1.0 / 256), writes=[R("ones256")])
    S.op("pool", lambda h: h.memset(epsb, EPS), writes=[R("epsb")])
    S.op("pool", lambda h: h.memset(pst, 0.0), writes=[R("pst")])
    S.dma("pool", R("cst"), lambda h: h.dma_start(out=idt, in_=ident), writes=[R("idt")])
    S.dma("pool", R("cst"), lambda h: h.dma_start(out=cw, in_=convw.rearrange("c p k -> p c k")), writes=[R("cw")])
    S.dma("pool", R("cst"), lambda h: h.dma_start(out=cv, in_=cvec), writes=[R("cv")])
    for g in range(4):
        c, r = g // 2, g % 2
        S.dma("pool", R("cst"), (lambda g, c, r: lambda h: h.dma_start(out=pst[64 * r:64 * r + 64, c, 64 * r:64 * r + 64], in_=pool_w[g]))(g, c, r),
              reads=[R("pst")], writes=[R(f"pst{g}")])
    for c in range(2):
        S.op("dve", (lambda c: lambda h: h.tensor_copy(out=PWf[:, c, :], in_=pst[:, c, :]))(c),
             reads=[R("pst0"), R("pst1"), R("pst2"), R("pst3")], writes=[R("PWf")])
        S.op("dve", (lambda c: lambda h: h.tensor_copy(out=PWh[:, c, :], in_=pst[:, c, :]))(c),
             reads=[R("pst0"), R("pst1"), R("pst2"), R("pst3")], writes=[R("PWh")])
        S.op("dve", (lambda c: lambda h: h.memset(PWh[0:64, c, :], 0.0))(c), reads=[R("PWh")], writes=[R("PWh")])
        for k in range(31):
            S.op("dve", (lambda c, k: lambda h: h.tensor_scalar(out=Dg[:, c, k, :], in0=idt, scalar1=cw[:, c, k:k + 1], scalar2=None,
                                                               op0=ALU.mult))(c, k),
                 reads=[R("idt"), R("cw")], writes=[R("Dg")])
    for dc in range(8):
        S.dma("pool", R("wld"), (lambda dc: lambda h: h.dma_start(out=Wo[:, dc, :], in_=w_out[dc * 128:(dc + 1) * 128, :]))(dc),
              writes=[R(f"Wo{dc}")])
    x1v = x1T.rearrange("c p t -> p c t")
    x2v = x2T.rearrange("c p t -> p c t")
    zv = zTh.rearrange("c p t -> p c t")
    uv = upTh.rearrange("c p t -> p c t")
    yav = yaT.rearrange("c p t -> p c t")
    icv = invcnt.rearrange("c p t -> p c t")

    def load(t):
        b = t % 2
        ts_ = slice(t * NT, (t + 1) * NT)
        th = slice(t * NT, (t + 1) * NT + 2 * HALO)
        S.dma(XQ, R(f"ld{b}"), lambda h: h.dma_start(out=xt[b], in_=x1v[:, :, ts_]), writes=[R(f"xt{b}")])
        S.dma(XQ, R(f"ld{b}"), lambda h: h.dma_start(out=zt[b], in_=zv[:, :, th]), writes=[R(f"zt{b}")], newgroup=False)
        S.dma(XQ, R(f"ld{b}"), lambda h: h.dma_start(out=ut[b], in_=uv[:, :, th]), writes=[R(f"ut{b}")], newgroup=False)
        S.dma(XQ, R(f"ld{b}"), lambda h: h.dma_start(out=ic[b], in_=icv[:, :, ts_]), writes=[R(f"ic{b}")], newgroup=False)
        S.dma(XQ, R(f"ld{b}"), lambda h: h.dma_start(out=ycat[b][:, 4:8, :], in_=yav[:, :, ts_]), writes=[R(f"ya{b}")], newgroup=False)

    def do_tile(t):
        b = t % 2
        ts_ = slice(t * NT, (t + 1) * NT)
        for c in range(2):
            for k in range(31):
                S.op("pe", (lambda c, k: lambda h: h.matmul(pc, lhsT=Dg[:, c, k, :], rhs=zt[b][:, c, k + 1:k + 1 + NT],
                                                            start=(k == 0), stop=(k == 30)))(c, k),
                     reads=[R("Dg"), R(f"zt{b}")], writes=[R("pc")])
            S.op("act", (lambda c: lambda h: h.activation(out=cz[:, c, :], in_=pc, func=AF.Identity, bias=cv[:, c:c + 1], scale=1.0))(c),
                 reads=[R("pc"), R("cv")], writes=[R(f"cz{c}")])
            S.op("act", (lambda c: lambda h: h.activation(out=sqc[:, c, :], in_=cz[:, c, :], func=AF.Square))(c),
                 reads=[R(f"cz{c}")], writes=[R(f"sqc{c}")])
        for c in range(2):
            S.op("pe", (lambda c: lambda h: h.matmul(pm, lhsT=ones256, rhs=cz[:, c, :], start=(c == 0), stop=(c == 1)))(c),
                 reads=[R("ones256"), R(f"cz{c}")], writes=[R("pm")])
        for c in range(2):
            S.op("pe", (lambda c: lambda h: h.matmul(pvv, lhsT=ones256, rhs=sqc[:, c, :], start=(c == 0), stop=(c == 1)))(c),
                 reads=[R("ones256"), R(f"sqc{c}")], writes=[R("pvv")])
        S.op("dve", lambda h: h.tensor_copy(out=mean, in_=pm), reads=[R("pm")], writes=[R("mean")])
        S.op("dve", lambda h: h.tensor_tensor(out=m2, in0=mean, in1=mean, op=ALU.mult), reads=[R("mean")], writes=[R("m2")])
        S.op("dve", lambda h: h.tensor_tensor(out=m2, in0=pvv, in1=m2, op=ALU.subtract), reads=[R("pvv"), R("m2")], writes=[R("m2")])
        S.op("act", lambda h: h.activation(out=rstd, in_=m2, func=AF.Sqrt, bias=epsb[:, 0:1], scale=1.0),
             reads=[R("m2"), R("epsb")], writes=[R("rstd")])
        S.op("dve", lambda h: h.reciprocal(out=rstd, in_=rstd), reads=[R("rstd")], writes=[R("rstd")])
        for c in range(2):
            S.op("dve", (lambda c: lambda h: h.tensor_tensor(out=t1[:, c, :], in0=cz[:, c, :], in1=mean, op=ALU.subtract))(c),
                 reads=[R(f"cz{c}"), R("mean")], writes=[R(f"t1{c}")])
            S.op("dve", (lambda c: lambda h: h.scalar_tensor_tensor(out=t1[:, c, :], in0=t1[:, c, :], scalar=cv[:, 2 + c:3 + c], in1=rstd,
                                                                   op0=ALU.mult, op1=ALU.mult))(c),
                 reads=[R(f"t1{c}"), R("rstd"), R("cv")], writes=[R(f"t1{c}")])
            S.op("act", (lambda c: lambda h: h.activation(out=ycat[b][:, c, :], in_=t1[:, c, :], func=AF.Silu, bias=cv[:, 4 + c:5 + c], scale=1.0))(c),
                 reads=[R(f"t1{c}"), R("cv")], writes=[R(f"yc{b}")])
        for c in range(2):
            taps = list(range(-2, 2)) if c == 0 else list(range(-8, 8))
            narrow = (1, ) if c == 0 else (4, )
            for i, tau in enumerate(taps):
                full = (-narrow[0] <= tau <= narrow[0] - 1)
                Wm = PWf if full else PWh
                S.op("pe", (lambda c, tau, Wm, i, n: lambda h: h.matmul(pA, lhsT=Wm[:, c, :], rhs=ut[b][:, c, HALO + tau:HALO + tau + NT],
                                                                        start=(i == 0), stop=(i == n - 1)))(c, tau, Wm, i, len(taps)),
                     reads=[R("PWf"), R("PWh"), R(f"ut{b}")], writes=[R("pA")])
            S.op("pe", (lambda c: lambda h: h.matmul(pB, lhsT=PWf[:, c, :], rhs=ut[b][:, c, HALO:HALO + NT], start=True, stop=True))(c),
                 reads=[R("PWf"), R(f"ut{b}")], writes=[R("pB")])
            S.op("dve", (lambda c: lambda h: h.tensor_tensor(out=pa, in0=pA, in1=ic[b][:, c, :], op=ALU.mult))(c),
                 reads=[R("pA"), R(f"ic{b}")], writes=[R("pa")])
            S.op("dve", (lambda c: lambda h: h.tensor_tensor(out=pa, in0=pa, in1=pB, op=ALU.subtract))(c),
                 reads=[R("pa"), R("pB")], writes=[R("pa")])
            S.op("dve", (lambda c: lambda h: h.tensor_scalar(out=ycat[b][:, 2 + c, :], in0=pa, scalar1=cv[:, 6 + c:7 + c], scalar2=None,
                                                            op0=ALU.mult))(c),
                 reads=[R("pa"), R("cv")], writes=[R(f"yc{b}")])
        for oc in range(8):
            k = oc % 2
            for c in range(8):
                S.op("pe", (lambda oc, c, k: lambda h: h.matmul(po[k], lhsT=Wo[:, c, oc * 128:(oc + 1) * 128], rhs=ycat[b][:, c, :],
                                                                start=(c == 0), stop=(c == 7)))(oc, c, k),
                     reads=[R(f"Wo{c}"), R(f"yc{b}"), R(f"ya{b}")], writes=[R(f"po{k}")])
            S.op("dve", (lambda oc, k: lambda h: h.tensor_tensor(out=xt[b][:, oc, :], in0=po[k], in1=xt[b][:, oc, :], op=ALU.add))(oc, k),
                 reads=[R(f"po{k}"), R(f"xt{b}")], writes=[R(f"xt{b}")])
        S.dma(XQ, R(f"st{b}"), lambda h: h.dma_start(out=x2v[:, :, ts_], in_=xt[b]), reads=[R(f"xt{b}")], writes=[R("outs")])
        if t + 2 < ntiles:
            load(t + 2)

    load(0)
    if ntiles > 1:
        load(1)
    for t in range(ntiles):
        do_tile(t)
    A.release()


def build_mix_prog(T):
    key = ("mix", T)
    if key in _prog_cache:
        return _prog_cache[key]
    nc = bass.Bass("TRN2", target_bir_lowering=False)
    x1T = nc.dram_tensor("x1T", [8, 128, T], F32, kind="ExternalInput").ap()
    zTh = nc.dram_tensor("zTh", [2, 128, T + 2 * HALO], BF16, kind="ExternalInput").ap()
    upTh = nc.dram_tensor("upTh", [2, 128, T + 2 * HALO], BF16, kind="ExternalInput").ap()
    yaT = nc.dram_tensor("yaT", [4, 128, T], BF16, kind="ExternalInput").ap()
    w_out = nc.dram_tensor("w_out", [D, D], F32, kind="ExternalInput").ap()
    convw = nc.dram_tensor("convw", [2, 128, 31], F32, kind="ExternalInput").ap()
    cvec = nc.dram_tensor("cvec", [128, 8], F32, kind="ExternalInput").ap()
    pool_w = nc.dram_tensor("pool_w", [4, 64, 64], F32, kind="ExternalInput").ap()
    invcnt = nc.dram_tensor("invcnt", [2, 128, T], F32, kind="ExternalInput").ap()
    ident = nc.dram_tensor("ident", [128, 128], F32, kind="ExternalInput").ap()
    x2T = nc.dram_tensor("x2T", [8, 128, T], F32, kind="ExternalOutput").ap()
    C = make_ctx(nc)
    emit_mix(C, T, x1T, zTh, upTh, yaT, w_out, convw, cvec, pool_w, invcnt, ident, x2T)
    C.S.op("pool", lambda h: None, reads=["m.outs"])
    close_ctx(C)
    _prog_cache[key] = nc
    return nc


def alibi_ramp(r, T, Sq):
    return np.ascontiguousarray(np.broadcast_to((np.arange(Sq + T, dtype=np.float32) - Sq + r * T)[None], (128, Sq + T)))


def pool_invcnt(pos0, T, Sq):
    t = np.arange(pos0, pos0 + T)
    out = np.zeros((2, 128, T), np.float32)
    for g, w in enumerate((2, 4, 8, 16)):
        lo = np.clip(t - w // 2, 0, Sq)
        hi = np.clip(t + w // 2, 0, Sq)
        out[g // 2, (g % 2) * 64:(g % 2) * 64 + 64, :] = (1.0 / (hi - lo).astype(np.float32))[None, :]
    return out


def _launch(nc, maps):
    res = run_bass_kernel_spmd(nc, maps, core_ids=list(range(len(maps))))
    return res.results


def kernel_unfused(x, ffn1_norm, ffn1_w_gate, ffn1_w_up, ffn1_w_down, mix_norm, w_in,
                   conv_dw, conv_dw_bias, conv_ln_gain, conv_ln_bias, pool_w, pool_scale,
                   q_norm, k_norm, lambda_q1, lambda_k1, lambda_q2, lambda_k2, attn_subln,
                   w_out, ffn2_norm, ffn2_w_gate, ffn2_w_up, ffn2_w_down, post_norm, _launch=_launch):
    import math
    f32 = np.float32
    x = np.asarray(x, f32)
    B, Sq, _ = x.shape
    T = Sq // 4
    ncore = 4 * B
    xT = [to_T(x[c // 4, (c % 4) * T:(c % 4 + 1) * T]) for c in range(ncore)]
    ident = np.eye(128, dtype=f32)
    kposc = (np.arange(Sq // 128, dtype=f32)[None] * 128 + np.arange(128, dtype=f32)[:, None]).copy()
    qpos = [alibi_ramp(c % 4, T, Sq) for c in range(ncore)]
    invc = [pool_invcnt((c % 4) * T, T, Sq) for c in range(ncore)]
    A = lambda a: np.ascontiguousarray(np.asarray(a, f32))
    for l in range(2):
        lambda_init = 0.8 - 0.6 * math.exp(-0.3 * l)
        nc = build_ffn_prog(T, False)
        g1 = gain_pc(A(ffn1_norm[l]))
        r = _launch(nc, [{"xin": xT[c], "wg": A(ffn1_w_gate[l]), "wu": A(ffn1_w_up[l]), "wd": A(ffn1_w_down[l]), "gvec": g1}
                         for c in range(ncore)])
        x1T = [r[c]["xout"] for c in range(ncore)]
        nc = build_inproj_prog(T)
        qg = np.tile(A(q_norm[l]), 2)[:, None].copy()
        kg = np.tile(A(k_norm[l]), 2)[:, None].copy()
        r = _launch(nc, [{"xin": x1T[c], "w_in": A(w_in[l]), "gvec": gain_pc(A(mix_norm[l])), "qg": qg, "kg": kg}
                         for c in range(ncore)])
        kTf, Vf, zh, uh = [], [], [], []
        for b in range(B):
            cs = range(4 * b, 4 * b + 4)
            kTf.append(np.concatenate([r[c]["kT"] for c in cs], axis=2))
            Vf.append(np.concatenate([r[c]["Vd"] for c in cs], axis=2))
            zf = np.concatenate([r[c]["zT"] for c in cs], axis=2)
            uf = np.concatenate([r[c]["upT"] for c in cs], axis=2)
            zh.append(np.pad(zf, ((0, 0), (0, 0), (HALO, HALO))))
            uh.append(np.pad(uf, ((0, 0), (0, 0), (HALO, HALO))))
        qTl = [r[c]["qT"] for c in range(ncore)]
        nc = build_attn_prog(T, Sq)
        lamv = np.ascontiguousarray(np.broadcast_to(np.stack([A(lambda_q1[l]), A(lambda_k1[l]), A(lambda_q2[l]), A(lambda_k2[l])])[None],
                                                    (128, 4, 64)))
        lamc = np.ascontiguousarray(np.broadcast_to(np.array([-lambda_init, 1.0 - lambda_init], f32)[None], (128, 2)))
        subg = A(attn_subln[l])[:, None].copy()
        r = _launch(nc, [{"qT": qTl[c], "kTf": kTf[c // 4], "Vf": Vf[c // 4], "ramp": qpos[c], "kposc": kposc, "lamv": lamv,
                          "subg": subg, "lamc": lamc} for c in range(ncore)])
        yaT = [r[c]["yT"] for c in range(ncore)]
        nc = build_mix_prog(T)
        cb, lg, lb, psc = A(conv_dw_bias[l]), A(conv_ln_gain[l]), A(conv_ln_bias[l]), A(pool_scale[l])
        cvec = np.stack([cb[:128], cb[128:], lg[:128], lg[128:], lb[:128], lb[128:], psc[:128], psc[128:]], 1).astype(f32)
        convw = np.ascontiguousarray(A(conv_dw[l]).T.reshape(2, 128, 31))
        maps = []
        for c in range(ncore):
            o = (c % 4) * T
            maps.append({"x1T": x1T[c], "zTh": np.ascontiguousarray(zh[c // 4][:, :, o:o + T + 2 * HALO]),
                         "upTh": np.ascontiguousarray(uh[c // 4][:, :, o:o + T + 2 * HALO]), "yaT": yaT[c], "w_out": A(w_out[l]),
                         "convw": convw, "cvec": cvec, "pool_w": A(pool_w[l]), "invcnt": invc[c], "ident": ident})
        r = _launch(nc, maps)
        x2T = [r[c]["x2T"] for c in range(ncore)]
        nc = build_ffn_prog(T, True)
        r = _launch(nc, [{"xin": x2T[c], "wg": A(ffn2_w_gate[l]), "wu": A(ffn2_w_up[l]), "wd": A(ffn2_w_down[l]),
                          "gvec": gain_pc(A(ffn2_norm[l])), "pgvec": gain_pc(A(post_norm[l]))} for c in range(ncore)])
        xT = [r[c]["xout"] for c in range(ncore)]
    out = np.empty((B, Sq, D), f32)
    for c in range(ncore):
        out[c // 4, (c % 4) * T:(c % 4 + 1) * T] = from_T(xT[c])
    return out


LAYERS_PER_LAUNCH = 2
LAYER_W = ("wg1", "wu1", "wd1", "w_in", "w_out", "wg2", "wu2", "wd2")


def emit_exchange(C, T, kT, Vd, kTg, Vg, zTh, upTh, Ed, Eg, selL, selR, tag="x"):
    S, A, nc = C.S, C.A, C.nc
    R = lambda n: f"{tag}.{n}"
    groups = [[0, 1, 2, 3], [4, 5, 6, 7]]
    A.mark()
    H = HALO
    eg = A.alloc([4, 4, 2 * H], BF16)
    hl = A.alloc([4, H], BF16)
    hr = A.alloc([4, H], BF16)
    sl = A.alloc([4], F32)
    sr = A.alloc([4], F32)
    S.dma("pool", R("cst"), lambda h: h.dma_start(out=sl, in_=selL), writes=[R("sl")])
    S.dma("pool", R("cst"), lambda h: h.dma_start(out=sr, in_=selR), writes=[R("sr")])
    for j, src in enumerate((zTh, upTh)):
        for c in range(2):
            S.dma("pool", R("edge"), (lambda j, c, src: lambda h: h.dma_start(out=Ed[2 * j + c, :, 0:H], in_=src[c, :, H:2 * H]))(j, c, src),
                  writes=[R(f"Ed{j}{c}a")], newgroup=(j == 0 and c == 0))
            S.dma("pool", R("edge"), (lambda j, c, src: lambda h: h.dma_start(out=Ed[2 * j + c, :, H:2 * H], in_=src[c, :, T:T + H]))(j, c, src),
                  writes=[R(f"Ed{j}{c}b")], newgroup=False)
    for hd in range(4):
        S.dma("pool", R("cc"), (lambda hd: lambda h: h.collective_compute("AllGather", ALU.bypass, replica_groups=groups,
                                                                          ins=[kT[hd]], outs=[kTg[hd].rearrange("r p t -> (r p) t")]))(hd),
              writes=[R(f"kTg{hd}")], inc=1)
        S.dma("pool", R("cc"), (lambda hd: lambda h: h.collective_compute("AllGather", ALU.bypass, replica_groups=groups,
                                                                          ins=[Vd[hd].rearrange("p c e -> p (c e)")],
                                                                          outs=[Vg[hd].rearrange("r p c e -> (r p) (c e)")]))(hd),
              writes=[R(f"Vg{hd}")], inc=1)
    S.dma("pool", R("cc"), lambda h: h.collective_compute("AllGather", ALU.bypass, replica_groups=groups,
                                                          ins=[Ed.rearrange("j p e -> (j p) e")], outs=[Eg.rearrange("r j p e -> (r j p) e")]),
          reads=[R(f"Ed{j}{c}{x}") for j in range(2) for c in range(2) for x in "ab"], writes=[R("Eg")], inc=1)
    S.dma("pool", R("egl"), lambda h: h.dma_start(out=eg, in_=Eg.rearrange("r j p e -> p r j e")), reads=[R("Eg")], writes=[R("eg")])
    for r in range(4):
        if r == 0:
            S.op("dve", lambda h: h.tensor_scalar(out=hl, in0=eg[:, 0, :, H:2 * H], scalar1=sl[:, 0:1], scalar2=None, op0=ALU.mult),
                 reads=[R("eg"), R("sl")], writes=[R("hl")])
            S.op("dve", lambda h: h.tensor_scalar(out=hr, in0=eg[:, 0, :, 0:H], scalar1=sr[:, 0:1], scalar2=None, op0=ALU.mult),
                 reads=[R("eg"), R("sr")], writes=[R("hr")])
        else:
            S.op("dve", (lambda r: lambda h: h.scalar_tensor_tensor(out=hl, in0=eg[:, r, :, H:2 * H], scalar=sl[:, r:r + 1], in1=hl,
                                                                   op0=ALU.mult, op1=ALU.add))(r),
                 reads=[R("eg"), R("sl"), R("hl")], writes=[R("hl")])
            S.op("dve", (lambda r: lambda h: h.scalar_tensor_tensor(out=hr, in0=eg[:, r, :, 0:H], scalar=sr[:, r:r + 1], in1=hr,
                                                                   op0=ALU.mult, op1=ALU.add))(r),
                 reads=[R("eg"), R("sr"), R("hr")], writes=[R("hr")])
    for j, dst in enumerate((zTh, upTh)):
        S.dma("pool", R("hst"), (lambda j, dst: lambda h: h.dma_start(out=dst[:, :, 0:H].rearrange("c p e -> p c e"),
                                                                    in_=hl[:, 2 * j:2 * j + 2, :]))(j, dst),
              reads=[R("hl")], writes=[R(f"haloL{j}")], newgroup=(j == 0))
        S.dma("pool", R("hst"), (lambda j, dst: lambda h: h.dma_start(out=dst[:, :, T + H:T + 2 * H].rearrange("c p e -> p c e"),
                                                                    in_=hr[:, 2 * j:2 * j + 2, :]))(j, dst),
              reads=[R("hr")], writes=[R(f"haloR{j}")], newgroup=False)
    A.release()


def build_fused_prog(T, Sq, nlayers=2):
    key = ("fused", T, Sq, nlayers)
    if key in _prog_cache:
        return _prog_cache[key]
    nc = bass.Bass("TRN2", target_bir_lowering=False)
    EI = lambda name, shape, dt=F32: nc.dram_tensor(name, shape, dt, kind="ExternalInput").ap()
    IN = lambda name, shape, dt: nc.dram_tensor(name, shape, dt).ap()
    xin = EI("xin", [8, 128, T])
    wshape = {"wg1": [D, DFF], "wu1": [D, DFF], "wd1": [DFF, D], "w_in": [D, DIN], "w_out": [D, D],
              "wg2": [D, DFF], "wu2": [D, DFF], "wd2": [DFF, D]}
    Wt = [{k: EI(f"{k}_{l}", wshape[k]) for k in LAYER_W} for l in range(nlayers)]
    P = []
    for l in range(nlayers):
        P.append({"g1": EI(f"g1_{l}", [128, 8]), "gm": EI(f"gm_{l}", [128, 8]), "g2": EI(f"g2_{l}", [128, 8]),
                  "gp": EI(f"gp_{l}", [128, 8]), "qg": EI(f"qg_{l}", [128, 1]), "kg": EI(f"kg_{l}", [128, 1]),
                  "lamv": EI(f"lamv_{l}", [128, 4, 64]), "subg": EI(f"subg_{l}", [128, 1]), "lamc": EI(f"lamc_{l}", [128, 2]),
                  "convw": EI(f"convw_{l}", [2, 128, 31]), "cvec": EI(f"cvec_{l}", [128, 8]), "pool_w": EI(f"poolw_{l}", [4, 64, 64])})
    qpos = EI("ramp", [128, Sq + T])
    kposc = EI("kposc", [128, Sq // 128])
    invcnt = EI("invcnt", [2, 128, T])
    ident = EI("ident", [128, 128])
    selL = EI("selL", [128, 4])
    selR = EI("selR", [128, 4])
    xout = nc.dram_tensor("xout", [8, 128, T], F32, kind="ExternalOutput").ap()
    xa = IN("xa", [8, 128, T], F32)
    xb = IN("xb", [8, 128, T], F32)
    xc = IN("xc", [8, 128, T], F32)
    zTh = IN("zTh", [2, 128, T + 2 * HALO], BF16)
    upTh = IN("upTh", [2, 128, T + 2 * HALO], BF16)
    qT = IN("qT", [4, 128, T], BF16)
    kT = IN("kT", [4, 128, T], BF16)
    Vd = IN("Vd", [4, 128, T // 128, 128], BF16)
    kTg = IN("kTg", [4, 4, 128, T], BF16)
    Vg = IN("Vg", [4, 4, 128, T // 128, 128], BF16)
    Ed = IN("Ed", [4, 128, 2 * HALO], BF16)
    Eg = IN("Eg", [4, 4, 128, 2 * HALO], BF16)
    yaT = IN("yaT", [4, 128, T], BF16)
    C = make_ctx(nc)
    S = C.S
    cur = xin
    for l in range(nlayers):
        w, p = Wt[l], P[l]
        emit_ffn(C, T, cur, xa, w["wg1"], w["wu1"], w["wd1"], p["g1"], None, tag="f")
        S.barrier()
        emit_inproj(C, T, xa, w["w_in"], p["gm"], p["qg"], p["kg"], zTh, upTh, qT, kT, Vd, tag="i", zoff=HALO)
        S.barrier()
        emit_exchange(C, T, kT, Vd, kTg, Vg, zTh, upTh, Ed, Eg, selL, selR, tag="x")
        S.barrier()
        emit_attn(C, T, Sq, qT, kTg, Vg, qpos, kposc, p["lamv"], p["subg"], p["lamc"], yaT, tag="a", gathered=True)
        S.barrier()
        emit_mix(C, T, xa, zTh, upTh, yaT, w["w_out"], p["convw"], p["cvec"], p["pool_w"], invcnt, ident, xb, tag="m")
        S.barrier()
        dst = xc if l < nlayers - 1 else xout
        emit_ffn(C, T, xb, dst, w["wg2"], w["wu2"], w["wd2"], p["g2"], p["gp"], tag="f")
        S.barrier()
        cur = xc
    close_ctx(C)
    _prog_cache[key] = nc
    return nc


def fused_inputs(c, T, Sq, xT, weights, params):
    f32 = np.float32
    r = c % 4
    m = {"xin": xT, "ramp": alibi_ramp(r, T, Sq),
         "kposc": (np.arange(Sq // 128, dtype=f32)[None] * 128 + np.arange(128, dtype=f32)[:, None]).copy(),
         "invcnt": pool_invcnt(r * T, T, Sq), "ident": np.eye(128, dtype=f32)}
    sl = np.zeros((128, 4), f32)
    sr = np.zeros((128, 4), f32)
    if r > 0:
        sl[:, r - 1] = 1.0
    if r < 3:
        sr[:, r + 1] = 1.0
    m["selL"], m["selR"] = sl, sr
    m.update(weights)
    m.update(params)
    return m


def kernel_fused(inputs, launch):
    import math
    f32 = np.float32
    A = lambda a: np.ascontiguousarray(np.asarray(a, f32))
    x = A(inputs["x"])
    B, Sq, _ = x.shape
    T = Sq // 4
    ncore = 4 * B
    weights, params = {}, {}
    names = {"wg1": "ffn1_w_gate", "wu1": "ffn1_w_up", "wd1": "ffn1_w_down", "w_in": "w_in", "w_out": "w_out",
             "wg2": "ffn2_w_gate", "wu2": "ffn2_w_up", "wd2": "ffn2_w_down"}
    for l in range(2):
        lambda_init = 0.8 - 0.6 * math.exp(-0.3 * l)
        for k, src in names.items():
            weights[f"{k}_{l}"] = A(inputs[src][l])
        params[f"g1_{l}"] = gain_pc(A(inputs["ffn1_norm"][l]))
        params[f"gm_{l}"] = gain_pc(A(inputs["mix_norm"][l]))
        params[f"g2_{l}"] = gain_pc(A(inputs["ffn2_norm"][l]))
        params[f"gp_{l}"] = gain_pc(A(inputs["post_norm"][l]))
        params[f"qg_{l}"] = np.tile(A(inputs["q_norm"][l]), 2)[:, None].copy()
        params[f"kg_{l}"] = np.tile(A(inputs["k_norm"][l]), 2)[:, None].copy()
        params[f"lamv_{l}"] = np.ascontiguousarray(np.broadcast_to(
            np.stack([A(inputs["lambda_q1"][l]), A(inputs["lambda_k1"][l]), A(inputs["lambda_q2"][l]), A(inputs["lambda_k2"][l])])[None],
            (128, 4, 64)))
        params[f"subg_{l}"] = A(inputs["attn_subln"][l])[:, None].copy()
        params[f"lamc_{l}"] = np.ascontiguousarray(np.broadcast_to(np.array([-lambda_init, 1.0 - lambda_init], f32)[None], (128, 2)))
        params[f"convw_{l}"] = np.ascontiguousarray(A(inputs["conv_dw"][l]).T.reshape(2, 128, 31))
        cb, lg, lb, psc = (A(inputs[n][l]) for n in ("conv_dw_bias", "conv_ln_gain", "conv_ln_bias", "pool_scale"))
        params[f"cvec_{l}"] = np.stack([cb[:128], cb[128:], lg[:128], lg[128:], lb[:128], lb[128:], psc[:128], psc[128:]], 1).astype(f32)
        params[f"poolw_{l}"] = A(inputs["pool_w"][l])
    xT = [to_T(x[c // 4, (c % 4) * T:(c % 4 + 1) * T]) for c in range(ncore)]
    if LAYERS_PER_LAUNCH == 2:
        nc = build_fused_prog(T, Sq, 2)
        r = launch(nc, [fused_inputs(c, T, Sq, xT[c], weights, params) for c in range(ncore)])
        xT = [r[c]["xout"] for c in range(ncore)]
    else:
        nc = build_fused_prog(T, Sq, 1)
        for l in range(2):
            wl = {k[:-2] + "_0": v for k, v in weights.items() if k.endswith(f"_{l}")}
            pl = {k[:-2] + "_0": v for k, v in params.items() if k.endswith(f"_{l}")}
            r = launch(nc, [fused_inputs(c, T, Sq, xT[c], wl, pl) for c in range(ncore)])
            xT = [r[c]["xout"] for c in range(ncore)]
    out = np.empty((B, Sq, D), f32)
    for c in range(ncore):
        out[c // 4, (c % 4) * T:(c % 4 + 1) * T] = from_T(xT[c])
    return out


def kernel(**inputs):
    return kernel_fused(inputs, _launch)
```
